# Optimizing a Trainium2 kernel written in Bass

```python
import jax, jax.numpy as jnp
from jax import lax
import numpy as np

D_MODEL = 2048
BATCH = 4
SEQ = 2048
DEPTH = 1
DEC_BATCH = 128
DEC_SEQ = 4
PAST_LEN = 16384
PAGE_SIZE = 128

MIX_WIDTH = D_MODEL
M_WIDTH = MIX_WIDTH // 2
G_WIDTH = MIX_WIDTH - M_WIDTH
M_HEADS = 4
M_HEAD_DIM = M_WIDTH // M_HEADS
G_HEADS = 8
G_HEAD_DIM = G_WIDTH // G_HEADS
CONV_W = 4
CHUNK = 64
D_FF = -(-8 * D_MODEL // (3 * 256)) * 256
DEEPNORM_ALPHA = (2.0 * DEPTH) ** 0.25
DEEPNORM_BETA = (8.0 * DEPTH) ** -0.25
LN_EPS = 1e-5
RMS_EPS = 1e-6
IN_SIZES = (M_WIDTH, M_WIDTH, M_WIDTH, M_WIDTH, 3 * G_WIDTH, G_WIDTH, M_HEADS, M_HEADS, G_HEADS, G_HEADS)
IN_COLS = sum(IN_SIZES)

kernel_name = 'hybrid_mlstm_gdn_step'


def layer_norm(x, g, b):
    xf = x.astype(jnp.float32)
    mu = jnp.mean(xf, -1, keepdims=True)
    var = jnp.mean(jnp.square(xf - mu), -1, keepdims=True)
    return ((xf - mu) * lax.rsqrt(var + LN_EPS) * g.astype(jnp.float32) + b.astype(jnp.float32)).astype(x.dtype)


def head_rms_norm(x, g):
    H, d = x.shape[-2:]
    return x * lax.rsqrt(jnp.mean(jnp.square(x), -1, keepdims=True) + RMS_EPS) * g.astype(jnp.float32).reshape(H, d)


def l2norm(x):
    x = x.astype(jnp.float32)
    return x * lax.rsqrt(jnp.sum(jnp.square(x), -1, keepdims=True) + RMS_EPS)


def chunk_len(T):
    return CHUNK if T % CHUNK == 0 else T


def to_chunks(a, L):
    B, T = a.shape[:2]
    return jnp.moveaxis(a.reshape(B, T // L, L, *a.shape[2:]), 1, 0)


def from_chunks(a):
    NC, B, L = a.shape[:3]
    return jnp.moveaxis(a, 0, 1).reshape(B, NC * L, *a.shape[3:])


def mlstm_scan(q, k, v, i_pre, f_pre, C0, n0, m0):
    B, T, H, dk = q.shape
    L = chunk_len(T)
    f32 = jnp.float32
    q = q.astype(f32)
    k = k.astype(f32) * (dk ** -0.5)
    v = v.astype(f32)
    ig = i_pre.astype(f32)
    lf = jax.nn.log_sigmoid(f_pre.astype(f32))
    causal = jnp.tril(jnp.ones((L, L), bool))

    def step(carry, xs):
        C, n, m = carry
        q_, k_, v_, ig_, lf_ = xs
        bt = jnp.moveaxis(jnp.cumsum(lf_, axis=1), 1, -1)
        it = jnp.moveaxis(ig_, 1, -1)
        logD = jnp.where(causal, bt[..., :, None] - bt[..., None, :] + it[..., None, :], -jnp.inf)
        inter = bt + m[..., None]
        m_t = jnp.maximum(inter, jnp.max(logD, -1))
        inter_w = jnp.exp(inter - m_t)
        s = jnp.einsum('blhd,bshd->bhls', q_, k_) * jnp.exp(logD - m_t[..., None])
        num = inter_w[..., None] * jnp.einsum('blhd,bhde->bhle', q_, C) + jnp.einsum('bhls,bshe->bhle', s, v_)
        den = inter_w * jnp.einsum('blhd,bhd->bhl', q_, n) + jnp.sum(s, -1)
        h = num / jnp.maximum(jnp.abs(den), jnp.exp(-m_t))[..., None]
        bL = bt[..., -1]
        wlog = bL[..., None] - bt + it
        m_new = jnp.maximum(bL + m, jnp.max(wlog, -1))
        w = jnp.moveaxis(jnp.exp(wlog - m_new[..., None]), -1, 1)
        decay = jnp.exp(bL + m - m_new)
        C_new = decay[..., None, None] * C + jnp.einsum('bshd,bshe->bhde', k_ * w[..., None], v_)
        n_new = decay[..., None] * n + jnp.einsum('bsh,bshd->bhd', w, k_)
        return (C_new, n_new, m_new), jnp.moveaxis(h, 2, 1)

    xs = tuple(to_chunks(a, L) for a in (q, k, v, ig, lf))
    (C1, n1, m1), hs = lax.scan(step, (C0.astype(f32), n0.astype(f32), m0.astype(f32)), xs)
    return from_chunks(hs), C1, n1, m1


def gdn_scan(q, k, v, log_a, beta, S0):
    B, T, H, dk = q.shape
    L = chunk_len(T)
    f32 = jnp.float32
    q = q.astype(f32) * (dk ** -0.5)
    k = k.astype(f32)
    v = v.astype(f32)
    causal = jnp.tril(jnp.ones((L, L), bool))
    strict = jnp.tril(jnp.ones((L, L), f32), -1)
    eye = jnp.eye(L, dtype=f32)

    def step(S, xs):
        q_, k_, v_, la_, be_ = xs
        b = jnp.moveaxis(jnp.cumsum(la_, axis=1), 1, -1)
        bet = jnp.moveaxis(be_, 1, -1)
        decay = jnp.exp(jnp.where(causal, b[..., :, None] - b[..., None, :], -jnp.inf))
        kh = jnp.moveaxis(k_, 1, 2)
        A = bet[..., :, None] * jnp.einsum('bhld,bhsd->bhls', kh, kh) * decay * strict
        rhs = jnp.concatenate([bet[..., None] * jnp.moveaxis(v_, 1, 2),
                               (bet * jnp.exp(b))[..., None] * kh], -1)
        sol = lax.linalg.triangular_solve(eye + A, rhs, left_side=True, lower=True, unit_diagonal=True)
        U0, W = sol[..., :v_.shape[-1]], sol[..., v_.shape[-1]:]
        U = U0 - jnp.einsum('bhld,bhde->bhle', W, S)
        qk = jnp.einsum('blhd,bhsd->bhls', q_, kh) * decay
        o = jnp.exp(b)[..., None] * jnp.einsum('blhd,bhde->bhle', q_, S) + jnp.einsum('bhls,bhse->bhle', qk, U)
        bL = b[..., -1]
        wk = jnp.exp(bL[..., None] - b)[..., None] * kh
        S_new = jnp.exp(bL)[..., None, None] * S + jnp.einsum('bhld,bhle->bhde', wk, U)
        return S_new, jnp.moveaxis(o, 2, 1)

    xs = tuple(to_chunks(a, L) for a in (q, k, v, log_a.astype(f32), beta.astype(f32)))
    S1, os_ = lax.scan(step, S0.astype(f32), xs)
    return from_chunks(os_), S1


def causal_conv(x, buf, w):
    xp = jnp.concatenate([buf.astype(x.dtype), x], 1)
    T = x.shape[1]
    y = sum(w[j] * xp[:, j:j + T] for j in range(CONV_W))
    return jax.nn.silu(y), xp[:, -(CONV_W - 1):]


def hybrid_layer(x, c, C0, n0, m0, S0, conv0, w_ada, b_ada, w_in, m_i_bias, m_f_bias, m_norm_g,
                 conv_w, g_dt_bias, g_A_log, g_norm_g, w_out, ln1_g, ln1_b, w_gu, w_down, ln2_g, ln2_b):
    B, T, _ = x.shape
    dt = x.dtype
    f32 = jnp.float32
    ada = jax.nn.silu(c) @ w_ada + b_ada
    sh1, sc1, gt1, sh2, sc2, gt2 = [a[:, None, :] for a in jnp.split(ada, 6, axis=-1)]
    h = x * (1 + sc1) + sh1
    proj = h @ w_in
    mq, mk, mv, mo, g_qkv, gz, mi, mf, gb, g_a = jnp.split(proj, np.cumsum(IN_SIZES)[:-1].tolist(), axis=-1)
    hm, C1, n1, m1 = mlstm_scan(mq.reshape(B, T, M_HEADS, M_HEAD_DIM), mk.reshape(B, T, M_HEADS, M_HEAD_DIM),
                                mv.reshape(B, T, M_HEADS, M_HEAD_DIM), mi + m_i_bias, mf + m_f_bias, C0, n0, m0)
    hm = head_rms_norm(hm, m_norm_g) * jax.nn.sigmoid(mo.astype(f32)).reshape(B, T, M_HEADS, M_HEAD_DIM)
    g_conv, conv1 = causal_conv(g_qkv, conv0, conv_w)
    gq, gk, gv = jnp.split(g_conv, 3, axis=-1)
    log_a = -jnp.exp(g_A_log.astype(f32)) * jax.nn.softplus((g_a + g_dt_bias).astype(f32))
    beta = jax.nn.sigmoid(gb.astype(f32))
    hg, S1 = gdn_scan(l2norm(gq.reshape(B, T, G_HEADS, G_HEAD_DIM)), l2norm(gk.reshape(B, T, G_HEADS, G_HEAD_DIM)),
                      gv.reshape(B, T, G_HEADS, G_HEAD_DIM), log_a, beta, S0)
    hg = head_rms_norm(hg, g_norm_g) * jax.nn.silu(gz.astype(f32)).reshape(B, T, G_HEADS, G_HEAD_DIM)
    mix = jnp.concatenate([hm.reshape(B, T, M_WIDTH), hg.reshape(B, T, G_WIDTH)], -1).astype(dt) @ w_out
    x = layer_norm(DEEPNORM_ALPHA * x + (1 + gt1) * mix, ln1_g, ln1_b)
    h2 = x * (1 + sc2) + sh2
    gate, up = jnp.split(h2 @ w_gu, 2, axis=-1)
    ffn = (jax.nn.silu(gate) * up) @ w_down
    x = layer_norm(DEEPNORM_ALPHA * x + (1 + gt2) * ffn, ln2_g, ln2_b)
    return x, C1.astype(dt), n1.astype(dt), m1.astype(dt), S1.astype(dt), conv1.astype(dt)


def setup_inputs(seed: int = 0) -> dict:
    key = jax.random.key(seed)
    ks = jax.random.split(key, 32)
    f32 = jnp.float32
    nrm = lambda k, shape, s: s * jax.random.normal(k, shape, f32)
    return {
        'x_prompt': nrm(ks[0], (BATCH, SEQ, D_MODEL), 1.0),
        'x_sample': nrm(ks[1], (DEC_BATCH, DEC_SEQ, D_MODEL), 1.0),
        'state_mlstm_C': nrm(ks[2], (DEC_BATCH, M_HEADS, M_HEAD_DIM, M_HEAD_DIM), 0.1),
        'state_mlstm_n': nrm(ks[3], (DEC_BATCH, M_HEADS, M_HEAD_DIM), 0.1),
        'state_mlstm_m': nrm(ks[4], (DEC_BATCH, M_HEADS), 1.0),
        'state_gdn_S': nrm(ks[5], (DEC_BATCH, G_HEADS, G_HEAD_DIM, G_HEAD_DIM), 0.1),
        'state_gdn_conv': nrm(ks[6], (DEC_BATCH, CONV_W - 1, 3 * G_WIDTH), 1.0),
        'c_prompt': nrm(ks[7], (BATCH, D_MODEL), 1.0),
        'c_sample': nrm(ks[8], (DEC_BATCH, D_MODEL), 1.0),
        'w_ada': nrm(ks[9], (D_MODEL, 6 * D_MODEL), 0.5 * D_MODEL ** -0.5),
        'b_ada': nrm(ks[10], (6 * D_MODEL,), 0.02),
        'w_in': nrm(ks[11], (D_MODEL, IN_COLS), D_MODEL ** -0.5),
        'm_i_bias': nrm(ks[12], (M_HEADS,), 0.1),
        'm_f_bias': 3.0 + nrm(ks[13], (M_HEADS,), 0.5),
        'm_norm_g': 1.0 + nrm(ks[14], (M_WIDTH,), 0.02),
        'conv_w': nrm(ks[15], (CONV_W, 3 * G_WIDTH), CONV_W ** -0.5),
        'g_dt_bias': nrm(ks[16], (G_HEADS,), 0.1),
        'g_A_log': jnp.log(jax.random.uniform(ks[17], (G_HEADS,), f32, 1.0, 16.0)),
        'g_norm_g': 1.0 + nrm(ks[18], (G_WIDTH,), 0.02),
        'w_out': nrm(ks[19], (MIX_WIDTH, D_MODEL), DEEPNORM_BETA * MIX_WIDTH ** -0.5),
        'ln1_g': 1.0 + nrm(ks[20], (D_MODEL,), 0.02),
        'ln1_b': nrm(ks[21], (D_MODEL,), 0.02),
        'w_gu': nrm(ks[22], (D_MODEL, 2 * D_FF), D_MODEL ** -0.5),
        'w_down': nrm(ks[23], (D_FF, D_MODEL), DEEPNORM_BETA * D_FF ** -0.5),
        'ln2_g': 1.0 + nrm(ks[24], (D_MODEL,), 0.02),
        'ln2_b': nrm(ks[25], (D_MODEL,), 0.02),
    }


def reference(x_prompt, x_sample, state_mlstm_C, state_mlstm_n, state_mlstm_m, state_gdn_S, state_gdn_conv,
              c_prompt, c_sample, w_ada, b_ada, w_in, m_i_bias, m_f_bias, m_norm_g, conv_w, g_dt_bias,
              g_A_log, g_norm_g, w_out, ln1_g, ln1_b, w_gu, w_down, ln2_g, ln2_b):
    f32 = jnp.float32
    Bp = x_prompt.shape[0]
    y_p = x_prompt
    y_s = x_sample
    for _ in range(DEPTH):
        y_p, p_C, p_n, p_m, p_S, p_conv = hybrid_layer(
            y_p, c_prompt,
            jnp.zeros((Bp, M_HEADS, M_HEAD_DIM, M_HEAD_DIM), f32), jnp.zeros((Bp, M_HEADS, M_HEAD_DIM), f32),
            jnp.zeros((Bp, M_HEADS), f32), jnp.zeros((Bp, G_HEADS, G_HEAD_DIM, G_HEAD_DIM), f32),
            jnp.zeros((Bp, CONV_W - 1, 3 * G_WIDTH), x_prompt.dtype),
            w_ada, b_ada, w_in, m_i_bias, m_f_bias, m_norm_g, conv_w, g_dt_bias, g_A_log, g_norm_g,
            w_out, ln1_g, ln1_b, w_gu, w_down, ln2_g, ln2_b)
        y_s, s_C, s_n, s_m, s_S, s_conv = hybrid_layer(
            y_s, c_sample, state_mlstm_C, state_mlstm_n, state_mlstm_m, state_gdn_S, state_gdn_conv,
            w_ada, b_ada, w_in, m_i_bias, m_f_bias, m_norm_g, conv_w, g_dt_bias, g_A_log, g_norm_g,
            w_out, ln1_g, ln1_b, w_gu, w_down, ln2_g, ln2_b)
    return (y_p, y_s, p_C, p_n, p_m, p_S, p_conv, s_C, s_n, s_m, s_S, s_conv)
```

```python
import os
import numpy as np
import concourse.bass as bass
import concourse.mybir as mybir
from concourse.bass_utils import run_bass_kernel_spmd

F32 = mybir.dt.float32
BF16 = mybir.dt.bfloat16
AF = mybir.ActivationFunctionType
ALU = mybir.AluOpType
AX = mybir.AxisListType

D = 2048
NT = 17
TT = NT * 128
MYT = 1088
DFF = 5632
INC = 8216
NSEQ = 16
ALPHA = 2.0 ** 0.25
NEG = -1.0e30


def _dsize(dt):
    return 2 if dt == BF16 else 4


class KB:
    ENG = ("tensor", "vector", "scalar", "gpsimd", "sync")

    def __init__(self, nc, esems, dsems):
        self.nc = nc
        self.ops = {e: [] for e in self.ENG}
        self.cnt = {e: 0 for e in self.ENG}
        self.esems = esems
        self.dsems = dsems
        self.dcnt = [0] * len(dsems)
        self.dnext = 0
        self.dnext_sw = 0
        self.waited = {e: {} for e in self.ENG}
        self.W = {}
        self.R = {}
        self.needed = {}

    def _sem(self, key):
        return self.esems[key] if isinstance(key, str) else self.dsems[key]

    coarse = False
    coarse_pe = False
    psum_excl = os.environ.get("KB_PSX", "1") == "1"
    ce = tuple(os.environ.get("KB_CE", "tensor").split(","))

    def region(self, ap):
        name = ap.tensor.name
        if self.coarse and str(ap.space) in ("SB", "SBUF"):
            return name, 0, 1 << 40
        if str(ap.space) not in ("SB", "SBUF", "PSUM"):
            ext = 1
            for st, n in ap.ap:
                ext += abs(st) * (n - 1)
            return name, ap.offset * 4, (ap.offset + ext) * 4
        ds = _dsize(ap.dtype)
        dims = list(ap.ap)
        pstride = dims[0][0]
        off = ap.offset % pstride if pstride > 0 else ap.offset
        ext = 1
        for st, n in dims[1:]:
            ext += abs(st) * (n - 1)
        return name, off * ds, (off + ext) * ds

    def _deps(self, eng, reads, writes):
        need = {}

        def add(sig):
            k, v = sig
            if eng == "tensor" and k == "tensor":
                return
            if need.get(k, 0) < v:
                need[k] = v

        rr = [self.region(a) for a in reads]
        ww = [self.region(a) for a in writes]
        if self.psum_excl and eng != "tensor":
            extra = [(n_, 0, 4096) for (n_, l_, h_) in rr if n_.startswith("ps")]
            rr = rr + extra
            if os.environ.get("KB_PSRW", "1") == "1":
                ww = ww + extra
        if self.coarse_pe and eng in self.ce:
            rr = [(n_, 0, 1 << 40) if n_ == "arena" else (n_, l_, h_) for (n_, l_, h_) in rr]
            ww = [(n_, 0, 1 << 40) if n_ == "arena" else (n_, l_, h_) for (n_, l_, h_) in ww]
        for name, lo, hi in rr:
            for (l, h, sig) in self.W.get(name, ()):
                if l < hi and lo < h:
                    add(sig)
        for name, lo, hi in ww:
            for (l, h, sig) in self.W.get(name, ()):
                if l < hi and lo < h:
                    add(sig)
            for (l, h, k), v in self.R.get(name, {}).items():
                if l < hi and lo < h:
                    add((k, v))
        return need, rr, ww

    def _commit(self, rr, ww, sig):
        k, v = sig
        for name, lo, hi in rr:
            self.R.setdefault(name, {})[(lo, hi, k)] = v
        for name, lo, hi in ww:
            wl = [(l, h, s) for (l, h, s) in self.W.get(name, []) if not (lo <= l and h <= hi)]
            wl.append((lo, hi, sig))
            self.W[name] = wl
            rd = self.R.get(name)
            if rd:
                for key in [key for key in rd if lo <= key[0] and key[1] <= hi]:
                    del rd[key]

    def _waits(self, eng, need):
        out = []
        wd = self.waited[eng]
        for k, v in need.items():
            if wd.get(k, 0) < v:
                wd[k] = v
                out.append((k, v))
                if isinstance(k, str):
                    self.needed.setdefault(k, set()).add(v)
        return out

    def op(self, eng, fn, reads=(), writes=()):
        need, rr, ww = self._deps(eng, reads, writes)
        waits = self._waits(eng, need)
        self.cnt[eng] += 1
        sig = (eng, self.cnt[eng])
        self.ops[eng].append((waits, fn, ("E", self.cnt[eng])))
        self._commit(rr, ww, sig)

    def dma(self, out, in_, q="sync", rd=True, wr=True, after=()):
        reads = [in_] if rd else []
        writes = [out] if wr else []
        need, rr, ww = self._deps(q, reads, writes)
        for (ka, va) in after:
            if need.get(ka, 0) < va:
                need[ka] = va
        nd = len(self.dsems)
        nsw = nd // 3
        if q == "gpsimd":
            k = nd - nsw + self.dnext_sw
            self.dnext_sw = (self.dnext_sw + 1) % nsw
        else:
            k = self.dnext
            self.dnext = (self.dnext + 1) % (nd - nsw)
        if self.dcnt[k] > 0:
            if need.get(k, 0) < self.dcnt[k]:
                need[k] = self.dcnt[k]
        waits = self._waits(q, need)
        self.dcnt[k] += 16
        sig = (k, self.dcnt[k])
        self.ops[q].append((waits, lambda e, o=out, i=in_: e.dma_start(out=o, in_=i, allow_slow_non_contiguous=True), ("D", k)))
        self._commit(rr, ww, sig)
        return sig

    def mm(self, out, lhsT, rhs, start=True, stop=True):
        self.op("tensor", lambda e: e.matmul(out, lhsT=lhsT, rhs=rhs, start=start, stop=stop),
                reads=[lhsT, rhs], writes=[out])

    def tr(self, out, in_, ident):
        self.op("tensor", lambda e: e.transpose(out, in_, ident), reads=[in_, ident], writes=[out])

    def act(self, out, in_, func, bias=None, scale=1.0, accum_out=None, eng="scalar"):
        rd = [in_]
        kw = {}
        if bias is not None:
            kw["bias"] = bias
            if not isinstance(bias, (int, float)):
                rd.append(bias)
        if not isinstance(scale, (int, float)):
            rd.append(scale)
        wr = [out]
        if accum_out is not None:
            kw["accum_out"] = accum_out
            wr.append(accum_out)
        self.op("scalar", lambda e: e.activation(out=out, in_=in_, func=func, scale=scale, **kw),
                reads=rd, writes=wr)

    def ascale(self, out, in_, scale_ap):
        self.op("scalar", lambda e: e.activation(out=out, in_=in_, func=AF.Copy, scale=scale_ap),
                reads=[in_, scale_ap], writes=[out])

    def tt(self, eng, out, in0, in1, op):
        self.op(eng, lambda e: e.tensor_tensor(out=out, in0=in0, in1=in1, op=op), reads=[in0, in1], writes=[out])

    def ts(self, eng, out, in0, s1, s2, op0, op1=None):
        rd = [in0] + [s for s in (s1, s2) if s is not None and not isinstance(s, (int, float))]
        if op1 is None:
            self.op(eng, lambda e: e.tensor_scalar(out=out, in0=in0, scalar1=s1, scalar2=None, op0=op0),
                    reads=rd, writes=[out])
        else:
            self.op(eng, lambda e: e.tensor_scalar(out=out, in0=in0, scalar1=s1, scalar2=s2, op0=op0, op1=op1),
                    reads=rd, writes=[out])

    def stt(self, eng, out, in0, scalar, in1, op0, op1):
        rd = [in0, in1] + ([] if isinstance(scalar, (int, float)) else [scalar])
        self.op(eng, lambda e: e.scalar_tensor_tensor(out=out, in0=in0, scalar=scalar, in1=in1, op0=op0, op1=op1),
                reads=rd, writes=[out])

    def cp(self, eng, out, in_):
        if eng == "scalar":
            self.op(eng, lambda e: e.copy(out=out, in_=in_), reads=[in_], writes=[out])
        else:
            self.op(eng, lambda e: e.tensor_copy(out=out, in_=in_), reads=[in_], writes=[out])

    def rsqrt(self, out, in_, eps):
        self.act(out, in_, AF.Sqrt, bias=eps)
        self.op("vector", lambda e: e.reciprocal(out=out, in_=out), reads=[out], writes=[out])

    def memset(self, eng, out, val):
        self.op(eng, lambda e: e.memset(out, val), reads=[], writes=[out])

    def scan(self, out, d0, d1, init, op0, op1):
        rd = [d0, d1] + ([] if isinstance(init, (int, float)) else [init])
        self.op("vector", lambda e: e.tensor_tensor_scan(out=out, data0=d0, data1=d1, initial=init, op0=op0, op1=op1),
                reads=rd, writes=[out])

    def emit(self, block):
        kb = self
        import bisect
        for e2 in ("tensor", "vector", "scalar", "gpsimd"):
            if kb.cnt[e2] > 0:
                kb.needed.setdefault(e2, set()).add(kb.cnt[e2])
        order = {k: sorted(v) for k, v in kb.needed.items()}

        def semval(k, v):
            if isinstance(k, str):
                return bisect.bisect_right(order[k], v)
            return v

        def run(eng, name):
            for waits, fn, inc in kb.ops[name]:
                for k, v in waits:
                    eng.wait_ge(kb._sem(k), semval(k, v))
                ins = fn(eng)
                if inc[0] == "D":
                    ins.then_inc(kb.dsems[inc[1]], 16)
                elif inc[1] in kb.needed.get(name, ()):
                    ins.then_inc(kb.esems[name], 1)
            if name == "sync":
                for k, c in enumerate(kb.dcnt):
                    if c > 0:
                        eng.wait_ge(kb.dsems[k], c)
                for e2 in ("tensor", "vector", "scalar", "gpsimd"):
                    if kb.cnt[e2] > 0:
                        eng.wait_ge(kb.esems[e2], semval(e2, kb.cnt[e2]))

        @block.tensor
        def _(e):
            run(e, "tensor")

        @block.vector
        def _(e):
            run(e, "vector")

        @block.scalar
        def _(e):
            run(e, "scalar")

        @block.gpsimd
        def _(e):
            run(e, "gpsimd")

        @block.sync
        def _(e):
            run(e, "sync")


class Carver:
    def __init__(self, arena, n):
        self.a = arena
        self.n = n
        self.pos = 0
        self.marks = []

    def f32(self, nelem, parts=128):
        off = self.pos
        self.pos += nelem
        assert self.pos <= self.n, ("sbuf arena overflow", self.pos, self.n)
        return self.a[0:parts, off:off + nelem]

    def bf(self, nelem, parts=128):
        n32 = (nelem + 1) // 2
        v = self.f32(n32, parts)
        return v.bitcast(BF16)[:, 0:nelem]

    def mark(self):
        self.marks.append(self.pos)

    def release(self):
        self.pos = self.marks.pop()


def build(stop_after=None):
    nc = bass.Bass("TRN2", target_bir_lowering=False)
    di = lambda n, s, dt=F32: nc.dram_tensor(n, list(s), dt, kind="ExternalInput").ap()
    do = lambda n, s, dt=F32: nc.dram_tensor(n, list(s), dt, kind="ExternalOutput").ap()
    dsx = lambda n, s, dt=F32: nc.dram_tensor(n, list(s), dt, kind="Internal").ap()
    xa_d = di("xa", [TT, D]); xm_d = di("xm", [MYT, D]); crow_d = di("crow", [17, D])
    w_ada_d = di("w_ada", [D, 6 * D]); b_ada_d = di("b_ada", [6 * D])
    w_in_d = di("w_in", [D, INC]); w_out_d = di("w_out", [D, D])
    w_gu_d = di("w_gu", [D, 2 * DFF]); w_down_d = di("w_down", [DFF, D])
    mbias_d = di("mbias", [4, 2]); mnorm_d = di("m_norm_g", [1024]); gnorm_d = di("g_norm_g", [1024])
    convw_d = di("convwT", [3072, 4]); gpar_d = di("gpar", [8, 2])
    ln_d = di("lnp", [4, D])
    sC_d = di("sC", [NSEQ, 4, 256, 256]); sn_d = di("sn", [NSEQ, 4, 256]); sm_d = di("smT", [4, NSEQ])
    sS_d = di("sS", [NSEQ, 8, 128, 128]); sconv_d = di("sconv", [48, 3072])
    cst_d = di("cst", [128, 1024]); sel_d = di("sel", [128, 256])
    ym_o = do("ym", [MYT, D])
    pC_o = do("pC", [4, 256, 256]); pn_o = do("pn", [4, 256]); pm_o = do("pm", [4, 1])
    pS_o = do("pS", [8, 128, 128]); pconv_o = do("pconv", [3, 3072])
    oC_o = do("oC", [NSEQ, 4, 256, 256]); on_o = do("on", [NSEQ, 4, 256]); om_o = do("om", [4, NSEQ])
    oS_o = do("oS", [NSEQ, 8, 128, 128]); oconv_o = do("oconv", [48, 3072])
    ada_d = dsx("ada_s", [17, 6 * D]); x1_d = dsx("x1_s", [MYT, D]); y_d = dsx("y_s", [MYT, D])

    NA = 53200
    from contextlib import ExitStack
    with ExitStack() as es:
        arena = es.enter_context(nc.sbuf_tensor("arena", [128, NA], F32))
        ps = [es.enter_context(nc.psum_tensor(f"ps{i}", [128, 512], F32)) for i in range(8)]
        esems = {e: es.enter_context(nc.semaphore(f"s_{e}")) for e in KB.ENG}
        dsems = [es.enter_context(nc.semaphore(f"d_{i}")) for i in range(int(os.environ.get("KB_NDMA", "24")))]
        block = es.enter_context(nc.Block())
        kb = KB(nc, esems, dsems)
        coarse_set = set(os.environ.get("KB_COARSE", "").split(","))
        pe_set = set(os.environ.get("KB_COARSE_PE", "").split(","))

        def _phase(nm):
            kb.coarse = nm in coarse_set
            kb.coarse_pe = nm in pe_set
        kb.phase = _phase
        kb.phase("p0")
        cv = Carver(arena, NA)
        P = [p[:, :] for p in ps]

        cst = cv.f32(1024)
        kb.dma(cst, cst_d[:, :])
        ident = cst[:, 0:128]
        capT_p = cst[:, 128:256]; capE_p = cst[:, 256:384]; capT_s = cst[:, 384:512]; capE_s = cst[:, 512:640]
        ones_f = cst[:, 640:768]
        selc = cv.f32(256)
        kb.dma(selc, sel_d[:, :])
        identb = cv.bf(128)
        kb.cp("vector", identb, ident)
        selb = cv.bf(256)
        kb.cp("vector", selb, selc)
        lnp = None

        mixT_off = cv.pos
        mixT = cv.bf(16 * MYT).rearrange("p (c t) -> p c t", c=16)

        xbuf = [arena[:, mixT_off + i_ * D:mixT_off + (i_ + 1) * D] for i_ in range(4)]
        for tl_ in range(4):
            kb.dma(xbuf[tl_], xa_d[tl_ * 128:(tl_ + 1) * 128, :])
        cv.mark()
        c_sb = cv.f32(D)
        kb.dma(c_sb[0:17, :], crow_d[:, :])
        kb.act(c_sb[0:17, :], c_sb[0:17, :], AF.Silu)
        for kc in range(16):
            kb.tr(P[0][:, kc * 17:(kc + 1) * 17], c_sb[0:17, kc * 128:(kc + 1) * 128], ident[0:17, 0:17])
        siluT = cv.bf(272)
        kb.cp("vector", siluT, P[0][:, 0:272])
        wab = [cv.bf(16 * 512).rearrange("p (k n) -> p k n", k=16) for _ in range(4)]
        bbb = [cv.f32(512) for _ in range(4)]
        aob = [cv.f32(512) for _ in range(4)]
        for cb in range(24):
            wb = wab[cb % 4]; bb = bbb[cb % 4]; ao = aob[cb % 4]
            kb.dma(wb, w_ada_d[:, cb * 512:(cb + 1) * 512].rearrange("(k p) n -> p k n", p=128), q="gpsimd")
            kb.dma(bb[0:17, :], b_ada_d[cb * 512:(cb + 1) * 512].partition_broadcast(17))
            pp = P[1 + cb % 2]
            for kc in range(16):
                kb.mm(pp[0:17, :], siluT[:, kc * 17:(kc + 1) * 17], wb[:, kc, :], start=(kc == 0), stop=(kc == 15))
            kb.tt("vector", ao[0:17, :], pp[0:17, :], bb[0:17, :], ALU.add)
            kb.dma(ada_d[:, cb * 512:(cb + 1) * 512], ao[0:17, :])
        cv.release()

        def load_mod(dst, col0, prompt, plus1):
            if prompt:
                kb.dma(dst, ada_d[0, col0:col0 + D].partition_broadcast(128))
            else:
                kb.memset("gpsimd", dst, 0.0)
                for t in range(4):
                    kb.dma(dst[t * 16:(t + 1) * 16, :], ada_d[1:17, col0:col0 + D])
            if plus1:
                kb.ts("gpsimd", dst, dst, 1.0, None, ALU.add)

        kb.phase("p1a")
        hT_off = cv.pos
        hT = cv.bf(16 * TT).rearrange("p (c t) -> p c t", c=16)
        cv.mark()
        modA = cv.f32(D); modB = cv.f32(D)
        load_mod(modA, 2048, True, True)
        load_mod(modB, 0, True, False)
        for tl in range(NT):
            if tl == 16:
                load_mod(modA, 2048, False, True)
                load_mod(modB, 0, False, False)
            xs = xbuf[tl % 4]
            kb.tt("vector", xs, xs, modA, ALU.mult)
            kb.tt("gpsimd", xs, xs, modB, ALU.add)
            for g in range(4):
                pp = P[(tl * 4 + g) % 4]
                for c in range(4):
                    kb.tr(pp[:, c * 128:(c + 1) * 128], xs[:, (g * 4 + c) * 128:(g * 4 + c + 1) * 128], ident)
                kb.cp("scalar" if g % 2 else "vector", hT[:, g * 4:(g + 1) * 4, tl * 128:(tl + 1) * 128],
                      pp.rearrange("p (c t) -> p c t", c=4))
            if tl + 4 < NT:
                kb.dma(xbuf[tl % 4], xa_d[(tl + 4) * 128:(tl + 5) * 128, :])
        cv.release()
        if stop_after == "hT":
            dbg = do("dbg", [128, 16 * TT], BF16)
            kb.dma(dbg.rearrange("p (c t) -> p c t", c=16), hT)
            kb.emit(block)
            return nc

        kb.phase("gates")
        scan_phase(nc, kb, cv, P, locals())
        cv.pos = hT_off
        kb.phase("p2")

        phase2(nc, kb, cv, P, locals())
        kb.emit(block)
    return nc


def _consts():
    c = np.zeros((128, 1024), np.float32)
    c[:, 0:128] = np.eye(128, dtype=np.float32)
    idx = np.arange(128)
    s = idx[:, None]; l = idx[None, :]
    c[:, 128:256] = np.where(l >= s, 0.0, NEG)
    c[:, 256:384] = np.where(idx[:, None] > idx[None, :], 0.0, -NEG)
    tt_ = idx // 16; bb_ = idx % 16
    same = (bb_[:, None] == bb_[None, :]) & (idx[:, None] < 64) & (idx[None, :] < 64)
    selfp = (idx[:, None] == idx[None, :])
    okT = (same & (tt_[None, :] >= tt_[:, None])) | selfp
    c[:, 384:512] = np.where(okT, 0.0, NEG)
    okE = same & (tt_[:, None] > tt_[None, :])
    c[:, 512:640] = np.where(okE, 0.0, -NEG)
    c[:, 640:768] = 1.0
    return c


def make_in_maps(inp):
    f = lambda a: np.ascontiguousarray(np.asarray(a, dtype=np.float32))
    xp = f(inp["x_prompt"]); xs = f(inp["x_sample"])
    cst = _consts()
    selh = np.zeros((8, 1024 + 2048), np.float32)
    for h in range(8):
        selh[h, h * 128:(h + 1) * 128] = 1.0
    selh[:, 1024:] = 1.0
    selh[:, 1024::128] = 0.0
    maskP = np.zeros((128, 16), np.float32)
    for p_ in range(64):
        maskP[p_, p_ % 16] = 1.0
    maps = []
    for c in range(8):
        p, j = c // 2, c % 2
        sq = slice(c * NSEQ, (c + 1) * NSEQ)
        xs_tb = xs[sq].transpose(1, 0, 2).reshape(64, D)
        xa = np.zeros((TT, D), np.float32)
        xa[0:2048] = xp[p]; xa[2048:2112] = xs_tb
        xm = np.concatenate([xp[p, j * 1024:(j + 1) * 1024], xs_tb], 0)
        crow = np.concatenate([f(inp["c_prompt"])[p:p + 1], f(inp["c_sample"])[sq]], 0)
        sel = np.zeros((128, 256), np.float32)
        sel[:, 0:128] = np.eye(128) * (1.0 if j == 0 else 0.0)
        sel[:, 128:256] = np.eye(128) * (1.0 if j == 1 else 0.0)
        m = {
            "xa": xa, "xm": f(xm), "crow": f(crow),
            "w_ada": f(inp["w_ada"]), "b_ada": f(inp["b_ada"]), "w_in": f(inp["w_in"]),
            "w_out": f(inp["w_out"]), "w_gu": f(inp["w_gu"]), "w_down": f(inp["w_down"]),
            "mbias": f(np.stack([inp["m_i_bias"], inp["m_f_bias"]], 1)),
            "m_norm_g": f(inp["m_norm_g"]), "g_norm_g": f(inp["g_norm_g"]),
            "convwT": f(np.asarray(inp["conv_w"]).T),
            "gpar": f(np.stack([inp["g_dt_bias"], inp["g_A_log"]], 1)),
            "lnp": f(np.stack([inp["ln1_g"], inp["ln1_b"], inp["ln2_g"], inp["ln2_b"]], 0)),
            "sC": f(inp["state_mlstm_C"])[sq], "sn": f(inp["state_mlstm_n"])[sq],
            "smT": f(np.asarray(inp["state_mlstm_m"])[sq].T),
            "sS": f(inp["state_gdn_S"])[sq],
            "sconv": f(np.asarray(inp["state_gdn_conv"])[sq].transpose(1, 0, 2).reshape(48, 3072)),
            "cst": cst, "sel": sel, "selh": selh, "maskP": maskP,
        }
        maps.append(m)
    return maps


def layer_norm_rows(kb, z, junk, st, np_, gam, bet, out):
    inv = 1.0 / D
    zz = z[0:np_, :]
    kb.op("vector", lambda e: e.tensor_reduce(out=st[0:np_, 0:1], in_=zz, axis=AX.X, op=ALU.add),
          reads=[zz], writes=[st[0:np_, 0:1]])
    kb.act(junk[0:np_, :], zz, AF.Square, accum_out=st[0:np_, 1:2])
    kb.ts("vector", st[0:np_, 2:3], st[0:np_, 0:1], inv, None, ALU.mult)
    kb.stt("vector", st[0:np_, 3:4], st[0:np_, 2:3], -1.0, st[0:np_, 2:3], ALU.mult, ALU.mult)
    kb.stt("vector", st[0:np_, 4:5], st[0:np_, 1:2], inv, st[0:np_, 3:4], ALU.mult, ALU.add)
    kb.rsqrt(st[0:np_, 5:6], st[0:np_, 4:5], 1e-5)
    kb.ts("vector", zz, zz, st[0:np_, 2:3], st[0:np_, 5:6], ALU.subtract, ALU.mult)
    kb.tt("gpsimd", zz, zz, gam[0:np_, :], ALU.mult)
    kb.tt("vector", out[0:np_, :], zz, bet[0:np_, :], ALU.add)


def phase2(nc, kb, cv, P, L):
    mixT = L["mixT"]; ident = L["ident"]; load_mod = L["load_mod"]
    xm_d = L["xm_d"]; ln_d = L["ln_d"]; x1_d = L["x1_d"]; y_d = L["y_d"]; ym_o = L["ym_o"]
    w_out_d = L["w_out_d"]; w_gu_d = L["w_gu_d"]; w_down_d = L["w_down_d"]
    tiles = [(i * 128, 128) for i in range(8)] + [(1024, 64)]
    cv.mark()
    h2T = cv.bf(16 * MYT).rearrange("p (c t) -> p c t", c=16)
    cv.mark()
    wo = cv.bf(16 * D).rearrange("p (k n) -> p k n", k=16)
    for q4 in range(4):
        kb.dma(wo[:, :, q4 * 512:(q4 + 1) * 512],
               w_out_d[:, q4 * 512:(q4 + 1) * 512].rearrange("(k p) n -> p k n", p=128), q="gpsimd")
    G1 = cv.f32(D); LG = cv.f32(D); LB = cv.f32(D); SC2 = cv.f32(D); SH2 = cv.f32(D)
    xt = [cv.f32(D)] * 2; zt = [cv.f32(D) for _ in range(2)]
    st8 = cv.f32(16)
    load_mod(G1, 2 * D, True, True); load_mod(SC2, 4 * D, True, True); load_mod(SH2, 3 * D, True, False)
    kb.dma(LG, ln_d[0, :].partition_broadcast(128)); kb.dma(LB, ln_d[1, :].partition_broadcast(128))
    def ln_stats(z, junk, st_, np_):
        inv = 1.0 / D
        zz = z[0:np_, :]
        kb.op("vector", lambda e: e.tensor_reduce(out=st_[0:np_, 0:1], in_=zz, axis=AX.X, op=ALU.add),
              reads=[zz], writes=[st_[0:np_, 0:1]])
        kb.act(junk[0:np_, :], zz, AF.Square, accum_out=st_[0:np_, 1:2])
        yield
        kb.ts("vector", st_[0:np_, 2:3], st_[0:np_, 0:1], inv, None, ALU.mult)
        yield
        kb.stt("vector", st_[0:np_, 3:4], st_[0:np_, 2:3], -1.0, st_[0:np_, 2:3], ALU.mult, ALU.mult)
        yield
        kb.stt("vector", st_[0:np_, 4:5], st_[0:np_, 1:2], inv, st_[0:np_, 3:4], ALU.mult, ALU.add)
        yield
        kb.act(st_[0:np_, 5:6], st_[0:np_, 4:5], AF.Ln, bias=1e-5)
        yield
        kb.act(st_[0:np_, 5:6], st_[0:np_, 5:6], AF.Exp, scale=-0.5)
        yield
        kb.ts("vector", zz, zz, st_[0:np_, 2:3], st_[0:np_, 5:6], ALU.subtract, ALU.mult)
        yield

    def tile2a(i, t0, np_):
        x = xt[0]; z = zt[i % 2]; st_ = st8[:, (i % 2) * 8:(i % 2) * 8 + 8]
        kb.dma(x[0:np_, :], xm_d[t0:t0 + np_, :])
        for cb in range(4):
            for kc in range(16):
                kb.mm(P[cb][0:np_, :], mixT[:, kc, t0:t0 + np_], wo[:, kc, cb * 512:(cb + 1) * 512],
                      start=(kc == 0), stop=(kc == 15))
        yield
        for cb in range(4):
            kb.tt("vector", z[0:np_, cb * 512:(cb + 1) * 512], P[cb][0:np_, :], G1[0:np_, cb * 512:(cb + 1) * 512], ALU.mult)
        yield
        kb.stt("vector", z[0:np_, :], x[0:np_, :], ALPHA, z[0:np_, :], ALU.mult, ALU.add)
        yield
        for _ in ln_stats(z, x, st_, np_):
            yield
        kb.tt("gpsimd", z[0:np_, :], z[0:np_, :], LG[0:np_, :], ALU.mult)
        yield
        kb.tt("vector", z[0:np_, :], z[0:np_, :], LB[0:np_, :], ALU.add)
        yield
        kb.dma(x1_d[t0:t0 + np_, :], z[0:np_, :])
        yield
        kb.tt("vector", z[0:np_, :], z[0:np_, :], SC2[0:np_, :], ALU.mult)
        yield
        kb.tt("gpsimd", z[0:np_, :], z[0:np_, :], SH2[0:np_, :], ALU.add)
        yield
        for g in range(4):
            pp = P[4 + g]
            for c in range(4):
                kb.tr(pp[:, c * 128:c * 128 + np_], z[0:np_, (g * 4 + c) * 128:(g * 4 + c + 1) * 128], ident[0:np_, 0:np_])
        yield
        for g in range(4):
            pp = P[4 + g]
            kb.cp("scalar" if g % 2 else "vector", h2T[:, g * 4:(g + 1) * 4, t0:t0 + np_],
                  pp.rearrange("p (c t) -> p c t", c=4)[:, :, 0:np_])

    def run_window(genfns, W, S=1):
        active = []; nxt = 0; last_steps = [S]
        while nxt < len(genfns) or active:
            if nxt < len(genfns) and len(active) < W and last_steps[0] >= S:
                active.append(genfns[nxt]()); nxt += 1; last_steps[0] = 0
            last_steps[0] += 1
            for g_ in list(active):
                try:
                    next(g_)
                except StopIteration:
                    active.remove(g_)

    run_window([(lambda i=i, t0=t0, np_=np_: tile2a(i, t0, np_)) for i, (t0, np_) in enumerate(tiles[:8])], 2, 5)
    load_mod(G1, 2 * D, False, True); load_mod(SC2, 4 * D, False, True); load_mod(SH2, 3 * D, False, False)
    for _ in tile2a(8, 1024, 64):
        pass
    cv.release()
    actT = cv.bf(44 * MYT).rearrange("p (f t) -> p f t", f=44)
    cv.mark()
    wgb = [cv.bf(16 * 256).rearrange("p (k n) -> p k n", k=16) for _ in range(4)]
    tmp = [cv.f32(512) for _ in range(2)]
    tbs = [(0, 512), (512, 512), (1024, 64)]
    it = 0
    for f in range(44):
        wb = wgb[f % 4]
        kb.dma(wb[:, :, 0:128], w_gu_d[:, f * 128:(f + 1) * 128].rearrange("(k p) n -> p k n", p=128), q="gpsimd")
        kb.dma(wb[:, :, 128:256], w_gu_d[:, DFF + f * 128:DFF + (f + 1) * 128].rearrange("(k p) n -> p k n", p=128), q="gpsimd")
        for (t0, n) in tbs:
            pg = P[(2 * it) % 8]; pu = P[(2 * it + 1) % 8]; tm = tmp[it % 2]; it += 1
            for kc in range(16):
                kb.mm(pg[:, 0:n], wb[:, kc, 0:128], h2T[:, kc, t0:t0 + n], start=(kc == 0), stop=(kc == 15))
            for kc in range(16):
                kb.mm(pu[:, 0:n], wb[:, kc, 128:256], h2T[:, kc, t0:t0 + n], start=(kc == 0), stop=(kc == 15))
            kb.act(tm[:, 0:n], pg[:, 0:n], AF.Silu)
            kb.tt("vector", actT[:, f, t0:t0 + n], tm[:, 0:n], pu[:, 0:n], ALU.mult)
    cv.release()
    cv.mark()
    save = cv.pos
    cv.pos = L["mixT_off"]
    wd0 = cv.bf(44 * 256).rearrange("p (f n) -> p f n", f=44)
    yst = [cv.f32(256) for _ in range(4)]
    cv.pos = save
    wdb = [wd0, cv.bf(44 * 256).rearrange("p (f n) -> p f n", f=44)]
    ysigs = []
    it = 0
    for cb in range(8):
        wb = wdb[cb % 2]
        kb.dma(wb, w_down_d[:, cb * 256:(cb + 1) * 256].rearrange("(f p) n -> p f n", p=128), q="gpsimd")
        for (t0, np_) in tiles:
            pp = P[it % 8]; ys = yst[it % 4]; it += 1
            for f in range(44):
                kb.mm(pp[0:np_, 0:256], actT[:, f, t0:t0 + np_], wb[:, f, :], start=(f == 0), stop=(f == 43))
            kb.cp("scalar" if it % 2 else "vector", ys[0:np_, :], pp[0:np_, 0:256])
            ysigs.append(kb.dma(y_d[t0:t0 + np_, cb * 256:(cb + 1) * 256], ys[0:np_, :], wr=False))
    cv.release()
    cv.release()
    cv.mark()
    G2 = cv.f32(D); LG2 = cv.f32(D); LB2 = cv.f32(D)
    xt = [cv.f32(D) for _ in range(2)]; zt = [cv.f32(D) for _ in range(2)]
    st8 = cv.f32(16)
    load_mod(G2, 5 * D, True, True)
    kb.dma(LG2, ln_d[2, :].partition_broadcast(128)); kb.dma(LB2, ln_d[3, :].partition_broadcast(128))
    def tileln2(i, t0, np_):
        x = xt[i % 2]; z = zt[i % 2]; st_ = st8[:, (i % 2) * 8:(i % 2) * 8 + 8]
        kb.dma(z[0:np_, :], y_d[t0:t0 + np_, :], rd=False, after=ysigs)
        kb.dma(x[0:np_, :], x1_d[t0:t0 + np_, :])
        yield
        kb.tt("vector", z[0:np_, :], z[0:np_, :], G2[0:np_, :], ALU.mult)
        yield
        kb.stt("vector", z[0:np_, :], x[0:np_, :], ALPHA, z[0:np_, :], ALU.mult, ALU.add)
        yield
        for _ in ln_stats(z, x, st_, np_):
            yield
        kb.tt("gpsimd", z[0:np_, :], z[0:np_, :], LG2[0:np_, :], ALU.mult)
        yield
        kb.tt("vector", z[0:np_, :], z[0:np_, :], LB2[0:np_, :], ALU.add)
        yield
        kb.dma(ym_o[t0:t0 + np_, :], z[0:np_, :])

    run_window([(lambda i=i, t0=t0, np_=np_: tileln2(i, t0, np_)) for i, (t0, np_) in enumerate(tiles[:8])], 2)
    load_mod(G2, 5 * D, False, True)
    for _ in tileln2(8, 1024, 64):
        pass
    cv.release()


def bc3(ap, shape, axis):
    return ap.unsqueeze(axis).to_broadcast(list(shape))


def proj_fm(kb, P, pit, wb, hT, dst_fn, evac):
    for (t0, n) in [(0, 512), (512, 512), (1024, 512), (1536, 512), (2048, 128)]:
        pp = P[pit[0] % 2]; pit[0] += 1
        for kc in range(16):
            kb.mm(pp[:, 0:n], wb[:, kc, :], hT[:, kc, t0:t0 + n], start=(kc == 0), stop=(kc == 15))
        evac(pp[:, 0:n], t0, n)


def scan_phase(nc, kb, cv, P, L):
    hT = L["hT"]; ident = L["ident"]; identb = L["identb"]; selb = L["selb"]; mixT = L["mixT"]
    w_in_d = L["w_in_d"]; cst = L["cst"]
    capT_p = L["capT_p"]; capE_p = L["capE_p"]; capT_s = L["capT_s"]; capE_s = L["capE_s"]
    di = lambda n, s, dt=F32: nc.dram_tensor(n, list(s), dt, kind="ExternalInput").ap()
    selh_d = di("selh", [8, 1024 + 2048]); maskP_d = di("maskP", [128, 16])
    cv.mark()
    selh = cv.f32(1024, parts=8)
    kb.dma(selh, selh_d[:, 0:1024])
    maskB = None
    maskP = cv.f32(16); kb.dma(maskP, maskP_d[:, :])
    maskPb = cv.bf(16); kb.cp("vector", maskPb, maskP)
    MR = cv.f32(TT, parts=8); BR = cv.f32(TT, parts=8)
    NQ = 48
    cols = cv.f32(NT * NQ).rearrange("p (t q) -> p t q", t=NT)
    dcol = cv.f32(NT * 48).rearrange("p (t q) -> p t q", t=NT)
    decB = cv.f32(64); ebLB = cv.f32(128 + 128)
    mb = cv.f32(2, parts=8); gp = cv.f32(4, parts=8)
    kb.dma(mb[0:4, :], L["mbias_d"][:, :]); kb.dma(gp[:, 0:2], L["gpar_d"][:, :])
    wbuf = [cv.bf(16 * 256).rearrange("p (k n) -> p k n", k=16) for _ in range(2)]
    wit = [0]

    def loadw(c0, n):
        wb = wbuf[wit[0] % 2]; wit[0] += 1
        kb.dma(wb[:, :, 0:n], w_in_d[:, c0:c0 + n].rearrange("(k p) n -> p k n", p=128), q="gpsimd")
        return wb

    pit = [0]
    cv.mark()
    Ri = cv.f32(TT, parts=8); Rf = cv.f32(TT, parts=8); Rb = cv.f32(TT, parts=8); Ra = cv.f32(TT, parts=8)
    Rt = cv.f32(TT, parts=8); m0r = cv.f32(128, parts=8)
    flag = cv.f32(2048, parts=8); kb.dma(flag, selh_d[:, 1024:3072])
    wg = loadw(8192, 24)
    for (dst, c0, n) in ((Ri, 0, 4), (Rf, 4, 4), (Rb, 8, 8), (Ra, 16, 8)):
        def ev(pp, t0, nn, dst=dst, n=n):
            kb.cp("vector", dst[0:n, t0:t0 + nn], pp[0:n, :])
        for (t0, nn) in [(0, 512), (512, 512), (1024, 512), (1536, 512), (2048, 128)]:
            pp = P[pit[0] % 2]; pit[0] += 1
            for kc in range(16):
                kb.mm(pp[0:n, 0:nn], wg[:, kc, c0:c0 + n], hT[:, kc, t0:t0 + nn], start=(kc == 0), stop=(kc == 15))
            ev(pp[:, 0:nn], t0, nn)
    S0 = 2048
    kb.ts("vector", Ri[0:4, :], Ri[0:4, :], mb[0:4, 0:1], None, ALU.add)
    kb.ts("vector", Rf[0:4, :], Rf[0:4, :], mb[0:4, 1:2], None, ALU.add)
    kb.act(Rf[0:4, :], Rf[0:4, :], AF.Exp, scale=-1.0)
    kb.act(Rf[0:4, :], Rf[0:4, :], AF.Ln, bias=1.0)
    kb.ts("vector", Rf[0:4, :], Rf[0:4, :], -1.0, None, ALU.mult)
    Bm = Rt
    ones_bc = cst[0:4, 640:641].to_broadcast([4, 2048])
    kb.scan(Bm[0:4, 0:2048], ones_bc, Rf[0:4, 0:2048], 0.0, ALU.mult, ALU.add)
    kb.cp("vector", Bm[0:4, S0:S0 + 16], Rf[0:4, S0:S0 + 16])
    for t in range(1, 4):
        kb.tt("vector", Bm[0:4, S0 + 16 * t:S0 + 16 * t + 16], Bm[0:4, S0 + 16 * t - 16:S0 + 16 * t], Rf[0:4, S0 + 16 * t:S0 + 16 * t + 16], ALU.add)
    kb.cp("vector", Bm[0:4, S0 + 64:TT], Rf[0:4, S0 + 64:TT])
    Am = Ri
    kb.tt("vector", Am[0:4, :], Ri[0:4, :], Bm[0:4, :], ALU.subtract)
    kb.scan(MR[0:4, 0:2048], Am[0:4, 0:2048], Am[0:4, 0:2048], 0.0, ALU.max, ALU.max)
    kb.dma(m0r[0:4, 0:16], L["sm_d"][:, :])
    kb.memset("vector", m0r[0:4, 16:128], 0.0)
    for t in range(1, 4):
        kb.cp("vector", m0r[0:4, 16 * t:16 * t + 16], m0r[0:4, 0:16])
    kb.tt("vector", MR[0:4, S0:S0 + 16], Am[0:4, S0:S0 + 16], m0r[0:4, 0:16], ALU.max)
    for t in range(1, 4):
        kb.tt("vector", MR[0:4, S0 + 16 * t:S0 + 16 * t + 16], MR[0:4, S0 + 16 * t - 16:S0 + 16 * t], Am[0:4, S0 + 16 * t:S0 + 16 * t + 16], ALU.max)
    kb.cp("vector", MR[0:4, S0 + 64:TT], Am[0:4, S0 + 64:TT])
    Wf = Rf
    kb.ts("vector", Wf[0:4, 0:2048], Am[0:4, 0:2048], MR[0:4, 2047:2048], None, ALU.subtract)
    for t in range(4):
        kb.tt("vector", Wf[0:4, S0 + 16 * t:S0 + 16 * t + 16], Am[0:4, S0 + 16 * t:S0 + 16 * t + 16], MR[0:4, S0 + 48:S0 + 64], ALU.subtract)
    kb.memset("vector", Wf[0:4, S0 + 64:TT], 0.0)
    mo = cv.f32(32, parts=8)
    kb.tt("vector", mo[0:4, 0:1], Bm[0:4, 2047:2048], MR[0:4, 2047:2048], ALU.add)
    kb.tt("vector", mo[0:4, 16:32], Bm[0:4, S0 + 48:S0 + 64], MR[0:4, S0 + 48:S0 + 64], ALU.add)
    kb.dma(L["pm_o"][:, :], mo[0:4, 0:1]); kb.dma(L["om_o"][:, :], mo[0:4, 16:32])
    dec = cv.f32(16, parts=8)
    kb.tt("vector", dec[0:4, :], m0r[0:4, 0:16], MR[0:4, S0 + 48:S0 + 64], ALU.subtract)
    kb.act(dec[0:4, :], dec[0:4, :], AF.Exp)
    kb.act(Ra[0:8, :], Ra[0:8, :], AF.Exp, bias=gp[0:8, 0:1])
    kb.act(Ra[0:8, :], Ra[0:8, :], AF.Ln, bias=1.0)
    kb.act(gp[0:8, 2:3], gp[0:8, 1:2], AF.Exp)
    kb.ts("vector", gp[0:8, 3:4], gp[0:8, 2:3], -1.0, None, ALU.mult)
    kb.ts("vector", Ra[0:8, :], Ra[0:8, :], gp[0:8, 3:4], None, ALU.mult)
    kb.scan(BR[0:8, 0:2048], flag[0:8, :], Ra[0:8, 0:2048], 0.0, ALU.mult, ALU.add)
    kb.cp("vector", BR[0:8, S0:S0 + 16], Ra[0:8, S0:S0 + 16])
    for t in range(1, 4):
        kb.tt("vector", BR[0:8, S0 + 16 * t:S0 + 16 * t + 16], BR[0:8, S0 + 16 * t - 16:S0 + 16 * t], Ra[0:8, S0 + 16 * t:S0 + 16 * t + 16], ALU.add)
    kb.memset("vector", BR[0:8, S0 + 64:TT], 0.0)
    Dl = Ra
    kb.tt("vector", Dl[0:8, 0:2048].rearrange("p (c l) -> p c l", c=16),
          bc3(BR[0:8, 127:2048:128], [8, 16, 128], 2), BR[0:8, 0:2048].rearrange("p (c l) -> p c l", c=16), ALU.subtract)
    for t in range(4):
        kb.tt("vector", Dl[0:8, S0 + 16 * t:S0 + 16 * t + 16], BR[0:8, S0 + 48:S0 + 64], BR[0:8, S0 + 16 * t:S0 + 16 * t + 16], ALU.subtract)
    kb.memset("vector", Dl[0:8, S0 + 64:TT], 0.0)
    ebl = cv.f32(32, parts=8)
    kb.act(ebl[0:8, 0:16], BR[0:8, 127:2048:128], AF.Exp)
    kb.act(ebl[0:8, 16:32], BR[0:8, S0 + 48:S0 + 64], AF.Exp)
    for h in range(8):
        kb.mm(P[2][:, h * 32:(h + 1) * 32], selh[0:8, h * 128:(h + 1) * 128], ebl[0:8, 0:32])
    kb.cp("vector", ebLB, P[2][:, 0:256])
    for h in range(4):
        kb.mm(P[3][:, h * 16:(h + 1) * 16], selh[0:4, h * 128:(h + 1) * 128], dec[0:4, 0:16])
    kb.cp("vector", decB, P[3][:, 0:64])
    for tl in range(NT):
        pp = P[4 + tl % 4]
        sl = slice(tl * 128, (tl + 1) * 128)
        for qi, (R_, n) in enumerate(((Am, 4), (MR, 4), (Bm, 4), (Wf, 4))):
            kb.tr(pp[:, qi * 4:qi * 4 + 4], R_[0:4, sl], ident[0:4, 0:4])
        for qi, R_ in enumerate((BR, Rb, Dl)):
            kb.tr(pp[:, 16 + qi * 8:24 + qi * 8], R_[0:8, sl], ident[0:8, 0:8])
        if tl == 16:
            kb.tr(pp[:, 40:44], m0r[0:4, 0:128], ident[0:4, 0:4])
            kb.cp("vector", cols[:, tl, 0:44], pp[:, 0:44])
        else:
            kb.cp("vector" if tl % 2 else "scalar", cols[:, tl, 0:40], pp[:, 0:40])
    cv.release()
    kb.tt("vector", dcol[:, :, 0:4], cols[:, :, 4:8], cols[:, :, 8:12], ALU.add)
    kb.act(dcol[:, :, 0:4], dcol[:, :, 0:4], AF.Exp, scale=-1.0)
    kb.act(dcol[:, :, 4:8], cols[:, :, 12:16], AF.Exp)
    kb.act(dcol[:, :, 8:16], cols[:, :, 24:32], AF.Sigmoid)
    kb.act(dcol[:, :, 16:24], cols[:, :, 16:24], AF.Exp)
    kb.tt("vector", dcol[:, :, 16:24], dcol[:, :, 16:24], dcol[:, :, 8:16], ALU.mult)
    kb.ts("vector", dcol[:, :, 24:32], dcol[:, :, 8:16], -1.0, None, ALU.mult)
    kb.act(dcol[:, :, 32:40], cols[:, :, 32:40], AF.Exp)
    kb.ts("vector", dcol[:, :, 40:48], cols[:, :, 16:24], -1.0, None, ALU.mult)
    iw = cv.f32(4)
    kb.tt("vector", iw, cols[:, 16, 40:44], cols[:, 16, 4:8], ALU.subtract)
    kb.act(iw, iw, AF.Exp)
    G = dict(cols=cols, dcol=dcol, decB=decB, ebLB=ebLB, iw=iw, MR=MR, BR=BR, selh=selh, maskB=maskB,
             maskP=maskP, maskPb=maskPb, loadw=loadw, pit=pit, wbuf=wbuf)
    kb.phase("mlstm")
    mlstm_heads(nc, kb, cv, P, L, G)
    kb.phase("gdn")
    gdn_heads(nc, kb, cv, P, L, G)
    cv.release()


def sel_store(kb, P, pidx, mixT, src_bf, ch, tl, identb, selb):
    pp = P[pidx]
    if tl == 16:
        kb.mm(pp[:, 0:128], src_bf, identb, start=True, stop=True)
        kb.cp("vector", mixT[:, ch, 1024:1088], pp[:, 0:64])
    elif tl < 8:
        kb.mm(pp[:, 0:128], src_bf, selb[:, 0:128], start=True, stop=True)
        kb.cp("vector", mixT[:, ch, tl * 128:(tl + 1) * 128], pp[:, 0:128])
    else:
        i2 = tl - 8
        kb.mm(pp[:, 0:128], src_bf, selb[:, 128:256], start=True, stop=True)
        kb.tt("vector", mixT[:, ch, i2 * 128:(i2 + 1) * 128], pp[:, 0:128], mixT[:, ch, i2 * 128:(i2 + 1) * 128], ALU.add)


def mlstm_heads(nc, kb, cv, P, L, G):
    hT = L["hT"]; ident = L["ident"]; identb = L["identb"]; selb = L["selb"]; mixT = L["mixT"]
    capT_p = L["capT_p"]; capT_s = L["capT_s"]
    cols = G["cols"]; dcol = G["dcol"]; decB = G["decB"]; iw = G["iw"]; MR = G["MR"]; selh = G["selh"]
    maskP = G["maskP"]; loadw = G["loadw"]; pit = G["pit"]
    cv.mark()
    qT = cv.bf(2 * TT).rearrange("p (c t) -> p c t", c=2)
    kT = cv.bf(2 * TT).rearrange("p (c t) -> p c t", c=2)
    vx = cv.bf(NT * 258).rearrange("p (t v) -> p t v", t=NT)
    go = cv.bf(NT * 256).rearrange("p (t v) -> p t v", t=NT)
    negMB = cv.f32(512); Dt = cv.f32(512); tmpd = cv.f32(128)
    sw = [cv.bf(512) for _ in range(2)]
    hmix = [cv.bf(256) for _ in range(4)]
    junk = cv.f32(256); gnb = cv.f32(256); ec4 = cv.f32(64)
    kwt = [cv.bf(256) for _ in range(2)]
    kwm = [cv.bf(256) for _ in range(2)]
    dt2_off = cv.pos
    C0 = cv.f32(2 * 257).rearrange("p (c v) -> p c v", c=2)
    Dts = [Dt, L["arena"][:, dt2_off:dt2_off + 512]]
    C0b = cv.bf(2 * 258).rearrange("p (c v) -> p c v", c=2)
    Cout = cv.f32(2 * 257).rearrange("p (c v) -> p c v", c=2)
    stage = [cv.f32(257) for _ in range(2)]
    accS = cv.f32(257)
    n0all = cv.f32(32).rearrange("p (b c) -> p b c", c=2)
    noall = cv.f32(32).rearrange("p (b c) -> p b c", c=2)
    ar = L["arena"]
    nm_off = negMB.offset % L["NA"]
    st_off = stage[0].offset % L["NA"]
    C0_1 = ar[:, nm_off:nm_off + 514].rearrange("p (c v) -> p c v", c=2)
    C0b_1 = ar[:, nm_off + 514:nm_off + 514 + 258].bitcast(BF16).rearrange("p (c v) -> p c v", c=2)
    Cout_1 = ar[:, st_off:st_off + 514].rearrange("p (c v) -> p c v", c=2)
    CS = [(C0, C0b, Cout), (C0_1, C0b_1, Cout_1)]
    kb.memset("vector", vx[:, :, 256:258], 1.0)
    for h in range(4):
        kb.dma(gnb, L["mnorm_d"][h * 256:(h + 1) * 256].partition_broadcast(128))
        for (dst, c0, sc) in ((qT, h * 256, 1.0), (kT, 1024 + h * 256, 1.0 / 16.0)):
            wb = loadw(c0, 256)
            for cc in range(2):
                def ev(pp, t0, n, dst=dst, cc=cc, sc=sc):
                    kb.act(dst[:, cc, t0:t0 + n], pp, AF.Copy, scale=sc)
                proj_fm(kb, P, pit, wb[:, :, cc * 128:(cc + 1) * 128], hT, None, ev)
        wv = loadw(2048 + h * 256, 256)
        for tl in range(NT):
            pp = P[pit[0] % 2]; pit[0] += 1
            for kc in range(16):
                kb.mm(pp[:, 0:256], hT[:, kc, tl * 128:(tl + 1) * 128], wv[:, kc, 0:256], start=(kc == 0), stop=(kc == 15))
            kb.cp("vector", vx[:, tl, 0:256], pp[:, 0:256])
        wo_ = loadw(3072 + h * 256, 256)
        for tl in range(NT):
            pp = P[pit[0] % 2]; pit[0] += 1
            for kc in range(16):
                kb.mm(pp[:, 0:256], hT[:, kc, tl * 128:(tl + 1) * 128], wo_[:, kc, 0:256], start=(kc == 0), stop=(kc == 15))
            kb.act(junk, pp[:, 0:256], AF.Sigmoid)
            kb.tt("vector", go[:, tl, :], junk, gnb, ALU.mult)
        kb.memset("vector", accS, 0.0)
        for cc_ in range(2):
            kb.dma(n0all[:, :, cc_], L["sn_d"][:, h, cc_ * 128:(cc_ + 1) * 128].rearrange("b p -> p b"))
        def ldC(b_):
            C0_, _, _ = CS[b_ % 2]
            kb.dma(C0_[:, :, 0:256], L["sC_d"][b_, h].rearrange("(c p) v -> p c v", p=128))
            kb.cp("vector", C0_[:, :, 256:257], n0all[:, b_, :].unsqueeze(2))

        ldC(0)
        for b in range(NSEQ):
            C0, C0b, Cout = CS[b % 2]
            if b + 1 < NSEQ:
                ldC(b + 1)
            kb.cp("scalar", C0b[:, :, 0:257], C0)
            for cc in range(2):
                kb.mm(P[2][:, 0:257], qT[:, cc, 2048:2176], C0b[:, cc, 0:257], start=(cc == 0), stop=(cc == 1))
            kb.stt("vector", accS, P[2][:, 0:257], maskP[:, b:b + 1], accS, ALU.mult, ALU.add)
            for cc in range(2):
                kt = kwt[cc]
                if b == 0:
                    kb.tr(P[3].bitcast(BF16)[:, cc * 128:(cc + 1) * 128], kT[:, cc, 2048:2176], identb)
                    kb.ts("vector", kt[:, 0:128], P[3].bitcast(BF16)[:, cc * 128:(cc + 1) * 128], dcol[:, 16, 4 + h:5 + h], None, ALU.mult)
            km = kwm[b % 2]
            kb.ts("vector", km[:, 0:128], kwt[0][:, 0:128], maskP[:, b:b + 1], None, ALU.mult)
            kb.ts("vector", km[:, 128:256], kwt[1][:, 0:128], maskP[:, b:b + 1], None, ALU.mult)
            for cc in range(2):
                kb.mm(P[4 + cc][:, 0:257], km[:, cc * 128:(cc + 1) * 128], vx[:, 16, 0:257], start=True, stop=True)
                kb.stt("vector", Cout[:, cc, :], C0[:, cc, :], decB[:, h * 16 + b:h * 16 + b + 1], P[4 + cc][:, 0:257], ALU.mult, ALU.add)
            kb.dma(L["oC_o"][b, h].rearrange("(c p) v -> p c v", p=128), Cout[:, :, 0:256])
            kb.cp("vector", noall[:, b, :].unsqueeze(2), Cout[:, :, 256:257])
        for cc_ in range(2):
            kb.dma(L["on_o"][:, h, cc_ * 128:(cc_ + 1) * 128].rearrange("b p -> p b"), noall[:, :, cc_])
        blocks = [(0, 4), (4, 4), (8, 4), (12, 4), (16, 1)]
        for (tb0, ntl) in blocks:
            n = ntl * 128
            t0 = tb0 * 128
            kb.mm(P[0][:, 0:n], selh[0:4, h * 128:(h + 1) * 128], MR[0:4, t0:t0 + n], start=True, stop=True)
            kb.ts("vector", negMB[:, 0:n], P[0][:, 0:n], -1.0, None, ALU.mult)
            jlist = list(range(0, tb0 + ntl)) if tb0 < 16 else [16]

            def qk(j):
                jl = j - tb0
                c0 = max(jl, 0) * 128
                pq = P[6 + j % 2]
                for cc in range(2):
                    kb.mm(pq[:, c0:n], kT[:, cc, j * 128:(j + 1) * 128], qT[:, cc, t0 + c0:t0 + n], start=(cc == 0), stop=(cc == 1))

            qk(jlist[0])
            for ji, j in enumerate(jlist):
                jl = j - tb0
                c0 = max(jl, 0) * 128
                pq = P[6 + j % 2]
                Dt = Dts[j % 2]
                acol = cols[:, j, h:h + 1]
                if jl >= 0:
                    cap = capT_s if tb0 == 16 else capT_p
                    kb.stt("vector", tmpd, negMB[:, c0:c0 + 128], acol, cap, ALU.add, ALU.min)
                    kb.act(Dt[:, c0:c0 + 128], tmpd, AF.Exp)
                    if c0 + 128 < n:
                        kb.act(Dt[:, c0 + 128:n], negMB[:, c0 + 128:n], AF.Exp, bias=acol)
                else:
                    kb.act(Dt[:, c0:n], negMB[:, c0:n], AF.Exp, bias=acol)
                if ji + 1 < len(jlist):
                    qk(jlist[ji + 1])
                s_ = sw[j % 2]
                kb.tt("vector", s_[:, c0:n], pq[:, c0:n], Dt[:, c0:n], ALU.mult)
                for il in range(max(jl, 0), ntl):
                    i = tb0 + il
                    kb.mm(P[2 + il][:, 0:257], s_[:, il * 128:(il + 1) * 128], vx[:, j, 0:257],
                          start=(j == jlist[0]), stop=(j == i))

            def epi(il):
                i = tb0 + il
                acc = P[2 + il]
                e_ = ec4[:, il * 16:(il + 1) * 16]
                if i == 16:
                    kb.stt("vector", accS, accS, iw[:, h:h + 1], acc[:, 0:257], ALU.mult, ALU.add)
                    num = accS
                else:
                    num = acc
                kb.cp("vector", e_[:, 8:9], num[:, 256:257])
                kb.act(hmix[il], num[:, 0:256], AF.Square, accum_out=e_[:, 3:4])
                yield
                kb.stt("vector", e_[:, 0:1], e_[:, 8:9], -1.0, e_[:, 8:9], ALU.mult, ALU.max)
                yield
                kb.tt("vector", e_[:, 1:2], e_[:, 0:1], dcol[:, i, h:h + 1], ALU.max)
                yield
                kb.op("vector", lambda e, o=e_[:, 2:3], a=e_[:, 1:2]: e.reciprocal(out=o, in_=a), reads=[e_[:, 1:2]], writes=[e_[:, 2:3]])
                yield
                kb.tt("vector", e_[:, 4:5], e_[:, 2:3], e_[:, 2:3], ALU.mult)
                yield
                kb.stt("vector", e_[:, 5:6], e_[:, 3:4], 1.0 / 256.0, e_[:, 4:5], ALU.mult, ALU.mult)
                yield
                kb.act(e_[:, 6:7], e_[:, 5:6], AF.Ln, bias=1e-6)
                yield
                kb.act(e_[:, 6:7], e_[:, 6:7], AF.Exp, scale=-0.5)
                yield
                kb.tt("vector", e_[:, 7:8], e_[:, 6:7], e_[:, 2:3], ALU.mult)
                yield
                hm = hmix[il]
                kb.stt("vector", hm, num[:, 0:256], e_[:, 7:8], go[:, i, :], ALU.mult, ALU.mult)
                yield
                for cc in range(2):
                    sel_store(kb, P, cc, mixT, hm[:, cc * 128:(cc + 1) * 128], 2 * h + cc, i, identb, selb)

            gens = [epi(il) for il in range(ntl)]
            while gens:
                for g_ in list(gens):
                    try:
                        next(g_)
                    except StopIteration:
                        gens.remove(g_)
        for cc in range(2):
            for tl in range(16):
                kt = kwt[tl % 2]
                kb.tr(P[0].bitcast(BF16)[:, (tl % 2) * 128:(tl % 2) * 128 + 128], kT[:, cc, tl * 128:(tl + 1) * 128], identb)
                kb.ts("vector", kt[:, 0:128], P[0].bitcast(BF16)[:, (tl % 2) * 128:(tl % 2) * 128 + 128], dcol[:, tl, 4 + h:5 + h], None, ALU.mult)
                kb.mm(P[6][:, 0:257], kt[:, 0:128], vx[:, tl, 0:257], start=(tl == 0), stop=(tl == 15))
            sg = stage[cc]
            kb.cp("vector", sg, P[6][:, 0:257])
            kb.dma(L["pC_o"][h, cc * 128:(cc + 1) * 128, :], sg[:, 0:256])
            kb.dma(L["pn_o"][h, cc * 128:(cc + 1) * 128].rearrange("(p o) -> p o", o=1), sg[:, 256:257])
    cv.release()


def gdn_heads(nc, kb, cv, P, L, G):
    hT = L["hT"]; ident = L["ident"]; identb = L["identb"]; selb = L["selb"]; mixT = L["mixT"]
    capT_p = L["capT_p"]; capE_p = L["capE_p"]; capT_s = L["capT_s"]; capE_s = L["capE_s"]
    cols = G["cols"]; dcol = G["dcol"]; ebLB = G["ebLB"]; BR = G["BR"]; selh = G["selh"]
    maskP = G["maskP"]; loadw = G["loadw"]; pit = G["pit"]
    GW = 4096
    w_in_d = L["w_in_d"]
    wbuf = G["wbuf"]
    wsl = [wbuf[0][:, :, 0:128], wbuf[0][:, :, 128:256], wbuf[1][:, :, 0:128], wbuf[1][:, :, 128:256]]
    wq3 = wsl[0:3]
    NG = int(os.environ.get("KB_NG", "4"))
    cv.mark()
    qT = cv.f32(TT); kT = cv.f32(TT)
    vtok = cv.bf(NT * 128).rearrange("p (t v) -> p t v", t=NT)
    cw = cv.f32(4); gnb = cv.f32(128); junk = cv.f32(128); ec = cv.f32(8)
    Sst = cv.f32(128); U = cv.f32(128)
    hmx = [cv.bf(128) for _ in range(2)]
    base = cv.pos
    cinS = [cv.f32(3 + 512 + 128) for _ in range(2)]; caccS = [cv.f32(512 + 128) for _ in range(2)]
    vblkS = [cv.f32(512) for _ in range(2)]
    sqS = [cv.f32(512) for _ in range(2)]; tl_rows = cv.f32(512); c0T = cv.f32(48)
    cit = [0]
    cw3 = cv.f32(12); c0T3 = cv.f32(144)
    end1 = cv.pos
    cv.pos = base
    TA = [[cv.f32(128) for _ in range(7)] for _ in range(max(NG, 4))]
    TB0 = [[cv.f32(128) for _ in range(6)] for _ in range(max(NG, 4))]
    mr_off = G["MR"].offset % L["NA"]
    pool2 = [L["arena"][:, mr_off + i * 128:mr_off + (i + 1) * 128] for i in range(17)]
    TB = TB0
    S0b = [TA[1][i] for i in range(4)]
    Sout = [TA[2][i] for i in range(2)]
    cv.pos = max(cv.pos, end1)
    pool2 += [cv.f32(128) for _ in range(24 - 17)]
    TB1 = [pool2[i * 6:(i + 1) * 6] for i in range(4)]
    TBS = [TB0, TB1]
    for h in range(8):
        kb.phase("gdnconv")
        kb.dma(gnb, L["gnorm_d"][h * 128:(h + 1) * 128].partition_broadcast(128))
        def conv_block(qi, bi, t0, n, dstT0, cbase, wb, cw, c0T, slot, cprev):
            pp = P[slot % 2]
            cin = cinS[slot % 2]; cacc = caccS[slot % 2]; vblk = vblkS[slot % 2]; sq = sqS[slot % 2]
            psq = P[2 + slot % 2]
            for kc in range(16):
                kb.mm(pp[:, 0:n], wb[:, kc, 0:128], hT[:, kc, t0:t0 + n], start=(kc == 0), stop=(kc == 15))
            yield
            if bi == 0:
                kb.memset("vector", cin[:, 0:3], 0.0)
            elif bi < 4:
                kb.cp("vector", cin[:, 0:3], cprev[:, 512:515])
            if bi < 4:
                kb.cp("scalar", cin[:, 3:3 + n], pp[:, 0:n])
                yield
                kb.ts("vector", cacc[:, 0:n], cin[:, 0:n], cw[:, 0:1], None, ALU.mult)
                for j in range(1, 4):
                    kb.stt("vector", cacc[:, 0:n], cin[:, j:j + n], cw[:, j:j + 1], cacc[:, 0:n], ALU.mult, ALU.add)
                if bi == 3:
                    kb.tr(P[2][0:3, 128:256], cin[:, 512:515], ident)
                    kb.cp("vector", tl_rows[0:3, 128:256], P[2][0:3, 128:256])
                    kb.dma(L["pconv_o"][:, cbase - GW:cbase - GW + 128], tl_rows[0:3, 128:256])
            else:
                kb.cp("vector", cin[:, 0:48], c0T)
                kb.cp("scalar", cin[:, 48:48 + 128], pp[:, 0:128])
                yield
                kb.ts("vector", cacc[:, 0:128], cin[:, 0:128], cw[:, 0:1], None, ALU.mult)
                for j in range(1, 4):
                    kb.stt("vector", cacc[:, 0:128], cin[:, 16 * j:16 * j + 128], cw[:, j:j + 1], cacc[:, 0:128], ALU.mult, ALU.add)
                kb.tr(P[2][0:48, 256:384], cin[:, 64:112], ident)
                kb.cp("vector", tl_rows[0:48, 256:384], P[2][0:48, 256:384])
                kb.dma(L["oconv_o"][:, cbase - GW:cbase - GW + 128], tl_rows[0:48, 256:384])
            yield
            if qi == 2:
                kb.act(vblk[:, 0:n], cacc[:, 0:n], AF.Silu)
                yield
                for ti in range(n // 128):
                    tl = t0 // 128 + ti
                    kb.tr(psq[:, (ti % 4) * 128:(ti % 4) * 128 + 128], vblk[:, ti * 128:(ti + 1) * 128], ident)
                yield
                for ti in range(n // 128):
                    tl = t0 // 128 + ti
                    kb.cp("scalar" if ti % 2 else "vector", vtok[:, tl, :], psq[:, (ti % 4) * 128:(ti % 4) * 128 + 128])
                return
            dstT = dstT0
            kb.act(dstT[:, t0:t0 + n], cacc[:, 0:n], AF.Silu)
            kb.act(sq[:, 0:n], dstT[:, t0:t0 + n], AF.Square)
            yield
            kb.mm(psq[:, 0:n], L["ones_f"], sq[:, 0:n], start=True, stop=True)
            yield
            kb.rsqrt(sq[:, 0:n], psq[:, 0:n], 1e-6)
            yield
            if qi == 0:
                kb.stt("vector", dstT[:, t0:t0 + n], dstT[:, t0:t0 + n], 128.0 ** -0.5, sq[:, 0:n], ALU.mult, ALU.mult)
            else:
                kb.tt("vector", dstT[:, t0:t0 + n], dstT[:, t0:t0 + n], sq[:, 0:n], ALU.mult)

        blocks = []
        for qi, (dstT0, cbase) in enumerate(((qT, GW + h * 128), (kT, GW + 1024 + h * 128), (None, GW + 2048 + h * 128))):
            wb = wq3[qi]
            kb.dma(wb, w_in_d[:, cbase:cbase + 128].rearrange("(k p) n -> p k n", p=128), q="gpsimd")
            cw = cw3[:, qi * 4:(qi + 1) * 4]
            kb.dma(cw, L["convw_d"][cbase - GW:cbase - GW + 128, :])
            kb.dma(tl_rows[0:48, 0:128], L["sconv_d"][:, cbase - GW:cbase - GW + 128])
            kb.tr(P[2][:, 384:432], tl_rows[0:48, 0:128], ident[0:48, 0:48])
            c0T = c0T3[:, qi * 48:(qi + 1) * 48]
            kb.cp("vector", c0T, P[2][:, 384:432])
            for bi, (t0, n) in enumerate([(0, 512), (512, 512), (1024, 512), (1536, 512), (2048, 128)]):
                blocks.append((qi, bi, t0, n, dstT0, cbase, wb, cw, c0T))
        active = []
        nxt = 0
        while nxt < len(blocks) or active:
            if nxt < len(blocks) and len(active) < 2:
                slot = cit[0]; cit[0] += 1
                active.append(conv_block(*blocks[nxt], slot, cinS[(slot + 1) % 2]))
                nxt += 1
            for g_ in list(active):
                try:
                    next(g_)
                except StopIteration:
                    active.remove(g_)
        wz = wsl[3]
        kb.dma(wz, w_in_d[:, GW + 3072 + h * 128:GW + 3072 + (h + 1) * 128].rearrange("(k p) n -> p k n", p=128), q="gpsimd")
        kb.phase("gdnchunk")
        kb.memset("vector", Sst, 0.0)

        def stageA(c, u, TB):
            sl = slice(c * 128, (c + 1) * 128)
            smp = (c == 16)
            capT = capT_s if smp else capT_p
            capE = capE_s if smp else capE_p
            bcol = cols[:, c, 16 + h:17 + h]; negb = dcol[:, c, 40 + h:41 + h]
            bet = dcol[:, c, 8 + h:9 + h]; bete = dcol[:, c, 16 + h:17 + h]; nbet = dcol[:, c, 24 + h:25 + h]
            edl = dcol[:, c, 32 + h:33 + h]
            ET, E, Pm, Q, R, kbe, vb = TA[u]
            WTn, U0, qe, PT, wk, szt = TB[u]
            bk = P[4 + u]
            r0, r1, r2, r3 = bk[:, 0:128], bk[:, 128:256], bk[:, 256:384], bk[:, 384:512]
            kb.mm(r0, selh[0:8, h * 128:(h + 1) * 128], BR[0:8, sl], start=True, stop=True)
            kb.mm(r1, kT[:, sl], kT[:, sl], start=True, stop=True)
            kb.mm(r2, kT[:, sl], qT[:, sl], start=True, stop=True)
            kb.tr(r3, kT[:, sl], ident)
            yield
            kb.stt("vector", ET, r0, negb, capT, ALU.add, ALU.min)
            kb.stt("vector", E, r0, bcol, capE, ALU.subtract, ALU.max)
            kb.act(WTn, r0, AF.Exp)
            kb.ts("vector", vb, vtok[:, c, :], bet, None, ALU.mult)
            yield
            kb.act(ET, ET, AF.Exp)
            kb.act(E, E, AF.Exp, scale=-1.0)
            kb.ts("vector", kbe, r3, bete, None, ALU.mult)
            kb.ts("vector", wk, r3, edl, None, ALU.mult)
            kb.tt("vector", qe, qT[:, sl], WTn, ALU.mult)
            yield
            kb.stt("vector", Pm, r1, nbet, E, ALU.mult, ALU.mult)
            kb.tt("vector", PT, r2, ET, ALU.mult)
            yield
            kb.tr(r0, Pm, ident)
            pz = P[pit[0] % 2]; pit[0] += 1
            for kc in range(16):
                kb.mm(pz[:, 0:128], hT[:, kc, sl], wz[:, kc, 0:128], start=(kc == 0), stop=(kc == 15))
            kb.act(szt, pz[:, 0:128], AF.Silu)
            yield
            kb.cp("scalar", Q, r0)
            kb.tt("vector", R, r0, ident, ALU.add)
            kb.tt("vector", szt, szt, gnb, ALU.mult)
            yield
            Pc, Qc, Pn, Qn = Pm, Q, E, ET
            nst = 2 if smp else 6
            for k in range(nst):
                kb.mm(r1, Qc, Pc, start=True, stop=True)
                yield
                kb.cp("scalar", Pn, r1)
                yield
                kb.mm(r3, Pn, R, start=True, stop=True)
                if k < nst - 1:
                    kb.tr(r2, Pn, ident)
                yield
                kb.tt("vector", R, R, r3, ALU.add)
                if k < nst - 1:
                    kb.cp("vector", Qn, r2)
                Pc, Qc, Pn, Qn = Pn, Qn, Pc, Qc
            yield
            kb.mm(r0, kbe, R, start=True, stop=True)
            kb.mm(r1, R, vb, start=True, stop=True)
            yield
            kb.ts("vector", WTn, r0, -1.0, None, ALU.mult)
            kb.cp("scalar", U0, r1)

        def out_epi(c, onum, szt):
            kb.act(junk, onum, AF.Square, accum_out=ec[:, 0:1])
            kb.ts("vector", ec[:, 1:2], ec[:, 0:1], 1.0 / 128.0, 1e-6, ALU.mult, ALU.add)
            kb.rsqrt(ec[:, 2:3], ec[:, 1:2], 0.0)
            hm = hmx[c % 2]
            kb.stt("vector", hm, onum, ec[:, 2:3], szt, ALU.mult, ALU.mult)
            sel_store(kb, P, c % 2, mixT, hm, 8 + h, c, identb, selb)

        def stageB(grp, TB):
            for u, c in enumerate(grp):
                WTn, U0, qe, PT, wk, szt = TB[u]
                if c < 16:
                    kb.mm(P[2][:, 0:128], WTn, Sst, start=True, stop=True)
                    kb.mm(P[3][:, 0:128], qe, Sst, start=True, stop=False)
                    yield
                    kb.tt("vector", U, U0, P[2][:, 0:128], ALU.add)
                    yield
                    kb.mm(P[3][:, 0:128], PT, U, start=False, stop=True)
                    kb.mm(P[2][:, 128:256], wk, U, start=True, stop=True)
                    yield
                    kb.stt("vector", Sst, Sst, ebLB[:, h * 32 + c:h * 32 + c + 1], P[2][:, 128:256], ALU.mult, ALU.add)
                    if c == 15:
                        kb.dma(L["pS_o"][h], Sst)
                    out_epi(c, P[3][:, 0:128], szt)
                    yield
                else:
                    kbe, vb = TA[u][5], TA[u][6]
                    kb.cp("vector", U, U0)
                    oacc = kbe
                    kb.memset("vector", oacc, 0.0)
                    for b_ in range(3):
                        kb.dma(S0b[b_ % 4], L["sS_d"][b_, h])
                    for b in range(NSEQ):
                        sb = S0b[b % 4]
                        if b + 3 < NSEQ:
                            kb.dma(S0b[(b + 3) % 4], L["sS_d"][b + 3, h])
                        pr = P[2] if b % 2 == 0 else P[3]
                        kb.mm(pr[:, 0:128], WTn, sb, start=True, stop=True)
                        kb.mm(pr[:, 128:256], qe, sb, start=True, stop=True)
                        kb.stt("vector", U, pr[:, 0:128], maskP[:, b:b + 1], U, ALU.mult, ALU.add)
                        kb.stt("vector", oacc, pr[:, 128:256], maskP[:, b:b + 1], oacc, ALU.mult, ALU.add)
                    kb.mm(P[2][:, 256:384], PT, U, start=True, stop=True)
                    kb.tt("vector", oacc, oacc, P[2][:, 256:384], ALU.add)
                    for b_ in range(3):
                        kb.dma(S0b[b_ % 4], L["sS_d"][b_, h])
                    for b in range(NSEQ):
                        sb = S0b[b % 4]
                        if b + 3 < NSEQ:
                            kb.dma(S0b[(b + 3) % 4], L["sS_d"][b + 3, h])
                        vm = TA[3][b % 4]
                        kb.ts("vector", vm, U, maskP[:, b:b + 1], None, ALU.mult)
                        pr = P[2] if b % 2 == 0 else P[3]
                        kb.mm(pr[:, 384:512], wk, vm, start=True, stop=True)
                        so = Sout[b % 2]
                        kb.stt("vector", so, sb, ebLB[:, h * 32 + 16 + b:h * 32 + 17 + b], pr[:, 384:512], ALU.mult, ALU.add)
                        kb.dma(L["oS_o"][b, h], so)
                    out_epi(c, oacc, szt)

        groups = [list(range(g, g + NG)) for g in range(0, 16, NG)] + [[16]]
        pend = None
        for gi, grp in enumerate(groups):
            TBc = TBS[gi % 2]
            gens = [stageA(c, u, TBc) for u, c in enumerate(grp)]
            if pend is not None:
                gens.append(pend)
            while gens:
                for g_ in list(gens):
                    try:
                        next(g_)
                    except StopIteration:
                        gens.remove(g_)
            pend = stageB(grp, TBc)
        for _ in pend:
            pass
    cv.release()


_NC_CACHE = {}


def kernel(**inputs):
    if "nc" not in _NC_CACHE:
        _NC_CACHE["nc"] = build()
    nc = _NC_CACHE["nc"]
    maps = make_in_maps(inputs)
    res = run_bass_kernel_spmd(nc, maps, core_ids=list(range(8)))
    R = res.results
    f32 = np.float32
    y_p = np.zeros((4, 2048, D), f32); y_s = np.zeros((128, 4, D), f32)
    p_C = np.zeros((4, 4, 256, 256), f32); p_n = np.zeros((4, 4, 256), f32); p_m = np.zeros((4, 4), f32)
    p_S = np.zeros((4, 8, 128, 128), f32); p_conv = np.zeros((4, 3, 3072), f32)
    s_C = np.zeros((128, 4, 256, 256), f32); s_n = np.zeros((128, 4, 256), f32); s_m = np.zeros((128, 4), f32)
    s_S = np.zeros((128, 8, 128, 128), f32); s_conv = np.zeros((128, 3, 3072), f32)
    for c in range(8):
        p, j = c // 2, c % 2
        sq = slice(c * NSEQ, (c + 1) * NSEQ)
        r = R[c]
        ym = np.asarray(r["ym"], f32)
        y_p[p, j * 1024:(j + 1) * 1024] = ym[0:1024]
        y_s[sq] = ym[1024:1088].reshape(4, NSEQ, D).transpose(1, 0, 2)
        if j == 0:
            p_C[p] = r["pC"]; p_n[p] = r["pn"]; p_m[p] = np.asarray(r["pm"])[:, 0]
            p_S[p] = r["pS"]; p_conv[p] = r["pconv"]
        s_C[sq] = r["oC"]; s_n[sq] = r["on"]; s_m[sq] = np.asarray(r["om"]).T
        s_S[sq] = r["oS"]
        s_conv[sq] = np.asarray(r["oconv"]).reshape(3, NSEQ, 3072).transpose(1, 0, 2)
    return (y_p, y_s, p_C, p_n, p_m, p_S, p_conv, s_C, s_n, s_m, s_S, s_conv)
```

```python
import os
import numpy as np
import concourse.bass as bass
import concourse.mybir as mybir
from concourse.bass_utils import run_bass_kernel_spmd

F32 = mybir.dt.float32
BF16 = mybir.dt.bfloat16
AF = mybir.ActivationFunctionType
ALU = mybir.AluOpType
AX = mybir.AxisListType

D = 2048
NT = 17
TT = NT * 128
MYT = 1088
DFF = 5632
INC = 8216
NSEQ = 16
ALPHA = 2.0 ** 0.25
NEG = -1.0e30


def _dsize(dt):
    return 2 if dt == BF16 else 4


class KB:
    ENG = ("tensor", "vector", "scalar", "gpsimd", "sync")

    def __init__(self, nc, esems, dsems):
        self.nc = nc
        self.ops = {e: [] for e in self.ENG}
        self.cnt = {e: 0 for e in self.ENG}
        self.esems = esems
        self.dsems = dsems
        self.dcnt = [0] * len(dsems)
        self.dnext = 0
        self.dnext_sw = 0
        self.waited = {e: {} for e in self.ENG}
        self.W = {}
        self.R = {}
        self.needed = {}

    def _sem(self, key):
        return self.esems[key] if isinstance(key, str) else self.dsems[key]

    coarse = False
    coarse_pe = False
    psum_excl = os.environ.get("KB_PSX", "1") == "1"
    ce = tuple(os.environ.get("KB_CE", "tensor").split(","))

    def region(self, ap):
        name = ap.tensor.name
        if self.coarse and str(ap.space) in ("SB", "SBUF"):
            return name, 0, 1 << 40
        if str(ap.space) not in ("SB", "SBUF", "PSUM"):
            ext = 1
            for st, n in ap.ap:
                ext += abs(st) * (n - 1)
            return name, ap.offset * 4, (ap.offset + ext) * 4
        ds = _dsize(ap.dtype)
        dims = list(ap.ap)
        pstride = dims[0][0]
        off = ap.offset % pstride if pstride > 0 else ap.offset
        ext = 1
        for st, n in dims[1:]:
            ext += abs(st) * (n - 1)
        return name, off * ds, (off + ext) * ds

    def _deps(self, eng, reads, writes):
        need = {}

        def add(sig):
            k, v = sig
            if eng == "tensor" and k == "tensor":
                return
            if need.get(k, 0) < v:
                need[k] = v

        rr = [self.region(a) for a in reads]
        ww = [self.region(a) for a in writes]
        if self.psum_excl and eng != "tensor":
            extra = [(n_, 0, 4096) for (n_, l_, h_) in rr if n_.startswith("ps")]
            rr = rr + extra
            if os.environ.get("KB_PSRW", "1") == "1":
                ww = ww + extra
        if self.coarse_pe and eng in self.ce:
            rr = [(n_, 0, 1 << 40) if n_ == "arena" else (n_, l_, h_) for (n_, l_, h_) in rr]
            ww = [(n_, 0, 1 << 40) if n_ == "arena" else (n_, l_, h_) for (n_, l_, h_) in ww]
        for name, lo, hi in rr:
            for (l, h, sig) in self.W.get(name, ()):
                if l < hi and lo < h:
                    add(sig)
        for name, lo, hi in ww:
            for (l, h, sig) in self.W.get(name, ()):
                if l < hi and lo < h:
                    add(sig)
            for (l, h, k), v in self.R.get(name, {}).items():
                if l < hi and lo < h:
                    add((k, v))
        return need, rr, ww

    def _commit(self, rr, ww, sig):
        k, v = sig
        for name, lo, hi in rr:
            self.R.setdefault(name, {})[(lo, hi, k)] = v
        for name, lo, hi in ww:
            wl = [(l, h, s) for (l, h, s) in self.W.get(name, []) if not (lo <= l and h <= hi)]
            wl.append((lo, hi, sig))
            self.W[name] = wl
            rd = self.R.get(name)
            if rd:
                for key in [key for key in rd if lo <= key[0] and key[1] <= hi]:
                    del rd[key]

    def _waits(self, eng, need):
        out = []
        wd = self.waited[eng]
        for k, v in need.items():
            if wd.get(k, 0) < v:
                wd[k] = v
                out.append((k, v))
                if isinstance(k, str):
                    self.needed.setdefault(k, set()).add(v)
        return out

    def op(self, eng, fn, reads=(), writes=()):
        need, rr, ww = self._deps(eng, reads, writes)
        waits = self._waits(eng, need)
        self.cnt[eng] += 1
        sig = (eng, self.cnt[eng])
        self.ops[eng].append((waits, fn, ("E", self.cnt[eng])))
        self._commit(rr, ww, sig)

    def dma(self, out, in_, q="sync", rd=True, wr=True, after=()):
        reads = [in_] if rd else []
        writes = [out] if wr else []
        need, rr, ww = self._deps(q, reads, writes)
        for (ka, va) in after:
            if need.get(ka, 0) < va:
                need[ka] = va
        nd = len(self.dsems)
        nsw = nd // 3
        if q == "gpsimd":
            k = nd - nsw + self.dnext_sw
            self.dnext_sw = (self.dnext_sw + 1) % nsw
        else:
            k = self.dnext
            self.dnext = (self.dnext + 1) % (nd - nsw)
        if self.dcnt[k] > 0:
            if need.get(k, 0) < self.dcnt[k]:
                need[k] = self.dcnt[k]
        waits = self._waits(q, need)
        self.dcnt[k] += 16
        sig = (k, self.dcnt[k])
        self.ops[q].append((waits, lambda e, o=out, i=in_: e.dma_start(out=o, in_=i, allow_slow_non_contiguous=True), ("D", k)))
        self._commit(rr, ww, sig)
        return sig

    def mm(self, out, lhsT, rhs, start=True, stop=True):
        self.op("tensor", lambda e: e.matmul(out, lhsT=lhsT, rhs=rhs, start=start, stop=stop),
                reads=[lhsT, rhs], writes=[out])

    def tr(self, out, in_, ident):
        self.op("tensor", lambda e: e.transpose(out, in_, ident), reads=[in_, ident], writes=[out])

    def act(self, out, in_, func, bias=None, scale=1.0, accum_out=None, eng="scalar"):
        rd = [in_]
        kw = {}
        if bias is not None:
            kw["bias"] = bias
            if not isinstance(bias, (int, float)):
                rd.append(bias)
        if not isinstance(scale, (int, float)):
            rd.append(scale)
        wr = [out]
        if accum_out is not None:
            kw["accum_out"] = accum_out
            wr.append(accum_out)
        self.op("scalar", lambda e: e.activation(out=out, in_=in_, func=func, scale=scale, **kw),
                reads=rd, writes=wr)

    def ascale(self, out, in_, scale_ap):
        self.op("scalar", lambda e: e.activation(out=out, in_=in_, func=AF.Copy, scale=scale_ap),
                reads=[in_, scale_ap], writes=[out])

    def tt(self, eng, out, in0, in1, op):
        self.op(eng, lambda e: e.tensor_tensor(out=out, in0=in0, in1=in1, op=op), reads=[in0, in1], writes=[out])

    def ts(self, eng, out, in0, s1, s2, op0, op1=None):
        rd = [in0] + [s for s in (s1, s2) if s is not None and not isinstance(s, (int, float))]
        if op1 is None:
            self.op(eng, lambda e: e.tensor_scalar(out=out, in0=in0, scalar1=s1, scalar2=None, op0=op0),
                    reads=rd, writes=[out])
        else:
            self.op(eng, lambda e: e.tensor_scalar(out=out, in0=in0, scalar1=s1, scalar2=s2, op0=op0, op1=op1),
                    reads=rd, writes=[out])

    def stt(self, eng, out, in0, scalar, in1, op0, op1):
        rd = [in0, in1] + ([] if isinstance(scalar, (int, float)) else [scalar])
        self.op(eng, lambda e: e.scalar_tensor_tensor(out=out, in0=in0, scalar=scalar, in1=in1, op0=op0, op1=op1),
                reads=rd, writes=[out])

    def cp(self, eng, out, in_):
        if eng == "scalar":
            self.op(eng, lambda e: e.copy(out=out, in_=in_), reads=[in_], writes=[out])
        else:
            self.op(eng, lambda e: e.tensor_copy(out=out, in_=in_), reads=[in_], writes=[out])

    def rsqrt(self, out, in_, eps):
        self.act(out, in_, AF.Sqrt, bias=eps)
        self.op("vector", lambda e: e.reciprocal(out=out, in_=out), reads=[out], writes=[out])

    def memset(self, eng, out, val):
        self.op(eng, lambda e: e.memset(out, val), reads=[], writes=[out])

    def scan(self, out, d0, d1, init, op0, op1):
        rd = [d0, d1] + ([] if isinstance(init, (int, float)) else [init])
        self.op("vector", lambda e: e.tensor_tensor_scan(out=out, data0=d0, data1=d1, initial=init, op0=op0, op1=op1),
                reads=rd, writes=[out])

    def emit(self, block):
        kb = self
        import bisect
        for e2 in ("tensor", "vector", "scalar", "gpsimd"):
            if kb.cnt[e2] > 0:
                kb.needed.setdefault(e2, set()).add(kb.cnt[e2])
        order = {k: sorted(v) for k, v in kb.needed.items()}

        def semval(k, v):
            if isinstance(k, str):
                return bisect.bisect_right(order[k], v)
            return v

        def run(eng, name):
            for waits, fn, inc in kb.ops[name]:
                for k, v in waits:
                    eng.wait_ge(kb._sem(k), semval(k, v))
                ins = fn(eng)
                if inc[0] == "D":
                    ins.then_inc(kb.dsems[inc[1]], 16)
                elif inc[1] in kb.needed.get(name, ()):
                    ins.then_inc(kb.esems[name], 1)
            if name == "sync":
                for k, c in enumerate(kb.dcnt):
                    if c > 0:
                        eng.wait_ge(kb.dsems[k], c)
                for e2 in ("tensor", "vector", "scalar", "gpsimd"):
                    if kb.cnt[e2] > 0:
                        eng.wait_ge(kb.esems[e2], semval(e2, kb.cnt[e2]))

        @block.tensor
        def _(e):
            run(e, "tensor")

        @block.vector
        def _(e):
            run(e, "vector")

        @block.scalar
        def _(e):
            run(e, "scalar")

        @block.gpsimd
        def _(e):
            run(e, "gpsimd")

        @block.sync
        def _(e):
            run(e, "sync")


class Carver:
    def __init__(self, arena, n):
        self.a = arena
        self.n = n
        self.pos = 0
        self.marks = []

    def f32(self, nelem, parts=128):
        off = self.pos
        self.pos += nelem
        assert self.pos <= self.n, ("sbuf arena overflow", self.pos, self.n)
        return self.a[0:parts, off:off + nelem]

    def bf(self, nelem, parts=128):
        n32 = (nelem + 1) // 2
        v = self.f32(n32, parts)
        return v.bitcast(BF16)[:, 0:nelem]

    def mark(self):
        self.marks.append(self.pos)

    def release(self):
        self.pos = self.marks.pop()


def build(stop_after=None):
    nc = bass.Bass("TRN2", target_bir_lowering=False)
    di = lambda n, s, dt=F32: nc.dram_tensor(n, list(s), dt, kind="ExternalInput").ap()
    do = lambda n, s, dt=F32: nc.dram_tensor(n, list(s), dt, kind="ExternalOutput").ap()
    dsx = lambda n, s, dt=F32: nc.dram_tensor(n, list(s), dt, kind="Internal").ap()
    xa_d = di("xa", [TT, D]); xm_d = di("xm", [MYT, D]); crow_d = di("crow", [17, D])
    w_ada_d = di("w_ada", [D, 6 * D]); b_ada_d = di("b_ada", [6 * D])
    w_in_d = di("w_in", [D, INC]); w_out_d = di("w_out", [D, D])
    w_gu_d = di("w_gu", [D, 2 * DFF]); w_down_d = di("w_down", [DFF, D])
    mbias_d = di("mbias", [4, 2]); mnorm_d = di("m_norm_g", [1024]); gnorm_d = di("g_norm_g", [1024])
    convw_d = di("convwT", [3072, 4]); gpar_d = di("gpar", [8, 2])
    ln_d = di("lnp", [4, D])
    sC_d = di("sC", [NSEQ, 4, 256, 256]); sn_d = di("sn", [NSEQ, 4, 256]); sm_d = di("smT", [4, NSEQ])
    sS_d = di("sS", [NSEQ, 8, 128, 128]); sconv_d = di("sconv", [48, 3072])
    cst_d = di("cst", [128, 1024]); sel_d = di("sel", [128, 256])
    ym_o = do("ym", [MYT, D])
    pC_o = do("pC", [4, 256, 256]); pn_o = do("pn", [4, 256]); pm_o = do("pm", [4, 1])
    pS_o = do("pS", [8, 128, 128]); pconv_o = do("pconv", [3, 3072])
    oC_o = do("oC", [NSEQ, 4, 256, 256]); on_o = do("on", [NSEQ, 4, 256]); om_o = do("om", [4, NSEQ])
    oS_o = do("oS", [NSEQ, 8, 128, 128]); oconv_o = do("oconv", [48, 3072])
    ada_d = dsx("ada_s", [17, 6 * D]); x1_d = dsx("x1_s", [MYT, D]); y_d = dsx("y_s", [MYT, D])

    NA = 53200
    from contextlib import ExitStack
    with ExitStack() as es:
        arena = es.enter_context(nc.sbuf_tensor("arena", [128, NA], F32))
        ps = [es.enter_context(nc.psum_tensor(f"ps{i}", [128, 512], F32)) for i in range(8)]
        esems = {e: es.enter_context(nc.semaphore(f"s_{e}")) for e in KB.ENG}
        dsems = [es.enter_context(nc.semaphore(f"d_{i}")) for i in range(int(os.environ.get("KB_NDMA", "24")))]
        block = es.enter_context(nc.Block())
        kb = KB(nc, esems, dsems)
        coarse_set = set(os.environ.get("KB_COARSE", "").split(","))
        pe_set = set(os.environ.get("KB_COARSE_PE", "").split(","))

        def _phase(nm):
            kb.coarse = nm in coarse_set
            kb.coarse_pe = nm in pe_set
        kb.phase = _phase
        kb.phase("p0")
        cv = Carver(arena, NA)
        P = [p[:, :] for p in ps]

        cst = cv.f32(1024)
        kb.dma(cst, cst_d[:, :])
        ident = cst[:, 0:128]
        capT_p = cst[:, 128:256]; capE_p = cst[:, 256:384]; capT_s = cst[:, 384:512]; capE_s = cst[:, 512:640]
        ones_f = cst[:, 640:768]
        selc = cv.f32(256)
        kb.dma(selc, sel_d[:, :])
        identb = cv.bf(128)
        kb.cp("vector", identb, ident)
        selb = cv.bf(256)
        kb.cp("vector", selb, selc)
        lnp = None

        mixT_off = cv.pos
        mixT = cv.bf(16 * MYT).rearrange("p (c t) -> p c t", c=16)

        xbuf = [arena[:, mixT_off + i_ * D:mixT_off + (i_ + 1) * D] for i_ in range(4)]
        for tl_ in range(4):
            kb.dma(xbuf[tl_], xa_d[tl_ * 128:(tl_ + 1) * 128, :])
        cv.mark()
        c_sb = cv.f32(D)
        kb.dma(c_sb[0:17, :], crow_d[:, :])
        kb.act(c_sb[0:17, :], c_sb[0:17, :], AF.Silu)
        for kc in range(16):
            kb.tr(P[0][:, kc * 17:(kc + 1) * 17], c_sb[0:17, kc * 128:(kc + 1) * 128], ident[0:17, 0:17])
        siluT = cv.bf(272)
        kb.cp("vector", siluT, P[0][:, 0:272])
        wab = [cv.bf(16 * 512).rearrange("p (k n) -> p k n", k=16) for _ in range(4)]
        bbb = [cv.f32(512) for _ in range(4)]
        aob = [cv.f32(512) for _ in range(4)]
        for cb in range(24):
            wb = wab[cb % 4]; bb = bbb[cb % 4]; ao = aob[cb % 4]
            kb.dma(wb, w_ada_d[:, cb * 512:(cb + 1) * 512].rearrange("(k p) n -> p k n", p=128), q="gpsimd")
            kb.dma(bb[0:17, :], b_ada_d[cb * 512:(cb + 1) * 512].partition_broadcast(17))
            pp = P[1 + cb % 2]
            for kc in range(16):
                kb.mm(pp[0:17, :], siluT[:, kc * 17:(kc + 1) * 17], wb[:, kc, :], start=(kc == 0), stop=(kc == 15))
            kb.tt("vector", ao[0:17, :], pp[0:17, :], bb[0:17, :], ALU.add)
            kb.dma(ada_d[:, cb * 512:(cb + 1) * 512], ao[0:17, :])
        cv.release()

        def load_mod(dst, col0, prompt, plus1):
            if prompt:
                kb.dma(dst, ada_d[0, col0:col0 + D].partition_broadcast(128))
            else:
                kb.memset("gpsimd", dst, 0.0)
                for t in range(4):
                    kb.dma(dst[t * 16:(t + 1) * 16, :], ada_d[1:17, col0:col0 + D])
            if plus1:
                kb.ts("gpsimd", dst, dst, 1.0, None, ALU.add)

        kb.phase("p1a")
        hT_off = cv.pos
        hT = cv.bf(16 * TT).rearrange("p (c t) -> p c t", c=16)
        cv.mark()
        modA = cv.f32(D); modB = cv.f32(D)
        load_mod(modA, 2048, True, True)
        load_mod(modB, 0, True, False)
        for tl in range(NT):
            if tl == 16:
                load_mod(modA, 2048, False, True)
                load_mod(modB, 0, False, False)
            xs = xbuf[tl % 4]
            kb.tt("vector", xs, xs, modA, ALU.mult)
            kb.tt("vector", xs[:, 0:1024], xs[:, 0:1024], modB[:, 0:1024], ALU.add)
            kb.tt("gpsimd", xs[:, 1024:2048], xs[:, 1024:2048], modB[:, 1024:2048], ALU.add)
            for g in range(4):
                pp = P[(tl * 4 + g) % 4]
                for c in range(4):
                    kb.tr(pp[:, c * 128:(c + 1) * 128], xs[:, (g * 4 + c) * 128:(g * 4 + c + 1) * 128], ident)
                kb.cp("scalar" if g % 2 else "vector", hT[:, g * 4:(g + 1) * 4, tl * 128:(tl + 1) * 128],
                      pp.rearrange("p (c t) -> p c t", c=4))
            if tl + 4 < NT:
                kb.dma(xbuf[tl % 4], xa_d[(tl + 4) * 128:(tl + 5) * 128, :])
        cv.release()
        if stop_after == "hT":
            dbg = do("dbg", [128, 16 * TT], BF16)
            kb.dma(dbg.rearrange("p (c t) -> p c t", c=16), hT)
            kb.emit(block)
            return nc

        kb.phase("gates")
        scan_phase(nc, kb, cv, P, locals())
        cv.pos = hT_off
        kb.phase("p2")

        phase2(nc, kb, cv, P, locals())
        kb.emit(block)
    return nc


def _consts():
    c = np.zeros((128, 1024), np.float32)
    c[:, 0:128] = np.eye(128, dtype=np.float32)
    idx = np.arange(128)
    s = idx[:, None]; l = idx[None, :]
    c[:, 128:256] = np.where(l >= s, 0.0, NEG)
    c[:, 256:384] = np.where(idx[:, None] > idx[None, :], 0.0, -NEG)
    tt_ = idx // 16; bb_ = idx % 16
    same = (bb_[:, None] == bb_[None, :]) & (idx[:, None] < 64) & (idx[None, :] < 64)
    selfp = (idx[:, None] == idx[None, :])
    okT = (same & (tt_[None, :] >= tt_[:, None])) | selfp
    c[:, 384:512] = np.where(okT, 0.0, NEG)
    okE = same & (tt_[:, None] > tt_[None, :])
    c[:, 512:640] = np.where(okE, 0.0, -NEG)
    c[:, 640:768] = 1.0
    return c


def make_in_maps(inp):
    f = lambda a: np.ascontiguousarray(np.asarray(a, dtype=np.float32))
    xp = f(inp["x_prompt"]); xs = f(inp["x_sample"])
    cst = _consts()
    selh = np.zeros((8, 1024 + 2048), np.float32)
    for h in range(8):
        selh[h, h * 128:(h + 1) * 128] = 1.0
    selh[:, 1024:] = 1.0
    selh[:, 1024::128] = 0.0
    maskP = np.zeros((128, 16), np.float32)
    for p_ in range(64):
        maskP[p_, p_ % 16] = 1.0
    maps = []
    for c in range(8):
        p, j = c // 2, c % 2
        sq = slice(c * NSEQ, (c + 1) * NSEQ)
        xs_tb = xs[sq].transpose(1, 0, 2).reshape(64, D)
        xa = np.zeros((TT, D), np.float32)
        xa[0:2048] = xp[p]; xa[2048:2112] = xs_tb
        xm = np.concatenate([xp[p, j * 1024:(j + 1) * 1024], xs_tb], 0)
        crow = np.concatenate([f(inp["c_prompt"])[p:p + 1], f(inp["c_sample"])[sq]], 0)
        sel = np.zeros((128, 256), np.float32)
        sel[:, 0:128] = np.eye(128) * (1.0 if j == 0 else 0.0)
        sel[:, 128:256] = np.eye(128) * (1.0 if j == 1 else 0.0)
        m = {
            "xa": xa, "xm": f(xm), "crow": f(crow),
            "w_ada": f(inp["w_ada"]), "b_ada": f(inp["b_ada"]), "w_in": f(inp["w_in"]),
            "w_out": f(inp["w_out"]), "w_gu": f(inp["w_gu"]), "w_down": f(inp["w_down"]),
            "mbias": f(np.stack([inp["m_i_bias"], inp["m_f_bias"]], 1)),
            "m_norm_g": f(inp["m_norm_g"]), "g_norm_g": f(inp["g_norm_g"]),
            "convwT": f(np.asarray(inp["conv_w"]).T),
            "gpar": f(np.stack([inp["g_dt_bias"], inp["g_A_log"]], 1)),
            "lnp": f(np.stack([inp["ln1_g"], inp["ln1_b"], inp["ln2_g"], inp["ln2_b"]], 0)),
            "sC": f(inp["state_mlstm_C"])[sq], "sn": f(inp["state_mlstm_n"])[sq],
            "smT": f(np.asarray(inp["state_mlstm_m"])[sq].T),
            "sS": f(inp["state_gdn_S"])[sq],
            "sconv": f(np.asarray(inp["state_gdn_conv"])[sq].transpose(1, 0, 2).reshape(48, 3072)),
            "cst": cst, "sel": sel, "selh": selh, "maskP": maskP,
        }
        maps.append(m)
    return maps


def layer_norm_rows(kb, z, junk, st, np_, gam, bet, out):
    inv = 1.0 / D
    zz = z[0:np_, :]
    kb.op("vector", lambda e: e.tensor_reduce(out=st[0:np_, 0:1], in_=zz, axis=AX.X, op=ALU.add),
          reads=[zz], writes=[st[0:np_, 0:1]])
    kb.act(junk[0:np_, :], zz, AF.Square, accum_out=st[0:np_, 1:2])
    kb.ts("vector", st[0:np_, 2:3], st[0:np_, 0:1], inv, None, ALU.mult)
    kb.stt("vector", st[0:np_, 3:4], st[0:np_, 2:3], -1.0, st[0:np_, 2:3], ALU.mult, ALU.mult)
    kb.stt("vector", st[0:np_, 4:5], st[0:np_, 1:2], inv, st[0:np_, 3:4], ALU.mult, ALU.add)
    kb.rsqrt(st[0:np_, 5:6], st[0:np_, 4:5], 1e-5)
    kb.ts("vector", zz, zz, st[0:np_, 2:3], st[0:np_, 5:6], ALU.subtract, ALU.mult)
    kb.tt("gpsimd", zz, zz, gam[0:np_, :], ALU.mult)
    kb.tt("vector", out[0:np_, :], zz, bet[0:np_, :], ALU.add)


def phase2(nc, kb, cv, P, L):
    mixT = L["mixT"]; ident = L["ident"]; load_mod = L["load_mod"]
    xm_d = L["xm_d"]; ln_d = L["ln_d"]; x1_d = L["x1_d"]; y_d = L["y_d"]; ym_o = L["ym_o"]
    w_out_d = L["w_out_d"]; w_gu_d = L["w_gu_d"]; w_down_d = L["w_down_d"]
    tiles = [(i * 128, 128) for i in range(8)] + [(1024, 64)]
    cv.mark()
    h2T = cv.bf(16 * MYT).rearrange("p (c t) -> p c t", c=16)
    cv.mark()
    wo = cv.bf(16 * D).rearrange("p (k n) -> p k n", k=16)
    for q4 in range(4):
        kb.dma(wo[:, :, q4 * 512:(q4 + 1) * 512],
               w_out_d[:, q4 * 512:(q4 + 1) * 512].rearrange("(k p) n -> p k n", p=128), q="gpsimd")
    G1 = cv.f32(D); LG = cv.f32(D); LB = cv.f32(D); SC2 = cv.f32(D); SH2 = cv.f32(D)
    xt = [cv.f32(D)] * 2; zt = [cv.f32(D) for _ in range(2)]
    st8 = cv.f32(16)
    load_mod(G1, 2 * D, True, True); load_mod(SC2, 4 * D, True, True); load_mod(SH2, 3 * D, True, False)
    kb.dma(LG, ln_d[0, :].partition_broadcast(128)); kb.dma(LB, ln_d[1, :].partition_broadcast(128))
    def ln_stats(z, junk, st_, np_):
        inv = 1.0 / D
        zz = z[0:np_, :]
        kb.op("vector", lambda e: e.tensor_reduce(out=st_[0:np_, 0:1], in_=zz, axis=AX.X, op=ALU.add),
              reads=[zz], writes=[st_[0:np_, 0:1]])
        kb.act(junk[0:np_, :], zz, AF.Square, accum_out=st_[0:np_, 1:2])
        yield
        kb.ts("vector", st_[0:np_, 2:3], st_[0:np_, 0:1], inv, None, ALU.mult)
        yield
        kb.stt("vector", st_[0:np_, 3:4], st_[0:np_, 2:3], -1.0, st_[0:np_, 2:3], ALU.mult, ALU.mult)
        yield
        kb.stt("vector", st_[0:np_, 4:5], st_[0:np_, 1:2], inv, st_[0:np_, 3:4], ALU.mult, ALU.add)
        yield
        kb.act(st_[0:np_, 5:6], st_[0:np_, 4:5], AF.Ln, bias=1e-5)
        yield
        kb.act(st_[0:np_, 5:6], st_[0:np_, 5:6], AF.Exp, scale=-0.5)
        yield
        kb.ts("vector", zz, zz, st_[0:np_, 2:3], st_[0:np_, 5:6], ALU.subtract, ALU.mult)
        yield

    def tile2a(i, t0, np_):
        x = xt[0]; z = zt[i % 2]; st_ = st8[:, (i % 2) * 8:(i % 2) * 8 + 8]
        kb.dma(x[0:np_, :], xm_d[t0:t0 + np_, :])
        for cb in range(4):
            for kc in range(16):
                kb.mm(P[cb][0:np_, :], mixT[:, kc, t0:t0 + np_], wo[:, kc, cb * 512:(cb + 1) * 512],
                      start=(kc == 0), stop=(kc == 15))
        yield
        for cb in range(4):
            kb.tt("vector", z[0:np_, cb * 512:(cb + 1) * 512], P[cb][0:np_, :], G1[0:np_, cb * 512:(cb + 1) * 512], ALU.mult)
        yield
        kb.stt("vector", z[0:np_, :], x[0:np_, :], ALPHA, z[0:np_, :], ALU.mult, ALU.add)
        yield
        for _ in ln_stats(z, x, st_, np_):
            yield
        kb.tt("vector", z[0:np_, 0:1024], z[0:np_, 0:1024], LG[0:np_, 0:1024], ALU.mult)
        kb.tt("gpsimd", z[0:np_, 1024:2048], z[0:np_, 1024:2048], LG[0:np_, 1024:2048], ALU.mult)
        yield
        kb.tt("vector", z[0:np_, :], z[0:np_, :], LB[0:np_, :], ALU.add)
        yield
        kb.dma(x1_d[t0:t0 + np_, :], z[0:np_, :])
        yield
        kb.tt("vector", z[0:np_, :], z[0:np_, :], SC2[0:np_, :], ALU.mult)
        yield
        kb.tt("vector", z[0:np_, 0:1024], z[0:np_, 0:1024], SH2[0:np_, 0:1024], ALU.add)
        kb.tt("gpsimd", z[0:np_, 1024:2048], z[0:np_, 1024:2048], SH2[0:np_, 1024:2048], ALU.add)
        yield
        for g in range(4):
            pp = P[4 + g]
            for c in range(4):
                kb.tr(pp[:, c * 128:c * 128 + np_], z[0:np_, (g * 4 + c) * 128:(g * 4 + c + 1) * 128], ident[0:np_, 0:np_])
        yield
        for g in range(4):
            pp = P[4 + g]
            kb.cp("scalar" if g % 2 else "vector", h2T[:, g * 4:(g + 1) * 4, t0:t0 + np_],
                  pp.rearrange("p (c t) -> p c t", c=4)[:, :, 0:np_])

    def run_window(genfns, W, S=1):
        active = []; nxt = 0; last_steps = [S]
        while nxt < len(genfns) or active:
            if nxt < len(genfns) and len(active) < W and last_steps[0] >= S:
                active.append(genfns[nxt]()); nxt += 1; last_steps[0] = 0
            last_steps[0] += 1
            for g_ in list(active):
                try:
                    next(g_)
                except StopIteration:
                    active.remove(g_)

    run_window([(lambda i=i, t0=t0, np_=np_: tile2a(i, t0, np_)) for i, (t0, np_) in enumerate(tiles[:8])], 2, 5)
    load_mod(G1, 2 * D, False, True); load_mod(SC2, 4 * D, False, True); load_mod(SH2, 3 * D, False, False)
    for _ in tile2a(8, 1024, 64):
        pass
    cv.release()
    actT = cv.bf(44 * MYT).rearrange("p (f t) -> p f t", f=44)
    cv.mark()
    wgb = [cv.bf(16 * 256).rearrange("p (k n) -> p k n", k=16) for _ in range(4)]
    tmp = [cv.f32(512) for _ in range(2)]
    tbs = [(0, 512), (512, 512), (1024, 64)]
    it = 0
    for f in range(44):
        wb = wgb[f % 4]
        kb.dma(wb[:, :, 0:128], w_gu_d[:, f * 128:(f + 1) * 128].rearrange("(k p) n -> p k n", p=128), q="gpsimd")
        kb.dma(wb[:, :, 128:256], w_gu_d[:, DFF + f * 128:DFF + (f + 1) * 128].rearrange("(k p) n -> p k n", p=128), q="gpsimd")
        for (t0, n) in tbs:
            pg = P[(2 * it) % 8]; pu = P[(2 * it + 1) % 8]; tm = tmp[it % 2]; it += 1
            for kc in range(16):
                kb.mm(pg[:, 0:n], wb[:, kc, 0:128], h2T[:, kc, t0:t0 + n], start=(kc == 0), stop=(kc == 15))
            for kc in range(16):
                kb.mm(pu[:, 0:n], wb[:, kc, 128:256], h2T[:, kc, t0:t0 + n], start=(kc == 0), stop=(kc == 15))
            kb.act(tm[:, 0:n], pg[:, 0:n], AF.Silu)
            kb.tt("vector", actT[:, f, t0:t0 + n], tm[:, 0:n], pu[:, 0:n], ALU.mult)
    cv.release()
    cv.mark()
    save = cv.pos
    cv.pos = L["mixT_off"]
    wd0 = cv.bf(44 * 256).rearrange("p (f n) -> p f n", f=44)
    yst = [cv.f32(256) for _ in range(4)]
    cv.pos = save
    wdb = [wd0, cv.bf(44 * 256).rearrange("p (f n) -> p f n", f=44)]
    ysigs = []
    it = 0
    for cb in range(8):
        wb = wdb[cb % 2]
        kb.dma(wb, w_down_d[:, cb * 256:(cb + 1) * 256].rearrange("(f p) n -> p f n", p=128), q="gpsimd")
        for (t0, np_) in tiles:
            pp = P[it % 8]; ys = yst[it % 4]; it += 1
            for f in range(44):
                kb.mm(pp[0:np_, 0:256], actT[:, f, t0:t0 + np_], wb[:, f, :], start=(f == 0), stop=(f == 43))
            kb.cp("scalar" if it % 2 else "vector", ys[0:np_, :], pp[0:np_, 0:256])
            ysigs.append(kb.dma(y_d[t0:t0 + np_, cb * 256:(cb + 1) * 256], ys[0:np_, :], wr=False))
    cv.release()
    cv.release()
    cv.mark()
    G2 = cv.f32(D); LG2 = cv.f32(D); LB2 = cv.f32(D)
    xt = [cv.f32(D) for _ in range(2)]; zt = [cv.f32(D) for _ in range(2)]
    st8 = cv.f32(16)
    load_mod(G2, 5 * D, True, True)
    kb.dma(LG2, ln_d[2, :].partition_broadcast(128)); kb.dma(LB2, ln_d[3, :].partition_broadcast(128))
    def tileln2(i, t0, np_):
        x = xt[i % 2]; z = zt[i % 2]; st_ = st8[:, (i % 2) * 8:(i % 2) * 8 + 8]
        kb.dma(z[0:np_, :], y_d[t0:t0 + np_, :], rd=False, after=ysigs)
        kb.dma(x[0:np_, :], x1_d[t0:t0 + np_, :])
        yield
        kb.tt("vector", z[0:np_, :], z[0:np_, :], G2[0:np_, :], ALU.mult)
        yield
        kb.stt("vector", z[0:np_, :], x[0:np_, :], ALPHA, z[0:np_, :], ALU.mult, ALU.add)
        yield
        for _ in ln_stats(z, x, st_, np_):
            yield
        kb.tt("vector", z[0:np_, 0:1024], z[0:np_, 0:1024], LG2[0:np_, 0:1024], ALU.mult)
        kb.tt("gpsimd", z[0:np_, 1024:2048], z[0:np_, 1024:2048], LG2[0:np_, 1024:2048], ALU.mult)
        yield
        kb.tt("vector", z[0:np_, :], z[0:np_, :], LB2[0:np_, :], ALU.add)
        yield
        kb.dma(ym_o[t0:t0 + np_, :], z[0:np_, :])

    run_window([(lambda i=i, t0=t0, np_=np_: tileln2(i, t0, np_)) for i, (t0, np_) in enumerate(tiles[:8])], 2)
    load_mod(G2, 5 * D, False, True)
    for _ in tileln2(8, 1024, 64):
        pass
    cv.release()


def bc3(ap, shape, axis):
    return ap.unsqueeze(axis).to_broadcast(list(shape))


def proj_fm(kb, P, pit, wb, hT, dst_fn, evac):
    for (t0, n) in [(0, 512), (512, 512), (1024, 512), (1536, 512), (2048, 128)]:
        pp = P[pit[0] % 2]; pit[0] += 1
        for kc in range(16):
            kb.mm(pp[:, 0:n], wb[:, kc, :], hT[:, kc, t0:t0 + n], start=(kc == 0), stop=(kc == 15))
        evac(pp[:, 0:n], t0, n)


def scan_phase(nc, kb, cv, P, L):
    hT = L["hT"]; ident = L["ident"]; identb = L["identb"]; selb = L["selb"]; mixT = L["mixT"]
    w_in_d = L["w_in_d"]; cst = L["cst"]
    capT_p = L["capT_p"]; capE_p = L["capE_p"]; capT_s = L["capT_s"]; capE_s = L["capE_s"]
    di = lambda n, s, dt=F32: nc.dram_tensor(n, list(s), dt, kind="ExternalInput").ap()
    selh_d = di("selh", [8, 1024 + 2048]); maskP_d = di("maskP", [128, 16])
    cv.mark()
    selh = cv.f32(1024, parts=8)
    kb.dma(selh, selh_d[:, 0:1024])
    maskB = None
    maskP = cv.f32(16); kb.dma(maskP, maskP_d[:, :])
    maskPb = cv.bf(16); kb.cp("vector", maskPb, maskP)
    MR = cv.f32(TT, parts=8); BR = cv.f32(TT, parts=8)
    NQ = 48
    cols = cv.f32(NT * NQ).rearrange("p (t q) -> p t q", t=NT)
    dcol = cv.f32(NT * 48).rearrange("p (t q) -> p t q", t=NT)
    decB = cv.f32(64); ebLB = cv.f32(128 + 128)
    mb = cv.f32(2, parts=8); gp = cv.f32(4, parts=8)
    kb.dma(mb[0:4, :], L["mbias_d"][:, :]); kb.dma(gp[:, 0:2], L["gpar_d"][:, :])
    wbuf = [cv.bf(16 * 256).rearrange("p (k n) -> p k n", k=16) for _ in range(2)]
    wit = [0]

    def loadw(c0, n):
        wb = wbuf[wit[0] % 2]; wit[0] += 1
        kb.dma(wb[:, :, 0:n], w_in_d[:, c0:c0 + n].rearrange("(k p) n -> p k n", p=128), q="gpsimd")
        return wb

    pit = [0]
    cv.mark()
    Ri = cv.f32(TT, parts=8); Rf = cv.f32(TT, parts=8); Rb = cv.f32(TT, parts=8); Ra = cv.f32(TT, parts=8)
    Rt = cv.f32(TT, parts=8); m0r = cv.f32(128, parts=8)
    flag = cv.f32(2048, parts=8); kb.dma(flag, selh_d[:, 1024:3072])
    wg = loadw(8192, 24)
    for (dst, c0, n) in ((Ri, 0, 4), (Rf, 4, 4), (Rb, 8, 8), (Ra, 16, 8)):
        def ev(pp, t0, nn, dst=dst, n=n):
            kb.cp("vector", dst[0:n, t0:t0 + nn], pp[0:n, :])
        for (t0, nn) in [(0, 512), (512, 512), (1024, 512), (1536, 512), (2048, 128)]:
            pp = P[pit[0] % 2]; pit[0] += 1
            for kc in range(16):
                kb.mm(pp[0:n, 0:nn], wg[:, kc, c0:c0 + n], hT[:, kc, t0:t0 + nn], start=(kc == 0), stop=(kc == 15))
            ev(pp[:, 0:nn], t0, nn)
    S0 = 2048
    kb.ts("vector", Ri[0:4, :], Ri[0:4, :], mb[0:4, 0:1], None, ALU.add)
    kb.ts("vector", Rf[0:4, :], Rf[0:4, :], mb[0:4, 1:2], None, ALU.add)
    kb.act(Rf[0:4, :], Rf[0:4, :], AF.Exp, scale=-1.0)
    kb.act(Rf[0:4, :], Rf[0:4, :], AF.Ln, bias=1.0)
    kb.ts("vector", Rf[0:4, :], Rf[0:4, :], -1.0, None, ALU.mult)
    Bm = Rt
    ones_bc = cst[0:4, 640:641].to_broadcast([4, 2048])
    kb.scan(Bm[0:4, 0:2048], ones_bc, Rf[0:4, 0:2048], 0.0, ALU.mult, ALU.add)
    kb.cp("vector", Bm[0:4, S0:S0 + 16], Rf[0:4, S0:S0 + 16])
    for t in range(1, 4):
        kb.tt("vector", Bm[0:4, S0 + 16 * t:S0 + 16 * t + 16], Bm[0:4, S0 + 16 * t - 16:S0 + 16 * t], Rf[0:4, S0 + 16 * t:S0 + 16 * t + 16], ALU.add)
    kb.cp("vector", Bm[0:4, S0 + 64:TT], Rf[0:4, S0 + 64:TT])
    Am = Ri
    kb.tt("vector", Am[0:4, :], Ri[0:4, :], Bm[0:4, :], ALU.subtract)
    kb.scan(MR[0:4, 0:2048], Am[0:4, 0:2048], Am[0:4, 0:2048], 0.0, ALU.max, ALU.max)
    kb.dma(m0r[0:4, 0:16], L["sm_d"][:, :])
    kb.memset("vector", m0r[0:4, 16:128], 0.0)
    for t in range(1, 4):
        kb.cp("vector", m0r[0:4, 16 * t:16 * t + 16], m0r[0:4, 0:16])
    kb.tt("vector", MR[0:4, S0:S0 + 16], Am[0:4, S0:S0 + 16], m0r[0:4, 0:16], ALU.max)
    for t in range(1, 4):
        kb.tt("vector", MR[0:4, S0 + 16 * t:S0 + 16 * t + 16], MR[0:4, S0 + 16 * t - 16:S0 + 16 * t], Am[0:4, S0 + 16 * t:S0 + 16 * t + 16], ALU.max)
    kb.cp("vector", MR[0:4, S0 + 64:TT], Am[0:4, S0 + 64:TT])
    Wf = Rf
    kb.ts("vector", Wf[0:4, 0:2048], Am[0:4, 0:2048], MR[0:4, 2047:2048], None, ALU.subtract)
    for t in range(4):
        kb.tt("vector", Wf[0:4, S0 + 16 * t:S0 + 16 * t + 16], Am[0:4, S0 + 16 * t:S0 + 16 * t + 16], MR[0:4, S0 + 48:S0 + 64], ALU.subtract)
    kb.memset("vector", Wf[0:4, S0 + 64:TT], 0.0)
    mo = cv.f32(32, parts=8)
    kb.tt("vector", mo[0:4, 0:1], Bm[0:4, 2047:2048], MR[0:4, 2047:2048], ALU.add)
    kb.tt("vector", mo[0:4, 16:32], Bm[0:4, S0 + 48:S0 + 64], MR[0:4, S0 + 48:S0 + 64], ALU.add)
    kb.dma(L["pm_o"][:, :], mo[0:4, 0:1]); kb.dma(L["om_o"][:, :], mo[0:4, 16:32])
    dec = cv.f32(16, parts=8)
    kb.tt("vector", dec[0:4, :], m0r[0:4, 0:16], MR[0:4, S0 + 48:S0 + 64], ALU.subtract)
    kb.act(dec[0:4, :], dec[0:4, :], AF.Exp)
    kb.act(Ra[0:8, :], Ra[0:8, :], AF.Exp, bias=gp[0:8, 0:1])
    kb.act(Ra[0:8, :], Ra[0:8, :], AF.Ln, bias=1.0)
    kb.act(gp[0:8, 2:3], gp[0:8, 1:2], AF.Exp)
    kb.ts("vector", gp[0:8, 3:4], gp[0:8, 2:3], -1.0, None, ALU.mult)
    kb.ts("vector", Ra[0:8, :], Ra[0:8, :], gp[0:8, 3:4], None, ALU.mult)
    kb.scan(BR[0:8, 0:2048], flag[0:8, :], Ra[0:8, 0:2048], 0.0, ALU.mult, ALU.add)
    kb.cp("vector", BR[0:8, S0:S0 + 16], Ra[0:8, S0:S0 + 16])
    for t in range(1, 4):
        kb.tt("vector", BR[0:8, S0 + 16 * t:S0 + 16 * t + 16], BR[0:8, S0 + 16 * t - 16:S0 + 16 * t], Ra[0:8, S0 + 16 * t:S0 + 16 * t + 16], ALU.add)
    kb.memset("vector", BR[0:8, S0 + 64:TT], 0.0)
    Dl = Ra
    kb.tt("vector", Dl[0:8, 0:2048].rearrange("p (c l) -> p c l", c=16),
          bc3(BR[0:8, 127:2048:128], [8, 16, 128], 2), BR[0:8, 0:2048].rearrange("p (c l) -> p c l", c=16), ALU.subtract)
    for t in range(4):
        kb.tt("vector", Dl[0:8, S0 + 16 * t:S0 + 16 * t + 16], BR[0:8, S0 + 48:S0 + 64], BR[0:8, S0 + 16 * t:S0 + 16 * t + 16], ALU.subtract)
    kb.memset("vector", Dl[0:8, S0 + 64:TT], 0.0)
    ebl = cv.f32(32, parts=8)
    kb.act(ebl[0:8, 0:16], BR[0:8, 127:2048:128], AF.Exp)
    kb.act(ebl[0:8, 16:32], BR[0:8, S0 + 48:S0 + 64], AF.Exp)
    for h in range(8):
        kb.mm(P[2][:, h * 32:(h + 1) * 32], selh[0:8, h * 128:(h + 1) * 128], ebl[0:8, 0:32])
    kb.cp("vector", ebLB, P[2][:, 0:256])
    for h in range(4):
        kb.mm(P[3][:, h * 16:(h + 1) * 16], selh[0:4, h * 128:(h + 1) * 128], dec[0:4, 0:16])
    kb.cp("vector", decB, P[3][:, 0:64])
    for tl in range(NT):
        pp = P[4 + tl % 4]
        sl = slice(tl * 128, (tl + 1) * 128)
        for qi, (R_, n) in enumerate(((Am, 4), (MR, 4), (Bm, 4), (Wf, 4))):
            kb.tr(pp[:, qi * 4:qi * 4 + 4], R_[0:4, sl], ident[0:4, 0:4])
        for qi, R_ in enumerate((BR, Rb, Dl)):
            kb.tr(pp[:, 16 + qi * 8:24 + qi * 8], R_[0:8, sl], ident[0:8, 0:8])
        if tl == 16:
            kb.tr(pp[:, 40:44], m0r[0:4, 0:128], ident[0:4, 0:4])
            kb.cp("vector", cols[:, tl, 0:44], pp[:, 0:44])
        else:
            kb.cp("vector" if tl % 2 else "scalar", cols[:, tl, 0:40], pp[:, 0:40])
    cv.release()
    kb.tt("vector", dcol[:, :, 0:4], cols[:, :, 4:8], cols[:, :, 8:12], ALU.add)
    kb.act(dcol[:, :, 0:4], dcol[:, :, 0:4], AF.Exp, scale=-1.0)
    kb.act(dcol[:, :, 4:8], cols[:, :, 12:16], AF.Exp)
    kb.act(dcol[:, :, 8:16], cols[:, :, 24:32], AF.Sigmoid)
    kb.act(dcol[:, :, 16:24], cols[:, :, 16:24], AF.Exp)
    kb.tt("vector", dcol[:, :, 16:24], dcol[:, :, 16:24], dcol[:, :, 8:16], ALU.mult)
    kb.ts("vector", dcol[:, :, 24:32], dcol[:, :, 8:16], -1.0, None, ALU.mult)
    kb.act(dcol[:, :, 32:40], cols[:, :, 32:40], AF.Exp)
    kb.ts("vector", dcol[:, :, 40:48], cols[:, :, 16:24], -1.0, None, ALU.mult)
    iw = cv.f32(4)
    kb.tt("vector", iw, cols[:, 16, 40:44], cols[:, 16, 4:8], ALU.subtract)
    kb.act(iw, iw, AF.Exp)
    G = dict(cols=cols, dcol=dcol, decB=decB, ebLB=ebLB, iw=iw, MR=MR, BR=BR, selh=selh, maskB=maskB,
             maskP=maskP, maskPb=maskPb, loadw=loadw, pit=pit, wbuf=wbuf)
    kb.phase("mlstm")
    mlstm_heads(nc, kb, cv, P, L, G)
    kb.phase("gdn")
    gdn_heads(nc, kb, cv, P, L, G)
    cv.release()


def sel_store(kb, P, pidx, mixT, src_bf, ch, tl, identb, selb):
    pp = P[pidx]
    if tl == 16:
        kb.mm(pp[:, 0:128], src_bf, identb, start=True, stop=True)
        kb.cp("vector", mixT[:, ch, 1024:1088], pp[:, 0:64])
    elif tl < 8:
        kb.mm(pp[:, 0:128], src_bf, selb[:, 0:128], start=True, stop=True)
        kb.cp("vector", mixT[:, ch, tl * 128:(tl + 1) * 128], pp[:, 0:128])
    else:
        i2 = tl - 8
        kb.mm(pp[:, 0:128], src_bf, selb[:, 128:256], start=True, stop=True)
        kb.tt("vector", mixT[:, ch, i2 * 128:(i2 + 1) * 128], pp[:, 0:128], mixT[:, ch, i2 * 128:(i2 + 1) * 128], ALU.add)


def mlstm_heads(nc, kb, cv, P, L, G):
    hT = L["hT"]; ident = L["ident"]; identb = L["identb"]; selb = L["selb"]; mixT = L["mixT"]
    capT_p = L["capT_p"]; capT_s = L["capT_s"]
    cols = G["cols"]; dcol = G["dcol"]; decB = G["decB"]; iw = G["iw"]; MR = G["MR"]; selh = G["selh"]
    maskP = G["maskP"]; loadw = G["loadw"]; pit = G["pit"]
    cv.mark()
    qT = cv.bf(2 * TT).rearrange("p (c t) -> p c t", c=2)
    kT = cv.bf(2 * TT).rearrange("p (c t) -> p c t", c=2)
    vx = cv.bf(NT * 258).rearrange("p (t v) -> p t v", t=NT)
    go = cv.bf(NT * 256).rearrange("p (t v) -> p t v", t=NT)
    negMB = cv.f32(512); Dt = cv.f32(512); tmpd = cv.f32(128)
    sw = [cv.bf(512) for _ in range(2)]
    hmix = [cv.bf(256) for _ in range(4)]
    junk = cv.f32(256); gnb = cv.f32(256); ec4 = cv.f32(64)
    kwt = [cv.bf(256) for _ in range(2)]
    kwm = [cv.bf(256) for _ in range(2)]
    dt2_off = cv.pos
    C0 = cv.f32(2 * 257).rearrange("p (c v) -> p c v", c=2)
    Dts = [Dt, L["arena"][:, dt2_off:dt2_off + 512]]
    C0b = cv.bf(2 * 258).rearrange("p (c v) -> p c v", c=2)
    Cout = cv.f32(2 * 257).rearrange("p (c v) -> p c v", c=2)
    stage = [cv.f32(257) for _ in range(2)]
    accS = cv.f32(257)
    n0all = cv.f32(32).rearrange("p (b c) -> p b c", c=2)
    noall = cv.f32(32).rearrange("p (b c) -> p b c", c=2)
    ar = L["arena"]
    nm_off = negMB.offset % L["NA"]
    st_off = stage[0].offset % L["NA"]
    C0_1 = ar[:, nm_off:nm_off + 514].rearrange("p (c v) -> p c v", c=2)
    C0b_1 = ar[:, nm_off + 514:nm_off + 514 + 258].bitcast(BF16).rearrange("p (c v) -> p c v", c=2)
    Cout_1 = ar[:, st_off:st_off + 514].rearrange("p (c v) -> p c v", c=2)
    CS = [(C0, C0b, Cout), (C0_1, C0b_1, Cout_1)]
    kb.memset("vector", vx[:, :, 256:258], 1.0)
    for h in range(4):
        kb.dma(gnb, L["mnorm_d"][h * 256:(h + 1) * 256].partition_broadcast(128))
        for (dst, c0, sc) in ((qT, h * 256, 1.0), (kT, 1024 + h * 256, 1.0 / 16.0)):
            wb = loadw(c0, 256)
            for cc in range(2):
                def ev(pp, t0, n, dst=dst, cc=cc, sc=sc):
                    kb.act(dst[:, cc, t0:t0 + n], pp, AF.Copy, scale=sc)
                proj_fm(kb, P, pit, wb[:, :, cc * 128:(cc + 1) * 128], hT, None, ev)
        wv = loadw(2048 + h * 256, 256)
        for tl in range(NT):
            pp = P[pit[0] % 2]; pit[0] += 1
            for kc in range(16):
                kb.mm(pp[:, 0:256], hT[:, kc, tl * 128:(tl + 1) * 128], wv[:, kc, 0:256], start=(kc == 0), stop=(kc == 15))
            kb.cp("vector", vx[:, tl, 0:256], pp[:, 0:256])
        wo_ = loadw(3072 + h * 256, 256)
        for tl in range(NT):
            pp = P[pit[0] % 2]; pit[0] += 1
            for kc in range(16):
                kb.mm(pp[:, 0:256], hT[:, kc, tl * 128:(tl + 1) * 128], wo_[:, kc, 0:256], start=(kc == 0), stop=(kc == 15))
            kb.act(junk, pp[:, 0:256], AF.Sigmoid)
            kb.tt("vector", go[:, tl, :], junk, gnb, ALU.mult)
        kb.memset("vector", accS, 0.0)
        for cc_ in range(2):
            kb.dma(n0all[:, :, cc_], L["sn_d"][:, h, cc_ * 128:(cc_ + 1) * 128].rearrange("b p -> p b"))
        def ldC(b_):
            C0_, _, _ = CS[b_ % 2]
            kb.dma(C0_[:, :, 0:256], L["sC_d"][b_, h].rearrange("(c p) v -> p c v", p=128))
            kb.cp("vector", C0_[:, :, 256:257], n0all[:, b_, :].unsqueeze(2))

        ldC(0)
        for b in range(NSEQ):
            C0, C0b, Cout = CS[b % 2]
            if b + 1 < NSEQ:
                ldC(b + 1)
            kb.cp("scalar", C0b[:, :, 0:257], C0)
            for cc in range(2):
                kb.mm(P[2][:, 0:257], qT[:, cc, 2048:2176], C0b[:, cc, 0:257], start=(cc == 0), stop=(cc == 1))
            kb.stt("vector", accS, P[2][:, 0:257], maskP[:, b:b + 1], accS, ALU.mult, ALU.add)
            for cc in range(2):
                kt = kwt[cc]
                if b == 0:
                    kb.tr(P[3].bitcast(BF16)[:, cc * 128:(cc + 1) * 128], kT[:, cc, 2048:2176], identb)
                    kb.ts("vector", kt[:, 0:128], P[3].bitcast(BF16)[:, cc * 128:(cc + 1) * 128], dcol[:, 16, 4 + h:5 + h], None, ALU.mult)
            km = kwm[b % 2]
            kb.ts("vector", km[:, 0:128], kwt[0][:, 0:128], maskP[:, b:b + 1], None, ALU.mult)
            kb.ts("vector", km[:, 128:256], kwt[1][:, 0:128], maskP[:, b:b + 1], None, ALU.mult)
            for cc in range(2):
                kb.mm(P[4 + cc][:, 0:257], km[:, cc * 128:(cc + 1) * 128], vx[:, 16, 0:257], start=True, stop=True)
                kb.stt("vector", Cout[:, cc, :], C0[:, cc, :], decB[:, h * 16 + b:h * 16 + b + 1], P[4 + cc][:, 0:257], ALU.mult, ALU.add)
            kb.dma(L["oC_o"][b, h].rearrange("(c p) v -> p c v", p=128), Cout[:, :, 0:256])
            kb.cp("vector", noall[:, b, :].unsqueeze(2), Cout[:, :, 256:257])
        for cc_ in range(2):
            kb.dma(L["on_o"][:, h, cc_ * 128:(cc_ + 1) * 128].rearrange("b p -> p b"), noall[:, :, cc_])
        blocks = [(0, 4), (4, 4), (8, 4), (12, 4), (16, 1)]
        for (tb0, ntl) in blocks:
            n = ntl * 128
            t0 = tb0 * 128
            kb.mm(P[0][:, 0:n], selh[0:4, h * 128:(h + 1) * 128], MR[0:4, t0:t0 + n], start=True, stop=True)
            kb.ts("vector", negMB[:, 0:n], P[0][:, 0:n], -1.0, None, ALU.mult)
            jlist = list(range(0, tb0 + ntl)) if tb0 < 16 else [16]

            def qk(j):
                jl = j - tb0
                c0 = max(jl, 0) * 128
                pq = P[6 + j % 2]
                for cc in range(2):
                    kb.mm(pq[:, c0:n], kT[:, cc, j * 128:(j + 1) * 128], qT[:, cc, t0 + c0:t0 + n], start=(cc == 0), stop=(cc == 1))

            qk(jlist[0])
            for ji, j in enumerate(jlist):
                jl = j - tb0
                c0 = max(jl, 0) * 128
                pq = P[6 + j % 2]
                Dt = Dts[j % 2]
                acol = cols[:, j, h:h + 1]
                if jl >= 0:
                    cap = capT_s if tb0 == 16 else capT_p
                    kb.stt("vector", tmpd, negMB[:, c0:c0 + 128], acol, cap, ALU.add, ALU.min)
                    kb.act(Dt[:, c0:c0 + 128], tmpd, AF.Exp)
                    if c0 + 128 < n:
                        kb.act(Dt[:, c0 + 128:n], negMB[:, c0 + 128:n], AF.Exp, bias=acol)
                else:
                    kb.act(Dt[:, c0:n], negMB[:, c0:n], AF.Exp, bias=acol)
                if ji + 1 < len(jlist):
                    qk(jlist[ji + 1])
                s_ = sw[j % 2]
                kb.tt("vector", s_[:, c0:n], pq[:, c0:n], Dt[:, c0:n], ALU.mult)
                for il in range(max(jl, 0), ntl):
                    i = tb0 + il
                    kb.mm(P[2 + il][:, 0:257], s_[:, il * 128:(il + 1) * 128], vx[:, j, 0:257],
                          start=(j == jlist[0]), stop=(j == i))

            def epi(il):
                i = tb0 + il
                acc = P[2 + il]
                e_ = ec4[:, il * 16:(il + 1) * 16]
                if i == 16:
                    kb.stt("vector", accS, accS, iw[:, h:h + 1], acc[:, 0:257], ALU.mult, ALU.add)
                    num = accS
                else:
                    num = acc
                kb.cp("vector", e_[:, 8:9], num[:, 256:257])
                kb.act(hmix[il], num[:, 0:256], AF.Square, accum_out=e_[:, 3:4])
                yield
                kb.stt("vector", e_[:, 0:1], e_[:, 8:9], -1.0, e_[:, 8:9], ALU.mult, ALU.max)
                yield
                kb.tt("vector", e_[:, 1:2], e_[:, 0:1], dcol[:, i, h:h + 1], ALU.max)
                yield
                kb.op("vector", lambda e, o=e_[:, 2:3], a=e_[:, 1:2]: e.reciprocal(out=o, in_=a), reads=[e_[:, 1:2]], writes=[e_[:, 2:3]])
                yield
                kb.tt("vector", e_[:, 4:5], e_[:, 2:3], e_[:, 2:3], ALU.mult)
                yield
                kb.stt("vector", e_[:, 5:6], e_[:, 3:4], 1.0 / 256.0, e_[:, 4:5], ALU.mult, ALU.mult)
                yield
                kb.act(e_[:, 6:7], e_[:, 5:6], AF.Ln, bias=1e-6)
                yield
                kb.act(e_[:, 6:7], e_[:, 6:7], AF.Exp, scale=-0.5)
                yield
                kb.tt("vector", e_[:, 7:8], e_[:, 6:7], e_[:, 2:3], ALU.mult)
                yield
                hm = hmix[il]
                kb.stt("vector", hm, num[:, 0:256], e_[:, 7:8], go[:, i, :], ALU.mult, ALU.mult)
                yield
                for cc in range(2):
                    sel_store(kb, P, cc, mixT, hm[:, cc * 128:(cc + 1) * 128], 2 * h + cc, i, identb, selb)

            gens = [epi(il) for il in range(ntl)]
            while gens:
                for g_ in list(gens):
                    try:
                        next(g_)
                    except StopIteration:
                        gens.remove(g_)
        for cc in range(2):
            for tl in range(16):
                kt = kwt[tl % 2]
                kb.tr(P[0].bitcast(BF16)[:, (tl % 2) * 128:(tl % 2) * 128 + 128], kT[:, cc, tl * 128:(tl + 1) * 128], identb)
                kb.ts("vector", kt[:, 0:128], P[0].bitcast(BF16)[:, (tl % 2) * 128:(tl % 2) * 128 + 128], dcol[:, tl, 4 + h:5 + h], None, ALU.mult)
                kb.mm(P[6][:, 0:257], kt[:, 0:128], vx[:, tl, 0:257], start=(tl == 0), stop=(tl == 15))
            sg = stage[cc]
            kb.cp("vector", sg, P[6][:, 0:257])
            kb.dma(L["pC_o"][h, cc * 128:(cc + 1) * 128, :], sg[:, 0:256])
            kb.dma(L["pn_o"][h, cc * 128:(cc + 1) * 128].rearrange("(p o) -> p o", o=1), sg[:, 256:257])
    cv.release()


def gdn_heads(nc, kb, cv, P, L, G):
    hT = L["hT"]; ident = L["ident"]; identb = L["identb"]; selb = L["selb"]; mixT = L["mixT"]
    capT_p = L["capT_p"]; capE_p = L["capE_p"]; capT_s = L["capT_s"]; capE_s = L["capE_s"]
    cols = G["cols"]; dcol = G["dcol"]; ebLB = G["ebLB"]; BR = G["BR"]; selh = G["selh"]
    maskP = G["maskP"]; loadw = G["loadw"]; pit = G["pit"]
    GW = 4096
    w_in_d = L["w_in_d"]
    wbuf = G["wbuf"]
    wsl = [wbuf[0][:, :, 0:128], wbuf[0][:, :, 128:256], wbuf[1][:, :, 0:128], wbuf[1][:, :, 128:256]]
    wq3 = wsl[0:3]
    NG = int(os.environ.get("KB_NG", "4"))
    cv.mark()
    qT = cv.f32(TT); kT = cv.f32(TT)
    vtok = cv.bf(NT * 128).rearrange("p (t v) -> p t v", t=NT)
    cw = cv.f32(4); gnb = cv.f32(128); junk = cv.f32(128); ec = cv.f32(8)
    Sst = cv.f32(128); U = cv.f32(128)
    hmx = [cv.bf(128) for _ in range(2)]
    base = cv.pos
    cinS = [cv.f32(3 + 512 + 128) for _ in range(2)]; caccS = [cv.f32(512 + 128) for _ in range(2)]
    vblkS = [cv.f32(512) for _ in range(2)]
    sqS = [cv.f32(512) for _ in range(2)]; tl_rows = cv.f32(512); c0T = cv.f32(48)
    cit = [0]
    cw3 = cv.f32(12); c0T3 = cv.f32(144)
    end1 = cv.pos
    cv.pos = base
    TA = [[cv.f32(128) for _ in range(7)] for _ in range(max(NG, 4))]
    TB0 = [[cv.f32(128) for _ in range(6)] for _ in range(max(NG, 4))]
    mr_off = G["MR"].offset % L["NA"]
    pool2 = [L["arena"][:, mr_off + i * 128:mr_off + (i + 1) * 128] for i in range(17)]
    TB = TB0
    S0b = [TA[1][i] for i in range(4)]
    Sout = [TA[2][i] for i in range(2)]
    cv.pos = max(cv.pos, end1)
    pool2 += [cv.f32(128) for _ in range(24 - 17)]
    TB1 = [pool2[i * 6:(i + 1) * 6] for i in range(4)]
    TBS = [TB0, TB1]
    for h in range(8):
        kb.phase("gdnconv")
        kb.dma(gnb, L["gnorm_d"][h * 128:(h + 1) * 128].partition_broadcast(128))
        def conv_block(qi, bi, t0, n, dstT0, cbase, wb, cw, c0T, slot, cprev):
            pp = P[slot % 2]
            cin = cinS[slot % 2]; cacc = caccS[slot % 2]; vblk = vblkS[slot % 2]; sq = sqS[slot % 2]
            psq = P[2 + slot % 2]
            for kc in range(16):
                kb.mm(pp[:, 0:n], wb[:, kc, 0:128], hT[:, kc, t0:t0 + n], start=(kc == 0), stop=(kc == 15))
            yield
            if bi == 0:
                kb.memset("vector", cin[:, 0:3], 0.0)
            elif bi < 4:
                kb.cp("vector", cin[:, 0:3], cprev[:, 512:515])
            if bi < 4:
                kb.cp("scalar", cin[:, 3:3 + n], pp[:, 0:n])
                yield
                kb.ts("vector", cacc[:, 0:n], cin[:, 0:n], cw[:, 0:1], None, ALU.mult)
                for j in range(1, 4):
                    kb.stt("vector", cacc[:, 0:n], cin[:, j:j + n], cw[:, j:j + 1], cacc[:, 0:n], ALU.mult, ALU.add)
                if bi == 3:
                    kb.tr(P[2][0:3, 128:256], cin[:, 512:515], ident)
                    kb.cp("vector", tl_rows[0:3, 128:256], P[2][0:3, 128:256])
                    kb.dma(L["pconv_o"][:, cbase - GW:cbase - GW + 128], tl_rows[0:3, 128:256])
            else:
                kb.cp("vector", cin[:, 0:48], c0T)
                kb.cp("scalar", cin[:, 48:48 + 128], pp[:, 0:128])
                yield
                kb.ts("vector", cacc[:, 0:128], cin[:, 0:128], cw[:, 0:1], None, ALU.mult)
                for j in range(1, 4):
                    kb.stt("vector", cacc[:, 0:128], cin[:, 16 * j:16 * j + 128], cw[:, j:j + 1], cacc[:, 0:128], ALU.mult, ALU.add)
                kb.tr(P[2][0:48, 256:384], cin[:, 64:112], ident)
                kb.cp("vector", tl_rows[0:48, 256:384], P[2][0:48, 256:384])
                kb.dma(L["oconv_o"][:, cbase - GW:cbase - GW + 128], tl_rows[0:48, 256:384])
            yield
            if qi == 2:
                kb.act(vblk[:, 0:n], cacc[:, 0:n], AF.Silu)
                yield
                for ti in range(n // 128):
                    tl = t0 // 128 + ti
                    kb.tr(psq[:, (ti % 4) * 128:(ti % 4) * 128 + 128], vblk[:, ti * 128:(ti + 1) * 128], ident)
                yield
                for ti in range(n // 128):
                    tl = t0 // 128 + ti
                    kb.cp("scalar" if ti % 2 else "vector", vtok[:, tl, :], psq[:, (ti % 4) * 128:(ti % 4) * 128 + 128])
                return
            dstT = dstT0
            kb.act(dstT[:, t0:t0 + n], cacc[:, 0:n], AF.Silu)
            kb.act(sq[:, 0:n], dstT[:, t0:t0 + n], AF.Square)
            yield
            kb.mm(psq[:, 0:n], L["ones_f"], sq[:, 0:n], start=True, stop=True)
            yield
            kb.rsqrt(sq[:, 0:n], psq[:, 0:n], 1e-6)
            yield
            if qi == 0:
                kb.stt("vector", dstT[:, t0:t0 + n], dstT[:, t0:t0 + n], 128.0 ** -0.5, sq[:, 0:n], ALU.mult, ALU.mult)
            else:
                kb.tt("vector", dstT[:, t0:t0 + n], dstT[:, t0:t0 + n], sq[:, 0:n], ALU.mult)

        blocks = []
        for qi, (dstT0, cbase) in enumerate(((qT, GW + h * 128), (kT, GW + 1024 + h * 128), (None, GW + 2048 + h * 128))):
            wb = wq3[qi]
            kb.dma(wb, w_in_d[:, cbase:cbase + 128].rearrange("(k p) n -> p k n", p=128), q="gpsimd")
            cw = cw3[:, qi * 4:(qi + 1) * 4]
            kb.dma(cw, L["convw_d"][cbase - GW:cbase - GW + 128, :])
            kb.dma(tl_rows[0:48, 0:128], L["sconv_d"][:, cbase - GW:cbase - GW + 128])
            kb.tr(P[2][:, 384:432], tl_rows[0:48, 0:128], ident[0:48, 0:48])
            c0T = c0T3[:, qi * 48:(qi + 1) * 48]
            kb.cp("vector", c0T, P[2][:, 384:432])
            for bi, (t0, n) in enumerate([(0, 512), (512, 512), (1024, 512), (1536, 512), (2048, 128)]):
                blocks.append((qi, bi, t0, n, dstT0, cbase, wb, cw, c0T))
        active = []
        nxt = 0
        while nxt < len(blocks) or active:
            if nxt < len(blocks) and len(active) < 2:
                slot = cit[0]; cit[0] += 1
                active.append(conv_block(*blocks[nxt], slot, cinS[(slot + 1) % 2]))
                nxt += 1
            for g_ in list(active):
                try:
                    next(g_)
                except StopIteration:
                    active.remove(g_)
        wz = wsl[3]
        kb.dma(wz, w_in_d[:, GW + 3072 + h * 128:GW + 3072 + (h + 1) * 128].rearrange("(k p) n -> p k n", p=128), q="gpsimd")
        kb.phase("gdnchunk")
        kb.memset("vector", Sst, 0.0)

        def stageA(c, u, TB):
            sl = slice(c * 128, (c + 1) * 128)
            smp = (c == 16)
            capT = capT_s if smp else capT_p
            capE = capE_s if smp else capE_p
            bcol = cols[:, c, 16 + h:17 + h]; negb = dcol[:, c, 40 + h:41 + h]
            bet = dcol[:, c, 8 + h:9 + h]; bete = dcol[:, c, 16 + h:17 + h]; nbet = dcol[:, c, 24 + h:25 + h]
            edl = dcol[:, c, 32 + h:33 + h]
            ET, E, Pm, Q, R, kbe, vb = TA[u]
            WTn, U0, qe, PT, wk, szt = TB[u]
            bk = P[4 + u]
            r0, r1, r2, r3 = bk[:, 0:128], bk[:, 128:256], bk[:, 256:384], bk[:, 384:512]
            kb.mm(r0, selh[0:8, h * 128:(h + 1) * 128], BR[0:8, sl], start=True, stop=True)
            kb.mm(r1, kT[:, sl], kT[:, sl], start=True, stop=True)
            kb.mm(r2, kT[:, sl], qT[:, sl], start=True, stop=True)
            kb.tr(r3, kT[:, sl], ident)
            yield
            kb.stt("vector", ET, r0, negb, capT, ALU.add, ALU.min)
            kb.stt("vector", E, r0, bcol, capE, ALU.subtract, ALU.max)
            kb.act(WTn, r0, AF.Exp)
            kb.ts("vector", vb, vtok[:, c, :], bet, None, ALU.mult)
            yield
            kb.act(ET, ET, AF.Exp)
            kb.act(E, E, AF.Exp, scale=-1.0)
            kb.ts("vector", kbe, r3, bete, None, ALU.mult)
            kb.ts("vector", wk, r3, edl, None, ALU.mult)
            kb.tt("vector", qe, qT[:, sl], WTn, ALU.mult)
            yield
            kb.stt("vector", Pm, r1, nbet, E, ALU.mult, ALU.mult)
            kb.tt("vector", PT, r2, ET, ALU.mult)
            yield
            kb.tr(r0, Pm, ident)
            pz = P[pit[0] % 2]; pit[0] += 1
            for kc in range(16):
                kb.mm(pz[:, 0:128], hT[:, kc, sl], wz[:, kc, 0:128], start=(kc == 0), stop=(kc == 15))
            kb.act(szt, pz[:, 0:128], AF.Silu)
            yield
            kb.cp("scalar", Q, r0)
            kb.tt("vector", R, r0, ident, ALU.add)
            kb.tt("vector", szt, szt, gnb, ALU.mult)
            yield
            Pc, Qc, Pn, Qn = Pm, Q, E, ET
            nst = 2 if smp else 6
            for k in range(nst):
                kb.mm(r1, Qc, Pc, start=True, stop=True)
                if k < nst - 1:
                    kb.mm(r2, Pc, Qc, start=True, stop=True)
                yield
                kb.cp("scalar", Pn, r1)
                if k < nst - 1:
                    kb.cp("vector", Qn, r2)
                yield
                kb.mm(r3, Pn, R, start=True, stop=True)
                yield
                kb.tt("vector", R, R, r3, ALU.add)
                Pc, Qc, Pn, Qn = Pn, Qn, Pc, Qc
            yield
            kb.mm(r0, kbe, R, start=True, stop=True)
            kb.mm(r1, R, vb, start=True, stop=True)
            yield
            kb.ts("vector", WTn, r0, -1.0, None, ALU.mult)
            kb.cp("scalar", U0, r1)

        def out_epi(c, onum, szt):
            kb.act(junk, onum, AF.Square, accum_out=ec[:, 0:1])
            kb.ts("vector", ec[:, 1:2], ec[:, 0:1], 1.0 / 128.0, 1e-6, ALU.mult, ALU.add)
            kb.rsqrt(ec[:, 2:3], ec[:, 1:2], 0.0)
            hm = hmx[c % 2]
            kb.stt("vector", hm, onum, ec[:, 2:3], szt, ALU.mult, ALU.mult)
            sel_store(kb, P, c % 2, mixT, hm, 8 + h, c, identb, selb)

        def stageB(grp, TB):
            for u, c in enumerate(grp):
                WTn, U0, qe, PT, wk, szt = TB[u]
                if c < 16:
                    kb.mm(P[2][:, 0:128], WTn, Sst, start=True, stop=True)
                    kb.mm(P[3][:, 0:128], qe, Sst, start=True, stop=False)
                    yield
                    kb.tt("vector", U, U0, P[2][:, 0:128], ALU.add)
                    yield
                    kb.mm(P[3][:, 0:128], PT, U, start=False, stop=True)
                    kb.mm(P[2][:, 128:256], wk, U, start=True, stop=True)
                    yield
                    kb.stt("vector", Sst, Sst, ebLB[:, h * 32 + c:h * 32 + c + 1], P[2][:, 128:256], ALU.mult, ALU.add)
                    if c == 15:
                        kb.dma(L["pS_o"][h], Sst)
                    out_epi(c, P[3][:, 0:128], szt)
                    yield
                else:
                    kbe, vb = TA[u][5], TA[u][6]
                    kb.cp("vector", U, U0)
                    oacc = kbe
                    kb.memset("vector", oacc, 0.0)
                    for b_ in range(3):
                        kb.dma(S0b[b_ % 4], L["sS_d"][b_, h])
                    for b in range(NSEQ):
                        sb = S0b[b % 4]
                        if b + 3 < NSEQ:
                            kb.dma(S0b[(b + 3) % 4], L["sS_d"][b + 3, h])
                        pr = P[2] if b % 2 == 0 else P[3]
                        kb.mm(pr[:, 0:128], WTn, sb, start=True, stop=True)
                        kb.mm(pr[:, 128:256], qe, sb, start=True, stop=True)
                        kb.stt("vector", U, pr[:, 0:128], maskP[:, b:b + 1], U, ALU.mult, ALU.add)
                        kb.stt("vector", oacc, pr[:, 128:256], maskP[:, b:b + 1], oacc, ALU.mult, ALU.add)
                    kb.mm(P[2][:, 256:384], PT, U, start=True, stop=True)
                    kb.tt("vector", oacc, oacc, P[2][:, 256:384], ALU.add)
                    for b_ in range(3):
                        kb.dma(S0b[b_ % 4], L["sS_d"][b_, h])
                    for b in range(NSEQ):
                        sb = S0b[b % 4]
                        if b + 3 < NSEQ:
                            kb.dma(S0b[(b + 3) % 4], L["sS_d"][b + 3, h])
                        vm = TA[3][b % 4]
                        kb.ts("vector", vm, U, maskP[:, b:b + 1], None, ALU.mult)
                        pr = P[2] if b % 2 == 0 else P[3]
                        kb.mm(pr[:, 384:512], wk, vm, start=True, stop=True)
                        so = Sout[b % 2]
                        kb.stt("vector", so, sb, ebLB[:, h * 32 + 16 + b:h * 32 + 17 + b], pr[:, 384:512], ALU.mult, ALU.add)
                        kb.dma(L["oS_o"][b, h], so)
                    out_epi(c, oacc, szt)

        groups = [list(range(g, g + NG)) for g in range(0, 16, NG)] + [[16]]
        pend = None
        for gi, grp in enumerate(groups):
            TBc = TBS[gi % 2]
            gens = [stageA(c, u, TBc) for u, c in enumerate(grp)]
            if pend is not None:
                gens.append(pend)
            while gens:
                for g_ in list(gens):
                    try:
                        next(g_)
                    except StopIteration:
                        gens.remove(g_)
            pend = stageB(grp, TBc)
        for _ in pend:
            pass
    cv.release()


_NC_CACHE = {}


def kernel(**inputs):
    if "nc" not in _NC_CACHE:
        _NC_CACHE["nc"] = build()
    nc = _NC_CACHE["nc"]
    maps = make_in_maps(inputs)
    res = run_bass_kernel_spmd(nc, maps, core_ids=list(range(8)))
    R = res.results
    f32 = np.float32
    y_p = np.zeros((4, 2048, D), f32); y_s = np.zeros((128, 4, D), f32)
    p_C = np.zeros((4, 4, 256, 256), f32); p_n = np.zeros((4, 4, 256), f32); p_m = np.zeros((4, 4), f32)
    p_S = np.zeros((4, 8, 128, 128), f32); p_conv = np.zeros((4, 3, 3072), f32)
    s_C = np.zeros((128, 4, 256, 256), f32); s_n = np.zeros((128, 4, 256), f32); s_m = np.zeros((128, 4), f32)
    s_S = np.zeros((128, 8, 128, 128), f32); s_conv = np.zeros((128, 3, 3072), f32)
    for c in range(8):
        p, j = c // 2, c % 2
        sq = slice(c * NSEQ, (c + 1) * NSEQ)
        r = R[c]
        ym = np.asarray(r["ym"], f32)
        y_p[p, j * 1024:(j + 1) * 1024] = ym[0:1024]
        y_s[sq] = ym[1024:1088].reshape(4, NSEQ, D).transpose(1, 0, 2)
        if j == 0:
            p_C[p] = r["pC"]; p_n[p] = r["pn"]; p_m[p] = np.asarray(r["pm"])[:, 0]
            p_S[p] = r["pS"]; p_conv[p] = r["pconv"]
        s_C[sq] = r["oC"]; s_n[sq] = r["on"]; s_m[sq] = np.asarray(r["om"]).T
        s_S[sq] = r["oS"]
        s_conv[sq] = np.asarray(r["oconv"]).reshape(3, NSEQ, 3072).transpose(1, 0, 2)
    return (y_p, y_s, p_C, p_n, p_m, p_S, p_conv, s_C, s_n, s_m, s_S, s_conv)
```

```python
import os
import numpy as np
import concourse.bass as bass
import concourse.mybir as mybir
from concourse.bass_utils import run_bass_kernel_spmd

F32 = mybir.dt.float32
BF16 = mybir.dt.bfloat16
AF = mybir.ActivationFunctionType
ALU = mybir.AluOpType
AX = mybir.AxisListType

D = 2048
NT = 17
TT = NT * 128
MYT = 1088
DFF = 5632
INC = 8216
NSEQ = 16
ALPHA = 2.0 ** 0.25
NEG = -1.0e30


def _dsize(dt):
    return 2 if dt == BF16 else 4


class KB:
    ENG = ("tensor", "vector", "scalar", "gpsimd", "sync")

    def __init__(self, nc, esems, dsems):
        self.nc = nc
        self.ops = {e: [] for e in self.ENG}
        self.cnt = {e: 0 for e in self.ENG}
        self.esems = esems
        self.dsems = dsems
        self.dcnt = [0] * len(dsems)
        self.dnext = 0
        self.dnext_sw = 0
        self.waited = {e: {} for e in self.ENG}
        self.W = {}
        self.R = {}
        self.needed = {}

    def _sem(self, key):
        return self.esems[key] if isinstance(key, str) else self.dsems[key]

    coarse = False
    coarse_pe = False
    psum_excl = os.environ.get("KB_PSX", "1") == "1"
    ce = tuple(os.environ.get("KB_CE", "tensor").split(","))

    def region(self, ap):
        name = ap.tensor.name
        if self.coarse and str(ap.space) in ("SB", "SBUF"):
            return name, 0, 1 << 40
        if str(ap.space) not in ("SB", "SBUF", "PSUM"):
            ext = 1
            for st, n in ap.ap:
                ext += abs(st) * (n - 1)
            return name, ap.offset * 4, (ap.offset + ext) * 4
        ds = _dsize(ap.dtype)
        dims = list(ap.ap)
        pstride = dims[0][0]
        off = ap.offset % pstride if pstride > 0 else ap.offset
        ext = 1
        for st, n in dims[1:]:
            ext += abs(st) * (n - 1)
        return name, off * ds, (off + ext) * ds

    def _deps(self, eng, reads, writes):
        need = {}

        def add(sig):
            k, v = sig
            if eng == "tensor" and k == "tensor":
                return
            if need.get(k, 0) < v:
                need[k] = v

        rr = [self.region(a) for a in reads]
        ww = [self.region(a) for a in writes]
        if self.psum_excl and eng != "tensor":
            extra = [(n_, 0, 4096) for (n_, l_, h_) in rr if n_.startswith("ps")]
            rr = rr + extra
            if os.environ.get("KB_PSRW", "1") == "1":
                ww = ww + extra
        if self.coarse_pe and eng in self.ce:
            rr = [(n_, 0, 1 << 40) if n_ == "arena" else (n_, l_, h_) for (n_, l_, h_) in rr]
            ww = [(n_, 0, 1 << 40) if n_ == "arena" else (n_, l_, h_) for (n_, l_, h_) in ww]
        for name, lo, hi in rr:
            for (l, h, sig) in self.W.get(name, ()):
                if l < hi and lo < h:
                    add(sig)
        for name, lo, hi in ww:
            for (l, h, sig) in self.W.get(name, ()):
                if l < hi and lo < h:
                    add(sig)
            for (l, h, k), v in self.R.get(name, {}).items():
                if l < hi and lo < h:
                    add((k, v))
        return need, rr, ww

    def _commit(self, rr, ww, sig):
        k, v = sig
        for name, lo, hi in rr:
            self.R.setdefault(name, {})[(lo, hi, k)] = v
        for name, lo, hi in ww:
            wl = [(l, h, s) for (l, h, s) in self.W.get(name, []) if not (lo <= l and h <= hi)]
            wl.append((lo, hi, sig))
            self.W[name] = wl
            rd = self.R.get(name)
            if rd:
                for key in [key for key in rd if lo <= key[0] and key[1] <= hi]:
                    del rd[key]

    def _waits(self, eng, need):
        out = []
        wd = self.waited[eng]
        for k, v in need.items():
            if wd.get(k, 0) < v:
                wd[k] = v
                out.append((k, v))
                if isinstance(k, str):
                    self.needed.setdefault(k, set()).add(v)
        return out

    def op(self, eng, fn, reads=(), writes=()):
        need, rr, ww = self._deps(eng, reads, writes)
        waits = self._waits(eng, need)
        self.cnt[eng] += 1
        sig = (eng, self.cnt[eng])
        self.ops[eng].append((waits, fn, ("E", self.cnt[eng])))
        self._commit(rr, ww, sig)

    def dma(self, out, in_, q="sync", rd=True, wr=True, after=()):
        reads = [in_] if rd else []
        writes = [out] if wr else []
        need, rr, ww = self._deps(q, reads, writes)
        for (ka, va) in after:
            if need.get(ka, 0) < va:
                need[ka] = va
        nd = len(self.dsems)
        nsw = nd // 3
        if q == "gpsimd":
            k = nd - nsw + self.dnext_sw
            self.dnext_sw = (self.dnext_sw + 1) % nsw
        else:
            k = self.dnext
            self.dnext = (self.dnext + 1) % (nd - nsw)
        if self.dcnt[k] > 0:
            if need.get(k, 0) < self.dcnt[k]:
                need[k] = self.dcnt[k]
        waits = self._waits(q, need)
        self.dcnt[k] += 16
        sig = (k, self.dcnt[k])
        self.ops[q].append((waits, lambda e, o=out, i=in_: e.dma_start(out=o, in_=i, allow_slow_non_contiguous=True), ("D", k)))
        self._commit(rr, ww, sig)
        return sig

    def mm(self, out, lhsT, rhs, start=True, stop=True):
        self.op("tensor", lambda e: e.matmul(out, lhsT=lhsT, rhs=rhs, start=start, stop=stop),
                reads=[lhsT, rhs], writes=[out])

    def tr(self, out, in_, ident):
        self.op("tensor", lambda e: e.transpose(out, in_, ident), reads=[in_, ident], writes=[out])

    def act(self, out, in_, func, bias=None, scale=1.0, accum_out=None, eng="scalar"):
        rd = [in_]
        kw = {}
        if bias is not None:
            kw["bias"] = bias
            if not isinstance(bias, (int, float)):
                rd.append(bias)
        if not isinstance(scale, (int, float)):
            rd.append(scale)
        wr = [out]
        if accum_out is not None:
            kw["accum_out"] = accum_out
            wr.append(accum_out)
        self.op("scalar", lambda e: e.activation(out=out, in_=in_, func=func, scale=scale, **kw),
                reads=rd, writes=wr)

    def ascale(self, out, in_, scale_ap):
        self.op("scalar", lambda e: e.activation(out=out, in_=in_, func=AF.Copy, scale=scale_ap),
                reads=[in_, scale_ap], writes=[out])

    def tt(self, eng, out, in0, in1, op):
        self.op(eng, lambda e: e.tensor_tensor(out=out, in0=in0, in1=in1, op=op), reads=[in0, in1], writes=[out])

    def ts(self, eng, out, in0, s1, s2, op0, op1=None):
        rd = [in0] + [s for s in (s1, s2) if s is not None and not isinstance(s, (int, float))]
        if op1 is None:
            self.op(eng, lambda e: e.tensor_scalar(out=out, in0=in0, scalar1=s1, scalar2=None, op0=op0),
                    reads=rd, writes=[out])
        else:
            self.op(eng, lambda e: e.tensor_scalar(out=out, in0=in0, scalar1=s1, scalar2=s2, op0=op0, op1=op1),
                    reads=rd, writes=[out])

    def stt(self, eng, out, in0, scalar, in1, op0, op1):
        rd = [in0, in1] + ([] if isinstance(scalar, (int, float)) else [scalar])
        self.op(eng, lambda e: e.scalar_tensor_tensor(out=out, in0=in0, scalar=scalar, in1=in1, op0=op0, op1=op1),
                reads=rd, writes=[out])

    def cp(self, eng, out, in_):
        if eng == "scalar":
            self.op(eng, lambda e: e.copy(out=out, in_=in_), reads=[in_], writes=[out])
        else:
            self.op(eng, lambda e: e.tensor_copy(out=out, in_=in_), reads=[in_], writes=[out])

    def rsqrt(self, out, in_, eps):
        self.act(out, in_, AF.Sqrt, bias=eps)
        self.op("vector", lambda e: e.reciprocal(out=out, in_=out), reads=[out], writes=[out])

    def memset(self, eng, out, val):
        self.op(eng, lambda e: e.memset(out, val), reads=[], writes=[out])

    def scan(self, out, d0, d1, init, op0, op1):
        rd = [d0, d1] + ([] if isinstance(init, (int, float)) else [init])
        self.op("vector", lambda e: e.tensor_tensor_scan(out=out, data0=d0, data1=d1, initial=init, op0=op0, op1=op1),
                reads=rd, writes=[out])

    def emit(self, block):
        kb = self
        import bisect
        for e2 in ("tensor", "vector", "scalar", "gpsimd"):
            if kb.cnt[e2] > 0:
                kb.needed.setdefault(e2, set()).add(kb.cnt[e2])
        order = {k: sorted(v) for k, v in kb.needed.items()}

        def semval(k, v):
            if isinstance(k, str):
                return bisect.bisect_right(order[k], v)
            return v

        def run(eng, name):
            for waits, fn, inc in kb.ops[name]:
                for k, v in waits:
                    eng.wait_ge(kb._sem(k), semval(k, v))
                ins = fn(eng)
                if inc[0] == "D":
                    ins.then_inc(kb.dsems[inc[1]], 16)
                elif inc[1] in kb.needed.get(name, ()):
                    ins.then_inc(kb.esems[name], 1)
            if name == "sync":
                for k, c in enumerate(kb.dcnt):
                    if c > 0:
                        eng.wait_ge(kb.dsems[k], c)
                for e2 in ("tensor", "vector", "scalar", "gpsimd"):
                    if kb.cnt[e2] > 0:
                        eng.wait_ge(kb.esems[e2], semval(e2, kb.cnt[e2]))

        @block.tensor
        def _(e):
            run(e, "tensor")

        @block.vector
        def _(e):
            run(e, "vector")

        @block.scalar
        def _(e):
            run(e, "scalar")

        @block.gpsimd
        def _(e):
            run(e, "gpsimd")

        @block.sync
        def _(e):
            run(e, "sync")


class Carver:
    def __init__(self, arena, n):
        self.a = arena
        self.n = n
        self.pos = 0
        self.marks = []

    def f32(self, nelem, parts=128):
        off = self.pos
        self.pos += nelem
        assert self.pos <= self.n, ("sbuf arena overflow", self.pos, self.n)
        return self.a[0:parts, off:off + nelem]

    def bf(self, nelem, parts=128):
        n32 = (nelem + 1) // 2
        v = self.f32(n32, parts)
        return v.bitcast(BF16)[:, 0:nelem]

    def mark(self):
        self.marks.append(self.pos)

    def release(self):
        self.pos = self.marks.pop()


def build(stop_after=None):
    nc = bass.Bass("TRN2", target_bir_lowering=False)
    di = lambda n, s, dt=F32: nc.dram_tensor(n, list(s), dt, kind="ExternalInput").ap()
    do = lambda n, s, dt=F32: nc.dram_tensor(n, list(s), dt, kind="ExternalOutput").ap()
    dsx = lambda n, s, dt=F32: nc.dram_tensor(n, list(s), dt, kind="Internal").ap()
    xa_d = di("xa", [TT, D]); xm_d = di("xm", [MYT, D]); crow_d = di("crow", [17, D])
    w_ada_d = di("w_ada", [D, 6 * D]); b_ada_d = di("b_ada", [6 * D])
    w_in_d = di("w_in", [D, INC]); w_out_d = di("w_out", [D, D])
    w_gu_d = di("w_gu", [D, 2 * DFF]); w_down_d = di("w_down", [DFF, D])
    mbias_d = di("mbias", [4, 2]); mnorm_d = di("m_norm_g", [1024]); gnorm_d = di("g_norm_g", [1024])
    convw_d = di("convwT", [3072, 4]); gpar_d = di("gpar", [8, 2])
    ln_d = di("lnp", [4, D])
    sC_d = di("sC", [NSEQ, 4, 256, 256]); sn_d = di("sn", [NSEQ, 4, 256]); sm_d = di("smT", [4, NSEQ])
    sS_d = di("sS", [NSEQ, 8, 128, 128]); sconv_d = di("sconv", [48, 3072])
    cst_d = di("cst", [128, 1024]); sel_d = di("sel", [128, 256])
    ym_o = do("ym", [MYT, D])
    pC_o = do("pC", [4, 256, 256]); pn_o = do("pn", [4, 256]); pm_o = do("pm", [4, 1])
    pS_o = do("pS", [8, 128, 128]); pconv_o = do("pconv", [3, 3072])
    oC_o = do("oC", [NSEQ, 4, 256, 256]); on_o = do("on", [NSEQ, 4, 256]); om_o = do("om", [4, NSEQ])
    oS_o = do("oS", [NSEQ, 8, 128, 128]); oconv_o = do("oconv", [48, 3072])
    ada_d = dsx("ada_s", [17, 6 * D]); x1_d = dsx("x1_s", [MYT, D]); y_d = dsx("y_s", [MYT, D])

    NA = 53200
    from contextlib import ExitStack
    with ExitStack() as es:
        arena = es.enter_context(nc.sbuf_tensor("arena", [128, NA], F32))
        ps = [es.enter_context(nc.psum_tensor(f"ps{i}", [128, 512], F32)) for i in range(8)]
        esems = {e: es.enter_context(nc.semaphore(f"s_{e}")) for e in KB.ENG}
        dsems = [es.enter_context(nc.semaphore(f"d_{i}")) for i in range(int(os.environ.get("KB_NDMA", "24")))]
        block = es.enter_context(nc.Block())
        kb = KB(nc, esems, dsems)
        coarse_set = set(os.environ.get("KB_COARSE", "").split(","))
        pe_set = set(os.environ.get("KB_COARSE_PE", "").split(","))

        def _phase(nm):
            kb.coarse = nm in coarse_set
            kb.coarse_pe = nm in pe_set
        kb.phase = _phase
        kb.phase("p0")
        cv = Carver(arena, NA)
        P = [p[:, :] for p in ps]

        cst = cv.f32(1024)
        kb.dma(cst, cst_d[:, :])
        ident = cst[:, 0:128]
        capT_p = cst[:, 128:256]; capE_p = cst[:, 256:384]; capT_s = cst[:, 384:512]; capE_s = cst[:, 512:640]
        ones_f = cst[:, 640:768]
        selc = cv.f32(256)
        kb.dma(selc, sel_d[:, :])
        identb = cv.bf(128)
        kb.cp("vector", identb, ident)
        selb = cv.bf(256)
        kb.cp("vector", selb, selc)
        lnp = None

        mixT_off = cv.pos
        mixT = cv.bf(16 * MYT).rearrange("p (c t) -> p c t", c=16)

        xbuf = [arena[:, mixT_off + i_ * D:mixT_off + (i_ + 1) * D] for i_ in range(4)]
        for tl_ in range(4):
            kb.dma(xbuf[tl_], xa_d[tl_ * 128:(tl_ + 1) * 128, :])
        cv.mark()
        c_sb = cv.f32(D)
        kb.dma(c_sb[0:17, :], crow_d[:, :])
        kb.act(c_sb[0:17, :], c_sb[0:17, :], AF.Silu)
        for kc in range(16):
            kb.tr(P[0][:, kc * 17:(kc + 1) * 17], c_sb[0:17, kc * 128:(kc + 1) * 128], ident[0:17, 0:17])
        siluT = cv.bf(272)
        kb.cp("vector", siluT, P[0][:, 0:272])
        wab = [cv.bf(16 * 512).rearrange("p (k n) -> p k n", k=16) for _ in range(4)]
        bbb = [cv.f32(512) for _ in range(4)]
        aob = [cv.f32(512) for _ in range(4)]
        for cb in range(24):
            wb = wab[cb % 4]; bb = bbb[cb % 4]; ao = aob[cb % 4]
            kb.dma(wb, w_ada_d[:, cb * 512:(cb + 1) * 512].rearrange("(k p) n -> p k n", p=128), q="gpsimd")
            kb.dma(bb[0:17, :], b_ada_d[cb * 512:(cb + 1) * 512].partition_broadcast(17))
            pp = P[1 + cb % 2]
            for kc in range(16):
                kb.mm(pp[0:17, :], siluT[:, kc * 17:(kc + 1) * 17], wb[:, kc, :], start=(kc == 0), stop=(kc == 15))
            kb.tt("vector", ao[0:17, :], pp[0:17, :], bb[0:17, :], ALU.add)
            kb.dma(ada_d[:, cb * 512:(cb + 1) * 512], ao[0:17, :])
        cv.release()

        def load_mod(dst, col0, prompt, plus1):
            if prompt:
                kb.dma(dst, ada_d[0, col0:col0 + D].partition_broadcast(128))
            else:
                kb.memset("gpsimd", dst, 0.0)
                for t in range(4):
                    kb.dma(dst[t * 16:(t + 1) * 16, :], ada_d[1:17, col0:col0 + D])
            if plus1:
                kb.ts("gpsimd", dst, dst, 1.0, None, ALU.add)

        kb.phase("p1a")
        hT_off = cv.pos
        hT = cv.bf(16 * TT).rearrange("p (c t) -> p c t", c=16)
        cv.mark()
        modA = cv.f32(D); modB = cv.f32(D)
        load_mod(modA, 2048, True, True)
        load_mod(modB, 0, True, False)
        for tl in range(NT):
            if tl == 16:
                load_mod(modA, 2048, False, True)
                load_mod(modB, 0, False, False)
            xs = xbuf[tl % 4]
            kb.tt("vector", xs, xs, modA, ALU.mult)
            kb.tt("vector", xs[:, 0:1024], xs[:, 0:1024], modB[:, 0:1024], ALU.add)
            kb.tt("gpsimd", xs[:, 1024:2048], xs[:, 1024:2048], modB[:, 1024:2048], ALU.add)
            for g in range(4):
                pp = P[(tl * 4 + g) % 4]
                for c in range(4):
                    kb.tr(pp[:, c * 128:(c + 1) * 128], xs[:, (g * 4 + c) * 128:(g * 4 + c + 1) * 128], ident)
                kb.cp("scalar" if g % 2 else "vector", hT[:, g * 4:(g + 1) * 4, tl * 128:(tl + 1) * 128],
                      pp.rearrange("p (c t) -> p c t", c=4))
            if tl + 4 < NT:
                kb.dma(xbuf[tl % 4], xa_d[(tl + 4) * 128:(tl + 5) * 128, :])
        cv.release()
        if stop_after == "hT":
            dbg = do("dbg", [128, 16 * TT], BF16)
            kb.dma(dbg.rearrange("p (c t) -> p c t", c=16), hT)
            kb.emit(block)
            return nc

        kb.phase("gates")
        scan_phase(nc, kb, cv, P, locals())
        cv.pos = hT_off
        kb.phase("p2")

        phase2(nc, kb, cv, P, locals())
        kb.emit(block)
    return nc


def _consts():
    c = np.zeros((128, 1024), np.float32)
    c[:, 0:128] = np.eye(128, dtype=np.float32)
    idx = np.arange(128)
    s = idx[:, None]; l = idx[None, :]
    c[:, 128:256] = np.where(l >= s, 0.0, NEG)
    c[:, 256:384] = np.where(idx[:, None] > idx[None, :], 0.0, -NEG)
    tt_ = idx // 16; bb_ = idx % 16
    same = (bb_[:, None] == bb_[None, :]) & (idx[:, None] < 64) & (idx[None, :] < 64)
    selfp = (idx[:, None] == idx[None, :])
    okT = (same & (tt_[None, :] >= tt_[:, None])) | selfp
    c[:, 384:512] = np.where(okT, 0.0, NEG)
    okE = same & (tt_[:, None] > tt_[None, :])
    c[:, 512:640] = np.where(okE, 0.0, -NEG)
    c[:, 640:768] = 1.0
    return c


def make_in_maps(inp):
    f = lambda a: np.ascontiguousarray(np.asarray(a, dtype=np.float32))
    xp = f(inp["x_prompt"]); xs = f(inp["x_sample"])
    cst = _consts()
    selh = np.zeros((8, 1024 + 2048), np.float32)
    for h in range(8):
        selh[h, h * 128:(h + 1) * 128] = 1.0
    selh[:, 1024:] = 1.0
    selh[:, 1024::128] = 0.0
    maskP = np.zeros((128, 16), np.float32)
    for p_ in range(64):
        maskP[p_, p_ % 16] = 1.0
    maps = []
    for c in range(8):
        p, j = c // 2, c % 2
        sq = slice(c * NSEQ, (c + 1) * NSEQ)
        xs_tb = xs[sq].transpose(1, 0, 2).reshape(64, D)
        xa = np.zeros((TT, D), np.float32)
        xa[0:2048] = xp[p]; xa[2048:2112] = xs_tb
        xm = np.concatenate([xp[p, j * 1024:(j + 1) * 1024], xs_tb], 0)
        crow = np.concatenate([f(inp["c_prompt"])[p:p + 1], f(inp["c_sample"])[sq]], 0)
        sel = np.zeros((128, 256), np.float32)
        sel[:, 0:128] = np.eye(128) * (1.0 if j == 0 else 0.0)
        sel[:, 128:256] = np.eye(128) * (1.0 if j == 1 else 0.0)
        m = {
            "xa": xa, "xm": f(xm), "crow": f(crow),
            "w_ada": f(inp["w_ada"]), "b_ada": f(inp["b_ada"]), "w_in": f(inp["w_in"]),
            "w_out": f(inp["w_out"]), "w_gu": f(inp["w_gu"]), "w_down": f(inp["w_down"]),
            "mbias": f(np.stack([inp["m_i_bias"], inp["m_f_bias"]], 1)),
            "m_norm_g": f(inp["m_norm_g"]), "g_norm_g": f(inp["g_norm_g"]),
            "convwT": f(np.asarray(inp["conv_w"]).T),
            "gpar": f(np.stack([inp["g_dt_bias"], inp["g_A_log"]], 1)),
            "lnp": f(np.stack([inp["ln1_g"], inp["ln1_b"], inp["ln2_g"], inp["ln2_b"]], 0)),
            "sC": f(inp["state_mlstm_C"])[sq], "sn": f(inp["state_mlstm_n"])[sq],
            "smT": f(np.asarray(inp["state_mlstm_m"])[sq].T),
            "sS": f(inp["state_gdn_S"])[sq],
            "sconv": f(np.asarray(inp["state_gdn_conv"])[sq].transpose(1, 0, 2).reshape(48, 3072)),
            "cst": cst, "sel": sel, "selh": selh, "maskP": maskP,
        }
        maps.append(m)
    return maps


def layer_norm_rows(kb, z, junk, st, np_, gam, bet, out):
    inv = 1.0 / D
    zz = z[0:np_, :]
    kb.op("vector", lambda e: e.tensor_reduce(out=st[0:np_, 0:1], in_=zz, axis=AX.X, op=ALU.add),
          reads=[zz], writes=[st[0:np_, 0:1]])
    kb.act(junk[0:np_, :], zz, AF.Square, accum_out=st[0:np_, 1:2])
    kb.ts("vector", st[0:np_, 2:3], st[0:np_, 0:1], inv, None, ALU.mult)
    kb.stt("vector", st[0:np_, 3:4], st[0:np_, 2:3], -1.0, st[0:np_, 2:3], ALU.mult, ALU.mult)
    kb.stt("vector", st[0:np_, 4:5], st[0:np_, 1:2], inv, st[0:np_, 3:4], ALU.mult, ALU.add)
    kb.rsqrt(st[0:np_, 5:6], st[0:np_, 4:5], 1e-5)
    kb.ts("vector", zz, zz, st[0:np_, 2:3], st[0:np_, 5:6], ALU.subtract, ALU.mult)
    kb.tt("gpsimd", zz, zz, gam[0:np_, :], ALU.mult)
    kb.tt("vector", out[0:np_, :], zz, bet[0:np_, :], ALU.add)


def phase2(nc, kb, cv, P, L):
    mixT = L["mixT"]; ident = L["ident"]; load_mod = L["load_mod"]
    xm_d = L["xm_d"]; ln_d = L["ln_d"]; x1_d = L["x1_d"]; y_d = L["y_d"]; ym_o = L["ym_o"]
    w_out_d = L["w_out_d"]; w_gu_d = L["w_gu_d"]; w_down_d = L["w_down_d"]
    tiles = [(i * 128, 128) for i in range(8)] + [(1024, 64)]
    cv.mark()
    h2T = cv.bf(16 * MYT).rearrange("p (c t) -> p c t", c=16)
    cv.mark()
    wo = cv.bf(16 * D).rearrange("p (k n) -> p k n", k=16)
    for q4 in range(4):
        kb.dma(wo[:, :, q4 * 512:(q4 + 1) * 512],
               w_out_d[:, q4 * 512:(q4 + 1) * 512].rearrange("(k p) n -> p k n", p=128), q="gpsimd")
    G1 = cv.f32(D); LG = cv.f32(D); LB = cv.f32(D); SC2 = cv.f32(D); SH2 = cv.f32(D)
    xt = [cv.f32(D)] * 2; zt = [cv.f32(D) for _ in range(2)]
    st8 = cv.f32(16)
    load_mod(G1, 2 * D, True, True); load_mod(SC2, 4 * D, True, True); load_mod(SH2, 3 * D, True, False)
    kb.dma(LG, ln_d[0, :].partition_broadcast(128)); kb.dma(LB, ln_d[1, :].partition_broadcast(128))
    def ln_stats(z, junk, st_, np_):
        inv = 1.0 / D
        zz = z[0:np_, :]
        kb.op("vector", lambda e: e.tensor_reduce(out=st_[0:np_, 0:1], in_=zz, axis=AX.X, op=ALU.add),
              reads=[zz], writes=[st_[0:np_, 0:1]])
        kb.act(junk[0:np_, :], zz, AF.Square, accum_out=st_[0:np_, 1:2])
        yield
        kb.ts("vector", st_[0:np_, 2:3], st_[0:np_, 0:1], inv, None, ALU.mult)
        yield
        kb.stt("vector", st_[0:np_, 3:4], st_[0:np_, 2:3], -1.0, st_[0:np_, 2:3], ALU.mult, ALU.mult)
        yield
        kb.stt("vector", st_[0:np_, 4:5], st_[0:np_, 1:2], inv, st_[0:np_, 3:4], ALU.mult, ALU.add)
        yield
        kb.act(st_[0:np_, 5:6], st_[0:np_, 4:5], AF.Ln, bias=1e-5)
        yield
        kb.act(st_[0:np_, 5:6], st_[0:np_, 5:6], AF.Exp, scale=-0.5)
        yield
        kb.ts("vector", zz, zz, st_[0:np_, 2:3], st_[0:np_, 5:6], ALU.subtract, ALU.mult)
        yield

    def tile2a(i, t0, np_):
        x = xt[0]; z = zt[i % 2]; st_ = st8[:, (i % 2) * 8:(i % 2) * 8 + 8]
        kb.dma(x[0:np_, :], xm_d[t0:t0 + np_, :])
        for cb in range(4):
            for kc in range(16):
                kb.mm(P[cb][0:np_, :], mixT[:, kc, t0:t0 + np_], wo[:, kc, cb * 512:(cb + 1) * 512],
                      start=(kc == 0), stop=(kc == 15))
        yield
        for cb in range(4):
            kb.tt("vector", z[0:np_, cb * 512:(cb + 1) * 512], P[cb][0:np_, :], G1[0:np_, cb * 512:(cb + 1) * 512], ALU.mult)
        yield
        kb.stt("vector", z[0:np_, :], x[0:np_, :], ALPHA, z[0:np_, :], ALU.mult, ALU.add)
        yield
        for _ in ln_stats(z, x, st_, np_):
            yield
        kb.tt("vector", z[0:np_, 0:1024], z[0:np_, 0:1024], LG[0:np_, 0:1024], ALU.mult)
        kb.tt("gpsimd", z[0:np_, 1024:2048], z[0:np_, 1024:2048], LG[0:np_, 1024:2048], ALU.mult)
        yield
        kb.tt("vector", z[0:np_, 0:1280], z[0:np_, 0:1280], LB[0:np_, 0:1280], ALU.add)
        kb.tt("gpsimd", z[0:np_, 1280:2048], z[0:np_, 1280:2048], LB[0:np_, 1280:2048], ALU.add)
        yield
        kb.dma(x1_d[t0:t0 + np_, :], z[0:np_, :])
        yield
        kb.tt("vector", z[0:np_, 0:1280], z[0:np_, 0:1280], SC2[0:np_, 0:1280], ALU.mult)
        kb.tt("gpsimd", z[0:np_, 1280:2048], z[0:np_, 1280:2048], SC2[0:np_, 1280:2048], ALU.mult)
        yield
        kb.tt("vector", z[0:np_, 0:1024], z[0:np_, 0:1024], SH2[0:np_, 0:1024], ALU.add)
        kb.tt("gpsimd", z[0:np_, 1024:2048], z[0:np_, 1024:2048], SH2[0:np_, 1024:2048], ALU.add)
        yield
        for g in range(4):
            pp = P[4 + g]
            for c in range(4):
                kb.tr(pp[:, c * 128:c * 128 + np_], z[0:np_, (g * 4 + c) * 128:(g * 4 + c + 1) * 128], ident[0:np_, 0:np_])
        yield
        for g in range(4):
            pp = P[4 + g]
            kb.cp("scalar" if g % 2 else "vector", h2T[:, g * 4:(g + 1) * 4, t0:t0 + np_],
                  pp.rearrange("p (c t) -> p c t", c=4)[:, :, 0:np_])

    def run_window(genfns, W, S=1):
        active = []; nxt = 0; last_steps = [S]
        while nxt < len(genfns) or active:
            if nxt < len(genfns) and len(active) < W and last_steps[0] >= S:
                active.append(genfns[nxt]()); nxt += 1; last_steps[0] = 0
            last_steps[0] += 1
            for g_ in list(active):
                try:
                    next(g_)
                except StopIteration:
                    active.remove(g_)

    run_window([(lambda i=i, t0=t0, np_=np_: tile2a(i, t0, np_)) for i, (t0, np_) in enumerate(tiles[:8])], 2, 4)
    load_mod(G1, 2 * D, False, True); load_mod(SC2, 4 * D, False, True); load_mod(SH2, 3 * D, False, False)
    for _ in tile2a(8, 1024, 64):
        pass
    cv.release()
    actT = cv.bf(44 * MYT).rearrange("p (f t) -> p f t", f=44)
    cv.mark()
    wgb = [cv.bf(16 * 256).rearrange("p (k n) -> p k n", k=16) for _ in range(4)]
    tmp = [cv.f32(512) for _ in range(2)]
    tbs = [(0, 512), (512, 512), (1024, 64)]
    it = 0
    for f in range(44):
        wb = wgb[f % 4]
        kb.dma(wb[:, :, 0:128], w_gu_d[:, f * 128:(f + 1) * 128].rearrange("(k p) n -> p k n", p=128), q="gpsimd")
        kb.dma(wb[:, :, 128:256], w_gu_d[:, DFF + f * 128:DFF + (f + 1) * 128].rearrange("(k p) n -> p k n", p=128), q="gpsimd")
        for (t0, n) in tbs:
            pg = P[(2 * it) % 8]; pu = P[(2 * it + 1) % 8]; tm = tmp[it % 2]; it += 1
            for kc in range(16):
                kb.mm(pg[:, 0:n], wb[:, kc, 0:128], h2T[:, kc, t0:t0 + n], start=(kc == 0), stop=(kc == 15))
            for kc in range(16):
                kb.mm(pu[:, 0:n], wb[:, kc, 128:256], h2T[:, kc, t0:t0 + n], start=(kc == 0), stop=(kc == 15))
            kb.act(tm[:, 0:n], pg[:, 0:n], AF.Silu)
            kb.tt("vector", actT[:, f, t0:t0 + n], tm[:, 0:n], pu[:, 0:n], ALU.mult)
    cv.release()
    cv.mark()
    save = cv.pos
    cv.pos = L["mixT_off"]
    wd0 = cv.bf(44 * 256).rearrange("p (f n) -> p f n", f=44)
    yst = [cv.f32(256) for _ in range(4)]
    cv.pos = save
    wdb = [wd0, cv.bf(44 * 256).rearrange("p (f n) -> p f n", f=44)]
    ysigs = []
    it = 0
    for cb in range(8):
        wb = wdb[cb % 2]
        kb.dma(wb, w_down_d[:, cb * 256:(cb + 1) * 256].rearrange("(f p) n -> p f n", p=128), q="gpsimd")
        for (t0, np_) in tiles:
            pp = P[it % 8]; ys = yst[it % 4]; it += 1
            for f in range(44):
                kb.mm(pp[0:np_, 0:256], actT[:, f, t0:t0 + np_], wb[:, f, :], start=(f == 0), stop=(f == 43))
            kb.cp("scalar" if it % 2 else "vector", ys[0:np_, :], pp[0:np_, 0:256])
            ysigs.append(kb.dma(y_d[t0:t0 + np_, cb * 256:(cb + 1) * 256], ys[0:np_, :], wr=False))
    cv.release()
    cv.release()
    cv.mark()
    G2 = cv.f32(D); LG2 = cv.f32(D); LB2 = cv.f32(D)
    xt = [cv.f32(D) for _ in range(2)]; zt = [cv.f32(D) for _ in range(2)]
    st8 = cv.f32(16)
    load_mod(G2, 5 * D, True, True)
    kb.dma(LG2, ln_d[2, :].partition_broadcast(128)); kb.dma(LB2, ln_d[3, :].partition_broadcast(128))
    def tileln2(i, t0, np_):
        x = xt[i % 2]; z = zt[i % 2]; st_ = st8[:, (i % 2) * 8:(i % 2) * 8 + 8]
        kb.dma(z[0:np_, :], y_d[t0:t0 + np_, :], rd=False, after=ysigs)
        kb.dma(x[0:np_, :], x1_d[t0:t0 + np_, :])
        yield
        kb.tt("vector", z[0:np_, 0:1280], z[0:np_, 0:1280], G2[0:np_, 0:1280], ALU.mult)
        kb.tt("gpsimd", z[0:np_, 1280:2048], z[0:np_, 1280:2048], G2[0:np_, 1280:2048], ALU.mult)
        yield
        kb.stt("vector", z[0:np_, :], x[0:np_, :], ALPHA, z[0:np_, :], ALU.mult, ALU.add)
        yield
        for _ in ln_stats(z, x, st_, np_):
            yield
        kb.tt("vector", z[0:np_, 0:1024], z[0:np_, 0:1024], LG2[0:np_, 0:1024], ALU.mult)
        kb.tt("gpsimd", z[0:np_, 1024:2048], z[0:np_, 1024:2048], LG2[0:np_, 1024:2048], ALU.mult)
        yield
        kb.tt("vector", z[0:np_, 0:1280], z[0:np_, 0:1280], LB2[0:np_, 0:1280], ALU.add)
        kb.tt("gpsimd", z[0:np_, 1280:2048], z[0:np_, 1280:2048], LB2[0:np_, 1280:2048], ALU.add)
        yield
        kb.dma(ym_o[t0:t0 + np_, :], z[0:np_, :])

    run_window([(lambda i=i, t0=t0, np_=np_: tileln2(i, t0, np_)) for i, (t0, np_) in enumerate(tiles[:8])], 2)
    load_mod(G2, 5 * D, False, True)
    for _ in tileln2(8, 1024, 64):
        pass
    cv.release()


def bc3(ap, shape, axis):
    return ap.unsqueeze(axis).to_broadcast(list(shape))


def proj_fm(kb, P, pit, wb, hT, dst_fn, evac):
    for (t0, n) in [(0, 512), (512, 512), (1024, 512), (1536, 512), (2048, 128)]:
        pp = P[pit[0] % 2]; pit[0] += 1
        for kc in range(16):
            kb.mm(pp[:, 0:n], wb[:, kc, :], hT[:, kc, t0:t0 + n], start=(kc == 0), stop=(kc == 15))
        evac(pp[:, 0:n], t0, n)


def scan_phase(nc, kb, cv, P, L):
    hT = L["hT"]; ident = L["ident"]; identb = L["identb"]; selb = L["selb"]; mixT = L["mixT"]
    w_in_d = L["w_in_d"]; cst = L["cst"]
    capT_p = L["capT_p"]; capE_p = L["capE_p"]; capT_s = L["capT_s"]; capE_s = L["capE_s"]
    di = lambda n, s, dt=F32: nc.dram_tensor(n, list(s), dt, kind="ExternalInput").ap()
    selh_d = di("selh", [8, 1024 + 2048]); maskP_d = di("maskP", [128, 16])
    cv.mark()
    selh = cv.f32(1024, parts=8)
    kb.dma(selh, selh_d[:, 0:1024])
    maskB = None
    maskP = cv.f32(16); kb.dma(maskP, maskP_d[:, :])
    maskPb = cv.bf(16); kb.cp("vector", maskPb, maskP)
    MR = cv.f32(TT, parts=8); BR = cv.f32(TT, parts=8)
    NQ = 48
    cols = cv.f32(NT * NQ).rearrange("p (t q) -> p t q", t=NT)
    dcol = cv.f32(NT * 48).rearrange("p (t q) -> p t q", t=NT)
    decB = cv.f32(64); ebLB = cv.f32(128 + 128)
    mb = cv.f32(2, parts=8); gp = cv.f32(4, parts=8)
    kb.dma(mb[0:4, :], L["mbias_d"][:, :]); kb.dma(gp[:, 0:2], L["gpar_d"][:, :])
    wbuf = [cv.bf(16 * 256).rearrange("p (k n) -> p k n", k=16) for _ in range(2)]
    wit = [0]

    def loadw(c0, n):
        wb = wbuf[wit[0] % 2]; wit[0] += 1
        kb.dma(wb[:, :, 0:n], w_in_d[:, c0:c0 + n].rearrange("(k p) n -> p k n", p=128), q="gpsimd")
        return wb

    pit = [0]
    cv.mark()
    Ri = cv.f32(TT, parts=8); Rf = cv.f32(TT, parts=8); Rb = cv.f32(TT, parts=8); Ra = cv.f32(TT, parts=8)
    Rt = cv.f32(TT, parts=8); m0r = cv.f32(128, parts=8)
    flag = cv.f32(2048, parts=8); kb.dma(flag, selh_d[:, 1024:3072])
    wg = loadw(8192, 24)
    for (dst, c0, n) in ((Ri, 0, 4), (Rf, 4, 4), (Rb, 8, 8), (Ra, 16, 8)):
        def ev(pp, t0, nn, dst=dst, n=n):
            kb.cp("vector", dst[0:n, t0:t0 + nn], pp[0:n, :])
        for (t0, nn) in [(0, 512), (512, 512), (1024, 512), (1536, 512), (2048, 128)]:
            pp = P[pit[0] % 2]; pit[0] += 1
            for kc in range(16):
                kb.mm(pp[0:n, 0:nn], wg[:, kc, c0:c0 + n], hT[:, kc, t0:t0 + nn], start=(kc == 0), stop=(kc == 15))
            ev(pp[:, 0:nn], t0, nn)
    S0 = 2048
    kb.ts("vector", Ri[0:4, :], Ri[0:4, :], mb[0:4, 0:1], None, ALU.add)
    kb.ts("vector", Rf[0:4, :], Rf[0:4, :], mb[0:4, 1:2], None, ALU.add)
    kb.act(Rf[0:4, :], Rf[0:4, :], AF.Exp, scale=-1.0)
    kb.act(Rf[0:4, :], Rf[0:4, :], AF.Ln, bias=1.0)
    kb.ts("vector", Rf[0:4, :], Rf[0:4, :], -1.0, None, ALU.mult)
    Bm = Rt
    ones_bc = cst[0:4, 640:641].to_broadcast([4, 2048])
    kb.scan(Bm[0:4, 0:2048], ones_bc, Rf[0:4, 0:2048], 0.0, ALU.mult, ALU.add)
    kb.cp("vector", Bm[0:4, S0:S0 + 16], Rf[0:4, S0:S0 + 16])
    for t in range(1, 4):
        kb.tt("vector", Bm[0:4, S0 + 16 * t:S0 + 16 * t + 16], Bm[0:4, S0 + 16 * t - 16:S0 + 16 * t], Rf[0:4, S0 + 16 * t:S0 + 16 * t + 16], ALU.add)
    kb.cp("vector", Bm[0:4, S0 + 64:TT], Rf[0:4, S0 + 64:TT])
    Am = Ri
    kb.tt("vector", Am[0:4, :], Ri[0:4, :], Bm[0:4, :], ALU.subtract)
    kb.scan(MR[0:4, 0:2048], Am[0:4, 0:2048], Am[0:4, 0:2048], 0.0, ALU.max, ALU.max)
    kb.dma(m0r[0:4, 0:16], L["sm_d"][:, :])
    kb.memset("vector", m0r[0:4, 16:128], 0.0)
    for t in range(1, 4):
        kb.cp("vector", m0r[0:4, 16 * t:16 * t + 16], m0r[0:4, 0:16])
    kb.tt("vector", MR[0:4, S0:S0 + 16], Am[0:4, S0:S0 + 16], m0r[0:4, 0:16], ALU.max)
    for t in range(1, 4):
        kb.tt("vector", MR[0:4, S0 + 16 * t:S0 + 16 * t + 16], MR[0:4, S0 + 16 * t - 16:S0 + 16 * t], Am[0:4, S0 + 16 * t:S0 + 16 * t + 16], ALU.max)
    kb.cp("vector", MR[0:4, S0 + 64:TT], Am[0:4, S0 + 64:TT])
    Wf = Rf
    kb.ts("vector", Wf[0:4, 0:2048], Am[0:4, 0:2048], MR[0:4, 2047:2048], None, ALU.subtract)
    for t in range(4):
        kb.tt("vector", Wf[0:4, S0 + 16 * t:S0 + 16 * t + 16], Am[0:4, S0 + 16 * t:S0 + 16 * t + 16], MR[0:4, S0 + 48:S0 + 64], ALU.subtract)
    kb.memset("vector", Wf[0:4, S0 + 64:TT], 0.0)
    mo = cv.f32(32, parts=8)
    kb.tt("vector", mo[0:4, 0:1], Bm[0:4, 2047:2048], MR[0:4, 2047:2048], ALU.add)
    kb.tt("vector", mo[0:4, 16:32], Bm[0:4, S0 + 48:S0 + 64], MR[0:4, S0 + 48:S0 + 64], ALU.add)
    kb.dma(L["pm_o"][:, :], mo[0:4, 0:1]); kb.dma(L["om_o"][:, :], mo[0:4, 16:32])
    dec = cv.f32(16, parts=8)
    kb.tt("vector", dec[0:4, :], m0r[0:4, 0:16], MR[0:4, S0 + 48:S0 + 64], ALU.subtract)
    kb.act(dec[0:4, :], dec[0:4, :], AF.Exp)
    kb.act(Ra[0:8, :], Ra[0:8, :], AF.Exp, bias=gp[0:8, 0:1])
    kb.act(Ra[0:8, :], Ra[0:8, :], AF.Ln, bias=1.0)
    kb.act(gp[0:8, 2:3], gp[0:8, 1:2], AF.Exp)
    kb.ts("vector", gp[0:8, 3:4], gp[0:8, 2:3], -1.0, None, ALU.mult)
    kb.ts("vector", Ra[0:8, :], Ra[0:8, :], gp[0:8, 3:4], None, ALU.mult)
    kb.scan(BR[0:8, 0:2048], flag[0:8, :], Ra[0:8, 0:2048], 0.0, ALU.mult, ALU.add)
    kb.cp("vector", BR[0:8, S0:S0 + 16], Ra[0:8, S0:S0 + 16])
    for t in range(1, 4):
        kb.tt("vector", BR[0:8, S0 + 16 * t:S0 + 16 * t + 16], BR[0:8, S0 + 16 * t - 16:S0 + 16 * t], Ra[0:8, S0 + 16 * t:S0 + 16 * t + 16], ALU.add)
    kb.memset("vector", BR[0:8, S0 + 64:TT], 0.0)
    Dl = Ra
    kb.tt("vector", Dl[0:8, 0:2048].rearrange("p (c l) -> p c l", c=16),
          bc3(BR[0:8, 127:2048:128], [8, 16, 128], 2), BR[0:8, 0:2048].rearrange("p (c l) -> p c l", c=16), ALU.subtract)
    for t in range(4):
        kb.tt("vector", Dl[0:8, S0 + 16 * t:S0 + 16 * t + 16], BR[0:8, S0 + 48:S0 + 64], BR[0:8, S0 + 16 * t:S0 + 16 * t + 16], ALU.subtract)
    kb.memset("vector", Dl[0:8, S0 + 64:TT], 0.0)
    ebl = cv.f32(32, parts=8)
    kb.act(ebl[0:8, 0:16], BR[0:8, 127:2048:128], AF.Exp)
    kb.act(ebl[0:8, 16:32], BR[0:8, S0 + 48:S0 + 64], AF.Exp)
    for h in range(8):
        kb.mm(P[2][:, h * 32:(h + 1) * 32], selh[0:8, h * 128:(h + 1) * 128], ebl[0:8, 0:32])
    kb.cp("vector", ebLB, P[2][:, 0:256])
    for h in range(4):
        kb.mm(P[3][:, h * 16:(h + 1) * 16], selh[0:4, h * 128:(h + 1) * 128], dec[0:4, 0:16])
    kb.cp("vector", decB, P[3][:, 0:64])
    for tl in range(NT):
        pp = P[4 + tl % 4]
        sl = slice(tl * 128, (tl + 1) * 128)
        for qi, (R_, n) in enumerate(((Am, 4), (MR, 4), (Bm, 4), (Wf, 4))):
            kb.tr(pp[:, qi * 4:qi * 4 + 4], R_[0:4, sl], ident[0:4, 0:4])
        for qi, R_ in enumerate((BR, Rb, Dl)):
            kb.tr(pp[:, 16 + qi * 8:24 + qi * 8], R_[0:8, sl], ident[0:8, 0:8])
        if tl == 16:
            kb.tr(pp[:, 40:44], m0r[0:4, 0:128], ident[0:4, 0:4])
            kb.cp("vector", cols[:, tl, 0:44], pp[:, 0:44])
        else:
            kb.cp("vector" if tl % 2 else "scalar", cols[:, tl, 0:40], pp[:, 0:40])
    cv.release()
    kb.tt("vector", dcol[:, :, 0:4], cols[:, :, 4:8], cols[:, :, 8:12], ALU.add)
    kb.act(dcol[:, :, 0:4], dcol[:, :, 0:4], AF.Exp, scale=-1.0)
    kb.act(dcol[:, :, 4:8], cols[:, :, 12:16], AF.Exp)
    kb.act(dcol[:, :, 8:16], cols[:, :, 24:32], AF.Sigmoid)
    kb.act(dcol[:, :, 16:24], cols[:, :, 16:24], AF.Exp)
    kb.tt("vector", dcol[:, :, 16:24], dcol[:, :, 16:24], dcol[:, :, 8:16], ALU.mult)
    kb.ts("vector", dcol[:, :, 24:32], dcol[:, :, 8:16], -1.0, None, ALU.mult)
    kb.act(dcol[:, :, 32:40], cols[:, :, 32:40], AF.Exp)
    kb.ts("vector", dcol[:, :, 40:48], cols[:, :, 16:24], -1.0, None, ALU.mult)
    iw = cv.f32(4)
    kb.tt("vector", iw, cols[:, 16, 40:44], cols[:, 16, 4:8], ALU.subtract)
    kb.act(iw, iw, AF.Exp)
    G = dict(cols=cols, dcol=dcol, decB=decB, ebLB=ebLB, iw=iw, MR=MR, BR=BR, selh=selh, maskB=maskB,
             maskP=maskP, maskPb=maskPb, loadw=loadw, pit=pit, wbuf=wbuf)
    kb.phase("mlstm")
    mlstm_heads(nc, kb, cv, P, L, G)
    kb.phase("gdn")
    gdn_heads(nc, kb, cv, P, L, G)
    cv.release()


def sel_store(kb, P, pidx, mixT, src_bf, ch, tl, identb, selb):
    pp = P[pidx]
    if tl == 16:
        kb.mm(pp[:, 0:128], src_bf, identb, start=True, stop=True)
        kb.cp("vector", mixT[:, ch, 1024:1088], pp[:, 0:64])
    elif tl < 8:
        kb.mm(pp[:, 0:128], src_bf, selb[:, 0:128], start=True, stop=True)
        kb.cp("vector", mixT[:, ch, tl * 128:(tl + 1) * 128], pp[:, 0:128])
    else:
        i2 = tl - 8
        kb.mm(pp[:, 0:128], src_bf, selb[:, 128:256], start=True, stop=True)
        kb.tt("vector", mixT[:, ch, i2 * 128:(i2 + 1) * 128], pp[:, 0:128], mixT[:, ch, i2 * 128:(i2 + 1) * 128], ALU.add)


def mlstm_heads(nc, kb, cv, P, L, G):
    hT = L["hT"]; ident = L["ident"]; identb = L["identb"]; selb = L["selb"]; mixT = L["mixT"]
    capT_p = L["capT_p"]; capT_s = L["capT_s"]
    cols = G["cols"]; dcol = G["dcol"]; decB = G["decB"]; iw = G["iw"]; MR = G["MR"]; selh = G["selh"]
    maskP = G["maskP"]; loadw = G["loadw"]; pit = G["pit"]
    cv.mark()
    qT = cv.bf(2 * TT).rearrange("p (c t) -> p c t", c=2)
    kT = cv.bf(2 * TT).rearrange("p (c t) -> p c t", c=2)
    vx = cv.bf(NT * 258).rearrange("p (t v) -> p t v", t=NT)
    go = cv.bf(NT * 256).rearrange("p (t v) -> p t v", t=NT)
    negMB = cv.f32(512); Dt = cv.f32(512); tmpd = cv.f32(128)
    sw = [cv.bf(512) for _ in range(2)]
    hmix = [cv.bf(256) for _ in range(4)]
    junk = cv.f32(256); gnb = cv.f32(256); ec4 = cv.f32(64)
    kwt = [cv.bf(256) for _ in range(2)]
    kwm = [cv.bf(256) for _ in range(2)]
    dt2_off = cv.pos
    C0 = cv.f32(2 * 257).rearrange("p (c v) -> p c v", c=2)
    Dts = [Dt, L["arena"][:, dt2_off:dt2_off + 512]]
    C0b = cv.bf(2 * 258).rearrange("p (c v) -> p c v", c=2)
    Cout = cv.f32(2 * 257).rearrange("p (c v) -> p c v", c=2)
    stage = [cv.f32(257) for _ in range(2)]
    accS = cv.f32(257)
    n0all = cv.f32(32).rearrange("p (b c) -> p b c", c=2)
    noall = cv.f32(32).rearrange("p (b c) -> p b c", c=2)
    ar = L["arena"]
    nm_off = negMB.offset % L["NA"]
    st_off = stage[0].offset % L["NA"]
    C0_1 = ar[:, nm_off:nm_off + 514].rearrange("p (c v) -> p c v", c=2)
    C0b_1 = ar[:, nm_off + 514:nm_off + 514 + 258].bitcast(BF16).rearrange("p (c v) -> p c v", c=2)
    Cout_1 = ar[:, st_off:st_off + 514].rearrange("p (c v) -> p c v", c=2)
    CS = [(C0, C0b, Cout), (C0_1, C0b_1, Cout_1)]
    kb.memset("vector", vx[:, :, 256:258], 1.0)
    for h in range(4):
        kb.dma(gnb, L["mnorm_d"][h * 256:(h + 1) * 256].partition_broadcast(128))
        for (dst, c0, sc) in ((qT, h * 256, 1.0), (kT, 1024 + h * 256, 1.0 / 16.0)):
            wb = loadw(c0, 256)
            for cc in range(2):
                def ev(pp, t0, n, dst=dst, cc=cc, sc=sc):
                    kb.act(dst[:, cc, t0:t0 + n], pp, AF.Copy, scale=sc)
                proj_fm(kb, P, pit, wb[:, :, cc * 128:(cc + 1) * 128], hT, None, ev)
        wv = loadw(2048 + h * 256, 256)
        for tl in range(NT):
            pp = P[pit[0] % 2]; pit[0] += 1
            for kc in range(16):
                kb.mm(pp[:, 0:256], hT[:, kc, tl * 128:(tl + 1) * 128], wv[:, kc, 0:256], start=(kc == 0), stop=(kc == 15))
            kb.cp("vector", vx[:, tl, 0:256], pp[:, 0:256])
        wo_ = loadw(3072 + h * 256, 256)
        for tl in range(NT):
            pp = P[pit[0] % 2]; pit[0] += 1
            for kc in range(16):
                kb.mm(pp[:, 0:256], hT[:, kc, tl * 128:(tl + 1) * 128], wo_[:, kc, 0:256], start=(kc == 0), stop=(kc == 15))
            kb.act(junk, pp[:, 0:256], AF.Sigmoid)
            kb.tt("vector", go[:, tl, :], junk, gnb, ALU.mult)
        kb.memset("vector", accS, 0.0)
        for cc_ in range(2):
            kb.dma(n0all[:, :, cc_], L["sn_d"][:, h, cc_ * 128:(cc_ + 1) * 128].rearrange("b p -> p b"))
        def ldC(b_):
            C0_, _, _ = CS[b_ % 2]
            kb.dma(C0_[:, :, 0:256], L["sC_d"][b_, h].rearrange("(c p) v -> p c v", p=128))
            kb.cp("vector", C0_[:, :, 256:257], n0all[:, b_, :].unsqueeze(2))

        ldC(0)
        for b in range(NSEQ):
            C0, C0b, Cout = CS[b % 2]
            if b + 1 < NSEQ:
                ldC(b + 1)
            kb.cp("scalar", C0b[:, :, 0:257], C0)
            for cc in range(2):
                kb.mm(P[2][:, 0:257], qT[:, cc, 2048:2176], C0b[:, cc, 0:257], start=(cc == 0), stop=(cc == 1))
            kb.stt("vector", accS, P[2][:, 0:257], maskP[:, b:b + 1], accS, ALU.mult, ALU.add)
            for cc in range(2):
                kt = kwt[cc]
                if b == 0:
                    kb.tr(P[3].bitcast(BF16)[:, cc * 128:(cc + 1) * 128], kT[:, cc, 2048:2176], identb)
                    kb.ts("vector", kt[:, 0:128], P[3].bitcast(BF16)[:, cc * 128:(cc + 1) * 128], dcol[:, 16, 4 + h:5 + h], None, ALU.mult)
            km = kwm[b % 2]
            kb.ts("vector", km[:, 0:128], kwt[0][:, 0:128], maskP[:, b:b + 1], None, ALU.mult)
            kb.ts("vector", km[:, 128:256], kwt[1][:, 0:128], maskP[:, b:b + 1], None, ALU.mult)
            for cc in range(2):
                kb.mm(P[4 + cc][:, 0:257], km[:, cc * 128:(cc + 1) * 128], vx[:, 16, 0:257], start=True, stop=True)
                kb.stt("vector", Cout[:, cc, :], C0[:, cc, :], decB[:, h * 16 + b:h * 16 + b + 1], P[4 + cc][:, 0:257], ALU.mult, ALU.add)
            kb.dma(L["oC_o"][b, h].rearrange("(c p) v -> p c v", p=128), Cout[:, :, 0:256])
            kb.cp("vector", noall[:, b, :].unsqueeze(2), Cout[:, :, 256:257])
        for cc_ in range(2):
            kb.dma(L["on_o"][:, h, cc_ * 128:(cc_ + 1) * 128].rearrange("b p -> p b"), noall[:, :, cc_])
        blocks = [(0, 4), (4, 4), (8, 4), (12, 4), (16, 1)]
        for (tb0, ntl) in blocks:
            n = ntl * 128
            t0 = tb0 * 128
            kb.mm(P[0][:, 0:n], selh[0:4, h * 128:(h + 1) * 128], MR[0:4, t0:t0 + n], start=True, stop=True)
            kb.ts("vector", negMB[:, 0:n], P[0][:, 0:n], -1.0, None, ALU.mult)
            jlist = list(range(0, tb0 + ntl)) if tb0 < 16 else [16]

            def qk(j):
                jl = j - tb0
                c0 = max(jl, 0) * 128
                pq = P[6 + j % 2]
                for cc in range(2):
                    kb.mm(pq[:, c0:n], kT[:, cc, j * 128:(j + 1) * 128], qT[:, cc, t0 + c0:t0 + n], start=(cc == 0), stop=(cc == 1))

            qk(jlist[0])
            for ji, j in enumerate(jlist):
                jl = j - tb0
                c0 = max(jl, 0) * 128
                pq = P[6 + j % 2]
                Dt = Dts[j % 2]
                acol = cols[:, j, h:h + 1]
                if jl >= 0:
                    cap = capT_s if tb0 == 16 else capT_p
                    kb.stt("vector", tmpd, negMB[:, c0:c0 + 128], acol, cap, ALU.add, ALU.min)
                    kb.act(Dt[:, c0:c0 + 128], tmpd, AF.Exp)
                    if c0 + 128 < n:
                        kb.act(Dt[:, c0 + 128:n], negMB[:, c0 + 128:n], AF.Exp, bias=acol)
                else:
                    kb.act(Dt[:, c0:n], negMB[:, c0:n], AF.Exp, bias=acol)
                if ji + 1 < len(jlist):
                    qk(jlist[ji + 1])
                s_ = sw[j % 2]
                kb.tt("vector", s_[:, c0:n], pq[:, c0:n], Dt[:, c0:n], ALU.mult)
                for il in range(max(jl, 0), ntl):
                    i = tb0 + il
                    kb.mm(P[2 + il][:, 0:257], s_[:, il * 128:(il + 1) * 128], vx[:, j, 0:257],
                          start=(j == jlist[0]), stop=(j == i))

            def epi(il):
                i = tb0 + il
                acc = P[2 + il]
                e_ = ec4[:, il * 16:(il + 1) * 16]
                if i == 16:
                    kb.stt("vector", accS, accS, iw[:, h:h + 1], acc[:, 0:257], ALU.mult, ALU.add)
                    num = accS
                else:
                    num = acc
                kb.cp("vector", e_[:, 8:9], num[:, 256:257])
                kb.act(hmix[il], num[:, 0:256], AF.Square, accum_out=e_[:, 3:4])
                yield
                kb.stt("vector", e_[:, 0:1], e_[:, 8:9], -1.0, e_[:, 8:9], ALU.mult, ALU.max)
                yield
                kb.tt("vector", e_[:, 1:2], e_[:, 0:1], dcol[:, i, h:h + 1], ALU.max)
                yield
                kb.op("vector", lambda e, o=e_[:, 2:3], a=e_[:, 1:2]: e.reciprocal(out=o, in_=a), reads=[e_[:, 1:2]], writes=[e_[:, 2:3]])
                yield
                kb.tt("vector", e_[:, 4:5], e_[:, 2:3], e_[:, 2:3], ALU.mult)
                yield
                kb.stt("vector", e_[:, 5:6], e_[:, 3:4], 1.0 / 256.0, e_[:, 4:5], ALU.mult, ALU.mult)
                yield
                kb.act(e_[:, 6:7], e_[:, 5:6], AF.Ln, bias=1e-6)
                yield
                kb.act(e_[:, 6:7], e_[:, 6:7], AF.Exp, scale=-0.5)
                yield
                kb.tt("vector", e_[:, 7:8], e_[:, 6:7], e_[:, 2:3], ALU.mult)
                yield
                hm = hmix[il]
                kb.stt("vector", hm, num[:, 0:256], e_[:, 7:8], go[:, i, :], ALU.mult, ALU.mult)
                yield
                for cc in range(2):
                    sel_store(kb, P, cc, mixT, hm[:, cc * 128:(cc + 1) * 128], 2 * h + cc, i, identb, selb)

            gens = [epi(il) for il in range(ntl)]
            while gens:
                for g_ in list(gens):
                    try:
                        next(g_)
                    except StopIteration:
                        gens.remove(g_)
        for cc in range(2):
            for tl in range(16):
                kt = kwt[tl % 2]
                kb.tr(P[0].bitcast(BF16)[:, (tl % 2) * 128:(tl % 2) * 128 + 128], kT[:, cc, tl * 128:(tl + 1) * 128], identb)
                kb.ts("vector", kt[:, 0:128], P[0].bitcast(BF16)[:, (tl % 2) * 128:(tl % 2) * 128 + 128], dcol[:, tl, 4 + h:5 + h], None, ALU.mult)
                kb.mm(P[6][:, 0:257], kt[:, 0:128], vx[:, tl, 0:257], start=(tl == 0), stop=(tl == 15))
            sg = stage[cc]
            kb.cp("vector", sg, P[6][:, 0:257])
            kb.dma(L["pC_o"][h, cc * 128:(cc + 1) * 128, :], sg[:, 0:256])
            kb.dma(L["pn_o"][h, cc * 128:(cc + 1) * 128].rearrange("(p o) -> p o", o=1), sg[:, 256:257])
    cv.release()


def gdn_heads(nc, kb, cv, P, L, G):
    hT = L["hT"]; ident = L["ident"]; identb = L["identb"]; selb = L["selb"]; mixT = L["mixT"]
    capT_p = L["capT_p"]; capE_p = L["capE_p"]; capT_s = L["capT_s"]; capE_s = L["capE_s"]
    cols = G["cols"]; dcol = G["dcol"]; ebLB = G["ebLB"]; BR = G["BR"]; selh = G["selh"]
    maskP = G["maskP"]; loadw = G["loadw"]; pit = G["pit"]
    GW = 4096
    w_in_d = L["w_in_d"]
    wbuf = G["wbuf"]
    wsl = [wbuf[0][:, :, 0:128], wbuf[0][:, :, 128:256], wbuf[1][:, :, 0:128], wbuf[1][:, :, 128:256]]
    wq3 = wsl[0:3]
    NG = int(os.environ.get("KB_NG", "4"))
    cv.mark()
    qT = cv.f32(TT); kT = cv.f32(TT)
    vtok = cv.bf(NT * 128).rearrange("p (t v) -> p t v", t=NT)
    cw = cv.f32(4); gnb = cv.f32(128); junk = cv.f32(128); ec = cv.f32(8)
    Sst = cv.f32(128); U = cv.f32(128)
    hmx = [cv.bf(128) for _ in range(2)]
    base = cv.pos
    cinS = [cv.f32(3 + 512 + 128) for _ in range(2)]; caccS = [cv.f32(512 + 128) for _ in range(2)]
    vblkS = [cv.f32(512) for _ in range(2)]
    sqS = [cv.f32(512) for _ in range(2)]; tl_rows = cv.f32(512); c0T = cv.f32(48)
    cit = [0]
    cw3 = cv.f32(12); c0T3 = cv.f32(144)
    end1 = cv.pos
    cv.pos = base
    TA = [[cv.f32(128) for _ in range(7)] for _ in range(max(NG, 4))]
    TB0 = [[cv.f32(128) for _ in range(6)] for _ in range(max(NG, 4))]
    mr_off = G["MR"].offset % L["NA"]
    pool2 = [L["arena"][:, mr_off + i * 128:mr_off + (i + 1) * 128] for i in range(17)]
    TB = TB0
    S0b = [TA[1][i] for i in range(4)]
    Sout = [TA[2][i] for i in range(2)]
    cv.pos = max(cv.pos, end1)
    pool2 += [cv.f32(128) for _ in range(24 - 17)]
    TB1 = [pool2[i * 6:(i + 1) * 6] for i in range(4)]
    TBS = [TB0, TB1]
    for h in range(8):
        kb.phase("gdnconv")
        kb.dma(gnb, L["gnorm_d"][h * 128:(h + 1) * 128].partition_broadcast(128))
        def conv_block(qi, bi, t0, n, dstT0, cbase, wb, cw, c0T, slot, cprev):
            pp = P[slot % 2]
            cin = cinS[slot % 2]; cacc = caccS[slot % 2]; vblk = vblkS[slot % 2]; sq = sqS[slot % 2]
            psq = P[2 + slot % 2]
            for kc in range(16):
                kb.mm(pp[:, 0:n], wb[:, kc, 0:128], hT[:, kc, t0:t0 + n], start=(kc == 0), stop=(kc == 15))
            yield
            if bi == 0:
                kb.memset("vector", cin[:, 0:3], 0.0)
            elif bi < 4:
                kb.cp("vector", cin[:, 0:3], cprev[:, 512:515])
            if bi < 4:
                kb.cp("scalar", cin[:, 3:3 + n], pp[:, 0:n])
                yield
                kb.ts("vector", cacc[:, 0:n], cin[:, 0:n], cw[:, 0:1], None, ALU.mult)
                for j in range(1, 4):
                    kb.stt("vector", cacc[:, 0:n], cin[:, j:j + n], cw[:, j:j + 1], cacc[:, 0:n], ALU.mult, ALU.add)
                if bi == 3:
                    kb.tr(P[2][0:3, 128:256], cin[:, 512:515], ident)
                    kb.cp("vector", tl_rows[0:3, 128:256], P[2][0:3, 128:256])
                    kb.dma(L["pconv_o"][:, cbase - GW:cbase - GW + 128], tl_rows[0:3, 128:256])
            else:
                kb.cp("vector", cin[:, 0:48], c0T)
                kb.cp("scalar", cin[:, 48:48 + 128], pp[:, 0:128])
                yield
                kb.ts("vector", cacc[:, 0:128], cin[:, 0:128], cw[:, 0:1], None, ALU.mult)
                for j in range(1, 4):
                    kb.stt("vector", cacc[:, 0:128], cin[:, 16 * j:16 * j + 128], cw[:, j:j + 1], cacc[:, 0:128], ALU.mult, ALU.add)
                kb.tr(P[2][0:48, 256:384], cin[:, 64:112], ident)
                kb.cp("vector", tl_rows[0:48, 256:384], P[2][0:48, 256:384])
                kb.dma(L["oconv_o"][:, cbase - GW:cbase - GW + 128], tl_rows[0:48, 256:384])
            yield
            if qi == 2:
                kb.act(vblk[:, 0:n], cacc[:, 0:n], AF.Silu)
                yield
                for ti in range(n // 128):
                    tl = t0 // 128 + ti
                    kb.tr(psq[:, (ti % 4) * 128:(ti % 4) * 128 + 128], vblk[:, ti * 128:(ti + 1) * 128], ident)
                yield
                for ti in range(n // 128):
                    tl = t0 // 128 + ti
                    kb.cp("scalar" if ti % 2 else "vector", vtok[:, tl, :], psq[:, (ti % 4) * 128:(ti % 4) * 128 + 128])
                return
            dstT = dstT0
            kb.act(dstT[:, t0:t0 + n], cacc[:, 0:n], AF.Silu)
            kb.act(sq[:, 0:n], dstT[:, t0:t0 + n], AF.Square)
            yield
            kb.mm(psq[:, 0:n], L["ones_f"], sq[:, 0:n], start=True, stop=True)
            yield
            kb.rsqrt(sq[:, 0:n], psq[:, 0:n], 1e-6)
            yield
            if qi == 0:
                kb.stt("vector", dstT[:, t0:t0 + n], dstT[:, t0:t0 + n], 128.0 ** -0.5, sq[:, 0:n], ALU.mult, ALU.mult)
            else:
                kb.tt("vector", dstT[:, t0:t0 + n], dstT[:, t0:t0 + n], sq[:, 0:n], ALU.mult)

        blocks = []
        for qi, (dstT0, cbase) in enumerate(((qT, GW + h * 128), (kT, GW + 1024 + h * 128), (None, GW + 2048 + h * 128))):
            wb = wq3[qi]
            kb.dma(wb, w_in_d[:, cbase:cbase + 128].rearrange("(k p) n -> p k n", p=128), q="gpsimd")
            cw = cw3[:, qi * 4:(qi + 1) * 4]
            kb.dma(cw, L["convw_d"][cbase - GW:cbase - GW + 128, :])
            kb.dma(tl_rows[0:48, 0:128], L["sconv_d"][:, cbase - GW:cbase - GW + 128])
            kb.tr(P[2][:, 384:432], tl_rows[0:48, 0:128], ident[0:48, 0:48])
            c0T = c0T3[:, qi * 48:(qi + 1) * 48]
            kb.cp("vector", c0T, P[2][:, 384:432])
            for bi, (t0, n) in enumerate([(0, 512), (512, 512), (1024, 512), (1536, 512), (2048, 128)]):
                blocks.append((qi, bi, t0, n, dstT0, cbase, wb, cw, c0T))
        active = []
        nxt = 0
        while nxt < len(blocks) or active:
            if nxt < len(blocks) and len(active) < 2:
                slot = cit[0]; cit[0] += 1
                active.append(conv_block(*blocks[nxt], slot, cinS[(slot + 1) % 2]))
                nxt += 1
            for g_ in list(active):
                try:
                    next(g_)
                except StopIteration:
                    active.remove(g_)
        wz = wsl[3]
        kb.dma(wz, w_in_d[:, GW + 3072 + h * 128:GW + 3072 + (h + 1) * 128].rearrange("(k p) n -> p k n", p=128), q="gpsimd")
        kb.phase("gdnchunk")
        kb.memset("vector", Sst, 0.0)

        def stageA(c, u, TB):
            sl = slice(c * 128, (c + 1) * 128)
            smp = (c == 16)
            capT = capT_s if smp else capT_p
            capE = capE_s if smp else capE_p
            bcol = cols[:, c, 16 + h:17 + h]; negb = dcol[:, c, 40 + h:41 + h]
            bet = dcol[:, c, 8 + h:9 + h]; bete = dcol[:, c, 16 + h:17 + h]; nbet = dcol[:, c, 24 + h:25 + h]
            edl = dcol[:, c, 32 + h:33 + h]
            ET, E, Pm, Q, R, kbe, vb = TA[u]
            WTn, U0, qe, PT, wk, szt = TB[u]
            bk = P[4 + u]
            r0, r1, r2, r3 = bk[:, 0:128], bk[:, 128:256], bk[:, 256:384], bk[:, 384:512]
            kb.mm(r0, selh[0:8, h * 128:(h + 1) * 128], BR[0:8, sl], start=True, stop=True)
            kb.mm(r1, kT[:, sl], kT[:, sl], start=True, stop=True)
            kb.mm(r2, kT[:, sl], qT[:, sl], start=True, stop=True)
            kb.tr(r3, kT[:, sl], ident)
            yield
            kb.stt("vector", ET, r0, negb, capT, ALU.add, ALU.min)
            kb.stt("vector", E, r0, bcol, capE, ALU.subtract, ALU.max)
            kb.act(WTn, r0, AF.Exp)
            kb.ts("vector", vb, vtok[:, c, :], bet, None, ALU.mult)
            yield
            kb.act(ET, ET, AF.Exp)
            kb.act(E, E, AF.Exp, scale=-1.0)
            kb.ts("vector", kbe, r3, bete, None, ALU.mult)
            kb.ts("vector", wk, r3, edl, None, ALU.mult)
            kb.tt("vector", qe, qT[:, sl], WTn, ALU.mult)
            yield
            kb.stt("vector", Pm, r1, nbet, E, ALU.mult, ALU.mult)
            kb.tt("vector", PT, r2, ET, ALU.mult)
            yield
            kb.tr(r0, Pm, ident)
            pz = P[pit[0] % 2]; pit[0] += 1
            for kc in range(16):
                kb.mm(pz[:, 0:128], hT[:, kc, sl], wz[:, kc, 0:128], start=(kc == 0), stop=(kc == 15))
            kb.act(szt, pz[:, 0:128], AF.Silu)
            yield
            kb.cp("scalar", Q, r0)
            kb.tt("vector", R, r0, ident, ALU.add)
            kb.tt("vector", szt, szt, gnb, ALU.mult)
            yield
            Pc, Qc, Pn, Qn = Pm, Q, E, ET
            nst = 2 if smp else 6
            for k in range(nst):
                kb.mm(r1, Qc, Pc, start=True, stop=True)
                if k < nst - 1:
                    kb.mm(r2, Pc, Qc, start=True, stop=True)
                yield
                kb.cp("scalar", Pn, r1)
                if k < nst - 1:
                    kb.cp("vector", Qn, r2)
                yield
                kb.mm(r3, Pn, R, start=True, stop=True)
                yield
                kb.tt("vector", R, R, r3, ALU.add)
                Pc, Qc, Pn, Qn = Pn, Qn, Pc, Qc
            yield
            kb.mm(r0, kbe, R, start=True, stop=True)
            kb.mm(r1, R, vb, start=True, stop=True)
            yield
            kb.ts("vector", WTn, r0, -1.0, None, ALU.mult)
            kb.cp("scalar", U0, r1)

        def out_epi(c, onum, szt):
            kb.act(junk, onum, AF.Square, accum_out=ec[:, 0:1])
            kb.ts("vector", ec[:, 1:2], ec[:, 0:1], 1.0 / 128.0, 1e-6, ALU.mult, ALU.add)
            kb.rsqrt(ec[:, 2:3], ec[:, 1:2], 0.0)
            hm = hmx[c % 2]
            kb.stt("vector", hm, onum, ec[:, 2:3], szt, ALU.mult, ALU.mult)
            sel_store(kb, P, c % 2, mixT, hm, 8 + h, c, identb, selb)

        def stageB(grp, TB):
            for u, c in enumerate(grp):
                WTn, U0, qe, PT, wk, szt = TB[u]
                if c < 16:
                    kb.mm(P[2][:, 0:128], WTn, Sst, start=True, stop=True)
                    kb.mm(P[3][:, 0:128], qe, Sst, start=True, stop=False)
                    yield
                    kb.tt("vector", U, U0, P[2][:, 0:128], ALU.add)
                    yield
                    kb.mm(P[3][:, 0:128], PT, U, start=False, stop=True)
                    kb.mm(P[2][:, 128:256], wk, U, start=True, stop=True)
                    yield
                    kb.stt("vector", Sst, Sst, ebLB[:, h * 32 + c:h * 32 + c + 1], P[2][:, 128:256], ALU.mult, ALU.add)
                    if c == 15:
                        kb.dma(L["pS_o"][h], Sst)
                    out_epi(c, P[3][:, 0:128], szt)
                    yield
                else:
                    kbe, vb = TA[u][5], TA[u][6]
                    kb.cp("vector", U, U0)
                    oacc = kbe
                    kb.memset("vector", oacc, 0.0)
                    for b_ in range(3):
                        kb.dma(S0b[b_ % 4], L["sS_d"][b_, h])
                    for b in range(NSEQ):
                        sb = S0b[b % 4]
                        if b + 3 < NSEQ:
                            kb.dma(S0b[(b + 3) % 4], L["sS_d"][b + 3, h])
                        pr = P[2] if b % 2 == 0 else P[3]
                        kb.mm(pr[:, 0:128], WTn, sb, start=True, stop=True)
                        kb.mm(pr[:, 128:256], qe, sb, start=True, stop=True)
                        kb.stt("vector", U, pr[:, 0:128], maskP[:, b:b + 1], U, ALU.mult, ALU.add)
                        kb.stt("vector", oacc, pr[:, 128:256], maskP[:, b:b + 1], oacc, ALU.mult, ALU.add)
                    kb.mm(P[2][:, 256:384], PT, U, start=True, stop=True)
                    kb.tt("vector", oacc, oacc, P[2][:, 256:384], ALU.add)
                    for b_ in range(3):
                        kb.dma(S0b[b_ % 4], L["sS_d"][b_, h])
                    for b in range(NSEQ):
                        sb = S0b[b % 4]
                        if b + 3 < NSEQ:
                            kb.dma(S0b[(b + 3) % 4], L["sS_d"][b + 3, h])
                        vm = TA[3][b % 4]
                        kb.ts("vector", vm, U, maskP[:, b:b + 1], None, ALU.mult)
                        pr = P[2] if b % 2 == 0 else P[3]
                        kb.mm(pr[:, 384:512], wk, vm, start=True, stop=True)
                        so = Sout[b % 2]
                        kb.stt("vector", so, sb, ebLB[:, h * 32 + 16 + b:h * 32 + 17 + b], pr[:, 384:512], ALU.mult, ALU.add)
                        kb.dma(L["oS_o"][b, h], so)
                    out_epi(c, oacc, szt)

        groups = [list(range(g, g + NG)) for g in range(0, 16, NG)] + [[16]]
        pend = None
        for gi, grp in enumerate(groups):
            TBc = TBS[gi % 2]
            gens = [stageA(c, u, TBc) for u, c in enumerate(grp)]
            if pend is not None:
                gens.append(pend)
            while gens:
                for g_ in list(gens):
                    try:
                        next(g_)
                    except StopIteration:
                        gens.remove(g_)
            pend = stageB(grp, TBc)
        for _ in pend:
            pass
    cv.release()


_NC_CACHE = {}


def kernel(**inputs):
    if "nc" not in _NC_CACHE:
        _NC_CACHE["nc"] = build()
    nc = _NC_CACHE["nc"]
    maps = make_in_maps(inputs)
    res = run_bass_kernel_spmd(nc, maps, core_ids=list(range(8)))
    R = res.results
    f32 = np.float32
    y_p = np.zeros((4, 2048, D), f32); y_s = np.zeros((128, 4, D), f32)
    p_C = np.zeros((4, 4, 256, 256), f32); p_n = np.zeros((4, 4, 256), f32); p_m = np.zeros((4, 4), f32)
    p_S = np.zeros((4, 8, 128, 128), f32); p_conv = np.zeros((4, 3, 3072), f32)
    s_C = np.zeros((128, 4, 256, 256), f32); s_n = np.zeros((128, 4, 256), f32); s_m = np.zeros((128, 4), f32)
    s_S = np.zeros((128, 8, 128, 128), f32); s_conv = np.zeros((128, 3, 3072), f32)
    for c in range(8):
        p, j = c // 2, c % 2
        sq = slice(c * NSEQ, (c + 1) * NSEQ)
        r = R[c]
        ym = np.asarray(r["ym"], f32)
        y_p[p, j * 1024:(j + 1) * 1024] = ym[0:1024]
        y_s[sq] = ym[1024:1088].reshape(4, NSEQ, D).transpose(1, 0, 2)
        if j == 0:
            p_C[p] = r["pC"]; p_n[p] = r["pn"]; p_m[p] = np.asarray(r["pm"])[:, 0]
            p_S[p] = r["pS"]; p_conv[p] = r["pconv"]
        s_C[sq] = r["oC"]; s_n[sq] = r["on"]; s_m[sq] = np.asarray(r["om"]).T
        s_S[sq] = r["oS"]
        s_conv[sq] = np.asarray(r["oconv"]).reshape(3, NSEQ, 3072).transpose(1, 0, 2)
    return (y_p, y_s, p_C, p_n, p_m, p_S, p_conv, s_C, s_n, s_m, s_S, s_conv)
```
